# Optimizing a Trainium2 kernel written in Bass

```python
import jax, jax.numpy as jnp
from jax import lax
import numpy as np

D_MODEL = 2048
BATCH = 2
SEQ = 16384
DEPTH = 1
DEC_BATCH = 4
DEC_SEQ = 2048
PAST_LEN = 128

N_HEADS = 32
N_KV_HEADS = 8
HEAD_DIM = 64
ATTN_WIDTH = N_HEADS * HEAD_DIM
KV_WIDTH = N_KV_HEADS * HEAD_DIM
WINDOW = 128
BLOCK = 128
ROPE_THETA = 500000.0
ROPE_DIM = HEAD_DIM // 4
SSD_HEADS = 32
SSD_HEAD_DIM = 64
D_INNER = SSD_HEADS * SSD_HEAD_DIM
D_STATE = 128
N_GROUPS = 8
CONV_W = 5
CONV_DIM = D_INNER + 2 * N_GROUPS * D_STATE
CHUNK = 128
D_FF = -(-8 * D_MODEL // (3 * 256)) * 256
N_MIX_BRANCHES = 2
IN_WIDTH = ATTN_WIDTH + 2 * KV_WIDTH + D_INNER + CONV_DIM + 2 * SSD_HEADS + N_MIX_BRANCHES * D_MODEL
EPS = 1e-6

kernel_name = "gated_parallel_swa_ssd_encoder"


def _split_cols(t, widths):
    idx = np.cumsum(widths)[:-1].tolist()
    return jnp.split(t, idx, axis=-1)


def rms_norm(x, w):
    xf = x.astype(jnp.float32)
    y = xf * lax.rsqrt(jnp.mean(xf * xf, axis=-1, keepdims=True) + EPS)
    return (y * w.astype(jnp.float32)).astype(x.dtype)


def partial_rope(t, pos):
    half = ROPE_DIM // 2
    inv = ROPE_THETA ** (-jnp.arange(half, dtype=jnp.float32) * 2.0 / ROPE_DIM)
    ang = pos.astype(jnp.float32)[:, None] * inv[None, :]
    cos = jnp.cos(ang)[None, :, None, :]
    sin = jnp.sin(ang)[None, :, None, :]
    tr = t[..., :ROPE_DIM].astype(jnp.float32)
    x1, x2 = tr[..., :half], tr[..., half:]
    rot = jnp.concatenate([x1 * cos - x2 * sin, x2 * cos + x1 * sin], axis=-1)
    return jnp.concatenate([rot.astype(t.dtype), t[..., ROPE_DIM:]], axis=-1)


def window_attention(q, k, v, sinks):
    b, L = q.shape[0], q.shape[1]
    nblk = L // BLOCK
    grp = N_HEADS // N_KV_HEADS
    qg = q.reshape(b, L, N_KV_HEADS, grp, HEAD_DIM)
    pad = ((0, 0), (BLOCK, BLOCK), (0, 0), (0, 0))
    kp = jnp.pad(k, pad)
    vp = jnp.pad(v, pad)
    scale = HEAD_DIM ** -0.5
    sink = sinks.astype(jnp.float32).reshape(N_KV_HEADS, grp)[None, :, :, None, None]

    def one_block(i):
        start = i * BLOCK
        qb = lax.dynamic_slice_in_dim(qg, start, BLOCK, axis=1)
        kb = lax.dynamic_slice_in_dim(kp, start, 3 * BLOCK, axis=1)
        vb = lax.dynamic_slice_in_dim(vp, start, 3 * BLOCK, axis=1)
        s = jnp.einsum('bqkgd,bskd->bkgqs', qb, kb, preferred_element_type=jnp.float32) * scale
        qpos = start + jnp.arange(BLOCK)
        kpos = start - BLOCK + jnp.arange(3 * BLOCK)
        mask = (jnp.abs(qpos[:, None] - kpos[None, :]) <= WINDOW) & (kpos >= 0)[None, :] & (kpos < L)[None, :]
        s = jnp.where(mask, s, -jnp.inf)
        logits = jnp.concatenate([s, jnp.broadcast_to(sink, s.shape[:-1] + (1,))], axis=-1)
        p = jax.nn.softmax(logits, axis=-1)[..., :-1]
        return jnp.einsum('bkgqs,bskd->bqkgd', p.astype(vb.dtype), vb)

    out = lax.map(one_block, jnp.arange(nblk))
    return jnp.moveaxis(out, 0, 1).reshape(b, L, ATTN_WIDTH)


def ssd_scan(x, dt, A, B, C):
    b, L = x.shape[0], x.shape[1]
    nc = L // CHUNK
    R = SSD_HEADS // N_GROUPS
    X = (x * dt[..., None]).reshape(b, nc, CHUNK, N_GROUPS, R, SSD_HEAD_DIM)
    dA = (dt * A).reshape(b, nc, CHUNK, N_GROUPS, R)
    At = jnp.moveaxis(jnp.cumsum(dA, axis=2), 2, -1)
    Bc = B.reshape(b, nc, CHUNK, N_GROUPS, D_STATE)
    Cc = C.reshape(b, nc, CHUNK, N_GROUPS, D_STATE)
    tril = jnp.tril(jnp.ones((CHUNK, CHUNK), dtype=bool))
    seg = At[..., :, None] - At[..., None, :]
    Lmat = jnp.exp(jnp.where(tril, seg, -jnp.inf))
    CB = jnp.einsum('bclgn,bcsgn->bcgls', Cc, Bc)
    y_diag = jnp.einsum('bcgls,bcgrls,bcsgrp->bclgrp', CB, Lmat, X)
    decay = jnp.exp(At[..., -1:] - At)
    states = jnp.einsum('bclgn,bcgrl,bclgrp->bcgrpn', Bc, decay, X)
    chunk_decay = jnp.exp(At[..., -1])

    def step(s, inp):
        st, dc = inp
        return s * dc[..., None, None] + st, s

    init = jnp.zeros((b, N_GROUPS, R, SSD_HEAD_DIM, D_STATE), jnp.float32)
    _, prev = lax.scan(step, init, (jnp.moveaxis(states, 1, 0), jnp.moveaxis(chunk_decay, 1, 0)))
    prev = jnp.moveaxis(prev, 0, 1)
    y_off = jnp.einsum('bclgn,bcgrpn,bcgrl->bclgrp', Cc, prev, jnp.exp(At))
    return (y_diag + y_off).reshape(b, L, SSD_HEADS, SSD_HEAD_DIM)


def ssd_branch(z, xbc, dt_raw, conv_w, conv_b, a_log_f, a_log_b, dt_bias_f, dt_bias_b, d_skip, ssd_norm_w):
    b, L = z.shape[0], z.shape[1]
    xbc = lax.conv_general_dilated(xbc, conv_w[:, None, :], window_strides=(1,),
                                   padding=[(CONV_W // 2, CONV_W // 2)],
                                   dimension_numbers=('NWC', 'WIO', 'NWC'),
                                   feature_group_count=CONV_DIM) + conv_b
    xbc = jax.nn.silu(xbc.astype(jnp.float32))
    xs, Bm, Cm = _split_cols(xbc, [D_INNER, N_GROUPS * D_STATE, N_GROUPS * D_STATE])
    xh = xs.reshape(b, L, SSD_HEADS, SSD_HEAD_DIM)
    Bm = Bm.reshape(b, L, N_GROUPS, D_STATE)
    Cm = Cm.reshape(b, L, N_GROUPS, D_STATE)
    dtr = dt_raw.astype(jnp.float32)
    dt_f = jax.nn.softplus(dtr[..., :SSD_HEADS] + dt_bias_f.astype(jnp.float32))
    dt_b = jax.nn.softplus(dtr[..., SSD_HEADS:] + dt_bias_b.astype(jnp.float32))
    A_f = -jnp.exp(a_log_f.astype(jnp.float32))
    A_b = -jnp.exp(a_log_b.astype(jnp.float32))
    fl = lambda t: jnp.flip(t, axis=1)
    y_f = ssd_scan(xh, dt_f, A_f, Bm, Cm)
    y_b = fl(ssd_scan(fl(xh), fl(dt_b), A_b, fl(Bm), fl(Cm)))
    y = y_f + y_b + d_skip.astype(jnp.float32)[:, None] * xh
    y = y.reshape(b, L, D_INNER) * jax.nn.silu(z.astype(jnp.float32))
    yg = y.reshape(b, L, N_GROUPS, D_INNER // N_GROUPS)
    yg = yg * lax.rsqrt(jnp.mean(yg * yg, axis=-1, keepdims=True) + EPS)
    y = yg.reshape(b, L, D_INNER) * ssd_norm_w.astype(jnp.float32)
    return y.astype(z.dtype)


def encoder_layer(x, c, w_ada, b_ada, g_pre1, g_post1, w_in, conv_w, conv_b, a_log_f, a_log_b,
                  dt_bias_f, dt_bias_b, d_skip, ssd_norm_w, sinks, w_out, g_pre2, g_post2, w_gu, w_down):
    b, L = x.shape[0], x.shape[1]
    mod = jax.nn.silu(c) @ w_ada + b_ada
    sh1, sc1, gt1, sh2, sc2, gt2 = jnp.split(mod[:, None, :], 6, axis=-1)
    h = rms_norm(x, g_pre1) * (1 + sc1) + sh1
    proj = h @ w_in
    q, k, v, z, xbc, dt_raw, gates = _split_cols(
        proj, [ATTN_WIDTH, KV_WIDTH, KV_WIDTH, D_INNER, CONV_DIM, 2 * SSD_HEADS, N_MIX_BRANCHES * D_MODEL])
    pos = jnp.arange(L)
    q = partial_rope(q.reshape(b, L, N_HEADS, HEAD_DIM), pos)
    k = partial_rope(k.reshape(b, L, N_KV_HEADS, HEAD_DIM), pos)
    v = v.reshape(b, L, N_KV_HEADS, HEAD_DIM)
    attn = window_attention(q, k, v, sinks)
    ssd = ssd_branch(z, xbc, dt_raw, conv_w, conv_b, a_log_f, a_log_b,
                     dt_bias_f, dt_bias_b, d_skip, ssd_norm_w)
    g_a, g_s = jnp.split(jax.nn.sigmoid(gates.astype(jnp.float32)), 2, axis=-1)
    merged = (g_a * attn.astype(jnp.float32) + g_s * ssd.astype(jnp.float32)).astype(x.dtype)
    mix = merged @ w_out
    x = x + gt1 * rms_norm(mix, g_post1)
    h2 = rms_norm(x, g_pre2) * (1 + sc2) + sh2
    gate, up = jnp.split(h2 @ w_gu, 2, axis=-1)
    f = (jax.nn.silu(gate) * up) @ w_down
    return x + gt2 * rms_norm(f, g_post2)


def setup_inputs(seed: int = 0) -> dict:
    key = jax.random.key(seed)
    ks = jax.random.split(key, 24)
    f32 = jnp.float32
    nrm = lambda k, shape, s: jax.random.normal(k, shape, f32) * s
    gain = lambda k, shape: 1.0 + 0.05 * jax.random.normal(k, shape, f32)
    dt0 = lambda k: jnp.exp(jax.random.uniform(k, (DEPTH, SSD_HEADS), f32, np.log(1e-3), np.log(1e-1)))
    inv_softplus = lambda d: d + jnp.log(-jnp.expm1(-d))
    return {
        "x_prompt": nrm(ks[0], (BATCH, SEQ, D_MODEL), 1.0),
        "x_sample": nrm(ks[1], (DEC_BATCH, DEC_SEQ, D_MODEL), 1.0),
        "c_prompt": nrm(ks[2], (BATCH, D_MODEL), 1.0),
        "c_sample": nrm(ks[3], (DEC_BATCH, D_MODEL), 1.0),
        "w_ada": nrm(ks[4], (DEPTH, D_MODEL, 6 * D_MODEL), D_MODEL ** -0.5),
        "b_ada": nrm(ks[5], (DEPTH, 6 * D_MODEL), 0.01),
        "g_pre1": gain(ks[6], (DEPTH, D_MODEL)),
        "g_post1": gain(ks[7], (DEPTH, D_MODEL)),
        "w_in": nrm(ks[8], (DEPTH, D_MODEL, IN_WIDTH), D_MODEL ** -0.5),
        "conv_w": nrm(ks[9], (DEPTH, CONV_W, CONV_DIM), CONV_W ** -0.5),
        "conv_b": nrm(ks[10], (DEPTH, CONV_DIM), 0.01),
        "a_log_f": jnp.log(jax.random.uniform(ks[11], (DEPTH, SSD_HEADS), f32, 1.0, 16.0)),
        "a_log_b": jnp.log(jax.random.uniform(ks[12], (DEPTH, SSD_HEADS), f32, 1.0, 16.0)),
        "dt_bias_f": inv_softplus(dt0(ks[13])),
        "dt_bias_b": inv_softplus(dt0(ks[14])),
        "d_skip": gain(ks[15], (DEPTH, SSD_HEADS)),
        "ssd_norm_w": gain(ks[16], (DEPTH, D_INNER)),
        "sinks": nrm(ks[17], (DEPTH, N_HEADS), 1.0),
        "w_out": nrm(ks[18], (DEPTH, D_MODEL, D_MODEL), D_MODEL ** -0.5),
        "g_pre2": gain(ks[19], (DEPTH, D_MODEL)),
        "g_post2": gain(ks[20], (DEPTH, D_MODEL)),
        "w_gu": nrm(ks[21], (DEPTH, D_MODEL, 2 * D_FF), D_MODEL ** -0.5),
        "w_down": nrm(ks[22], (DEPTH, D_FF, D_MODEL), D_FF ** -0.5),
    }


def reference(x_prompt, x_sample, c_prompt, c_sample, w_ada, b_ada, g_pre1, g_post1, w_in, conv_w, conv_b,
              a_log_f, a_log_b, dt_bias_f, dt_bias_b, d_skip, ssd_norm_w, sinks, w_out, g_pre2, g_post2,
              w_gu, w_down):
    y_prompt = x_prompt
    y_sample = x_sample
    for l in range(DEPTH):
        p = (w_ada[l], b_ada[l], g_pre1[l], g_post1[l], w_in[l], conv_w[l], conv_b[l], a_log_f[l], a_log_b[l],
             dt_bias_f[l], dt_bias_b[l], d_skip[l], ssd_norm_w[l], sinks[l], w_out[l], g_pre2[l], g_post2[l],
             w_gu[l], w_down[l])
        y_prompt = encoder_layer(y_prompt, c_prompt, *p)
        y_sample = encoder_layer(y_sample, c_sample, *p)
    return (y_prompt, y_sample)
```

```python
import os
import numpy as np
from contextlib import ExitStack
import concourse.bass as bass
import concourse.mybir as mybir
from concourse.bass_utils import run_bass_kernel_spmd

F32 = mybir.dt.float32
BF16 = mybir.dt.bfloat16
AF = mybir.ActivationFunctionType
ALU = mybir.AluOpType
AX = mybir.AxisListType

D = 2048
KC = 16
DFF = 5632
FC = 44
NH = 32
EPS = 1e-6
IN_W = 13376
NWB = int(os.environ.get("KNWB", "2"))


class Buf:
    def __init__(self, name, const=False, excl=False):
        self.name = name
        self.excl = excl
        self.w = None
        self.r = []
        self.dsem = None
        self.dcount = 0
        self.const = const


class Eng:
    def __init__(self, name):
        self.name = name
        self.ops = []
        self.count = 0
        self.waited = {}


class FW:
    def __init__(self, nc, stack):
        self.nc = nc
        self.stack = stack
        self.E = {n: Eng(n) for n in ("pe", "dve", "act", "pool", "sp")}
        self.sems = {}
        for n in self.E:
            self.sems[n] = stack.enter_context(nc.semaphore("s_" + n))
        self.nd = 0
        self.final_waits = []
        self.dma_bufs = []

    def dma_sem(self, buf):
        if buf.dsem is None:
            key = "d%d" % self.nd
            self.nd += 1
            self.sems[key] = self.stack.enter_context(self.nc.semaphore(key))
            buf.dsem = key
            self.dma_bufs.append(buf)
        return buf.dsem

    def _need(self, eng, deps):
        e = self.E[eng]
        best = {}
        for k, v in deps:
            if v > best.get(k, 0):
                best[k] = v
        for k, v in best.items():
            if e.waited.get(k, 0) >= v:
                continue
            e.waited[k] = v
            h = self.sems[k]
            e.ops.append(lambda en, h=h, v=v: en.wait_ge(h, v))

    @staticmethod
    def _deps(reads, writes):
        deps = []
        for b in reads:
            if b.w is not None:
                deps.append(b.w)
            if b.excl:
                deps.extend(b.r)
        for b in writes:
            if b.w is not None:
                deps.append(b.w)
            deps.extend(b.r)
        return deps

    def _mark(self, ev, reads, writes):
        for b in reads:
            if not b.const:
                b.r.append(ev)
        for b in writes:
            b.w = ev
            b.r = []

    def op(self, eng, fn, reads=(), writes=()):
        if eng == "pool" and os.environ.get("KPOOL"):
            eng = os.environ["KPOOL"]
        e = self.E[eng]
        self._need(eng, self._deps(reads, writes))
        e.count += 1
        ev = (eng, e.count)
        h = self.sems[eng]
        e.ops.append(lambda en, fn=fn, h=h: fn(en).then_inc(h, 1))
        self._mark(ev, reads, writes)
        return ev

    def group(self, eng, fns, reads=(), writes=()):
        e = self.E[eng]
        self._need(eng, self._deps(reads, writes))
        e.count += 1
        ev = (eng, e.count)
        h = self.sems[eng]
        n = len(fns)
        for i, fn in enumerate(fns):
            if i == n - 1:
                e.ops.append(lambda en, fn=fn, h=h: fn(en).then_inc(h, 1))
            else:
                e.ops.append(lambda en, fn=fn: fn(en))
        self._mark(ev, reads, writes)
        return ev

    def dma(self, q, out_ap, in_ap, reads=(), writes=(), sem_buf=None, final=False, **kw):
        if q == "pool" and os.environ.get("KDMAQ"):
            q = os.environ["KDMAQ"]
        e = self.E[q]
        self._need(q, self._deps(reads, writes))
        key = self.dma_sem(sem_buf)
        sem_buf.dcount += 16
        ev = (key, sem_buf.dcount)
        h = self.sems[key]
        e.ops.append(lambda en, o=out_ap, i=in_ap, h=h, kw=kw: en.dma_start(out=o, in_=i, **kw).then_inc(h, 16))
        self._mark(ev, reads, writes)
        if final:
            self.final_waits.append(ev)
        return ev

    def barrier(self):
        evs = [(n, e.count) for n, e in self.E.items() if e.count > 0]
        evs += [(b.dsem, b.dcount) for b in self.dma_bufs if b.dcount > 0]
        for n in self.E:
            self._need(n, evs)

    def emit(self):
        nc = self.nc
        E = self.E
        with nc.Block() as block:
            @block.tensor
            def _(en):
                for f in E["pe"].ops:
                    f(en)

            @block.vector
            def _(en):
                for f in E["dve"].ops:
                    f(en)

            @block.scalar
            def _(en):
                for f in E["act"].ops:
                    f(en)

            @block.gpsimd
            def _(en):
                for f in E["pool"].ops:
                    f(en)

            @block.sync
            def _(en):
                for f in E["sp"].ops:
                    f(en)


def I(name, *args, **kwargs):
    return lambda en: getattr(en, name)(*args, **kwargs)


def weight_units():
    U = []
    for i in range(4):
        U.append(("w_in", 0, 16, 512 * i, 512))
    U.append(("w_in", 0, 16, 2048, 512))
    U.append(("w_in", 0, 16, 2560, 512))
    for i in range(4):
        U.append(("w_in", 0, 16, 3072 + 512 * i, 512))
    for i in range(8):
        U.append(("w_in", 0, 16, 5120 + 512 * i, 512))
    U.append(("w_in", 0, 16, 9216, 64))
    for i in range(8):
        U.append(("w_in", 0, 16, 9280 + 512 * i, 512))
    n_in = len(U)
    for i in range(4):
        U.append(("w_out", 0, 16, 512 * i, 512))
    for j in range(11):
        U.append(("w_gu", 0, 16, 512 * j, 512))
        U.append(("w_gu", 0, 16, DFF + 512 * j, 512))
    for kg in range(4):
        for cb in range(4):
            U.append(("w_down", 11 * kg, 11, 512 * cb, 512))
    return U, n_in


UNITS, N_IN_UNITS = weight_units()
U_OUT0 = N_IN_UNITS
U_GU0 = U_OUT0 + 4
U_DN0 = U_GU0 + 22

SM = {}
_o = 0
for _n, _w in [("cfm", 128), ("gpre1", 16), ("gpre2", 16), ("convw", 160), ("convb", 32), ("alog", 64),
               ("dtb", 64), ("dskip", 32), ("sinks", 32)]:
    SM[_n] = (_o, _w)
    _o += _w
SM_W = _o
CONST_W = 6 * 128


def build(pieces):
    pieces = list(pieces)
    NPC = len(pieces)
    assert NPC % 2 == 0 and NPC <= 8
    NCH = sum(pieces)
    NT = NCH * 128
    nc = bass.Bass("TRN2", target_bir_lowering=False)

    def din(name, shape):
        return nc.dram_tensor(name, shape, F32, kind="ExternalInput").ap()

    xin = din("xin", [NT, D])
    small = din("small", [128, SM_W])
    consts = din("consts", [128, CONST_W])
    rope = din("rope", [128, NCH * 16])
    rowbc = din("rowbc", [128, 3 * D + 6 * D])
    W = {"w_ada": din("w_ada", [D, 6 * D]), "w_in": din("w_in", [D, IN_W]), "w_out": din("w_out", [D, D]),
         "w_gu": din("w_gu", [D, 2 * DFF]), "w_down": din("w_down", [DFF, D])}
    yout = nc.dram_tensor("yout", [NT, D], F32, kind="ExternalOutput").ap()

    def dscr(name, shape, dt):
        return nc.dram_tensor(name, shape, dt, kind="Internal").ap()

    wscr = dscr("wscr", [len(UNITS), 128, 8192], BF16)
    class PieceScr:
        def __init__(self, name, width, dt):
            self.t = [dscr("%s_%d" % (name, k), [n, 128, width], dt) for k, n in enumerate(pieces)]
            self.map = []
            for k, n in enumerate(pieces):
                self.map += [(k, i) for i in range(n)]

        def __getitem__(self, gc):
            k, i = self.map[gc]
            return self.t[k][i]

    st_ypart = PieceScr("st_ypart", D, F32)
    st_zs = PieceScr("st_zs", D, BF16)
    st_gs = PieceScr("st_gs", D, BF16)
    st_ya = PieceScr("st_ya", D, BF16)
    st_ct = PieceScr("st_ct", 1024, BF16)
    st_sb = PieceScr("st_sb", D, F32)
    st_vec = PieceScr("st_vec", 64, F32)
    st_gg = dscr("st_gg", [2 * NPC, 128, D], F32)

    with ExitStack() as st:
        fw = FW(nc, st)
        op, grp, dma = fw.op, fw.group, fw.dma

        def sb(name, shape, dt, stack=st):
            return stack.enter_context(nc.sbuf_tensor(name, shape, dt))

        def ps(name, shape, dt, stack=st):
            return stack.enter_context(nc.psum_tensor(name, shape, dt))

        PA = [ps("PA%d" % i, [128, 512], F32) for i in range(2)]
        BPA = [Buf("PA%d" % i, excl=True) for i in range(2)]
        PT = [ps("PT%d" % i, [128, 1024], BF16) for i in range(2)]
        BPT = [Buf("PT%d" % i, excl=True) for i in range(2)]
        PX1 = ps("PX1", [128, 512], F32); BPX1 = Buf("PX1", excl=True)
        PX2 = ps("PX2", [128, 512], F32); _b2 = Buf("PX2", excl=True); BPX2 = [_b2, _b2]
        PX3 = ps("PX3", [128, 512], F32); _b3 = Buf("PX3", excl=True); BPX3 = [_b3, _b3]
        PM = ps("PM", [128, 512], F32); BPMa = Buf("PM", excl=True); BPMb = BPMa

        smt = sb("smt", [128, SM_W], F32); Bsm = Buf("smt", const=True)
        cst = sb("cst", [128, CONST_W], F32); Bcst = Buf("cst", const=True)
        cbf = sb("cbf", [128, 3 * 128], BF16); Bcbf = Buf("cbf", const=True)
        fm = sb("fm", [128, NPC, 4, 16], F32); Bfm = Buf("fm")
        vecs = sb("vecs", [128, 4 * 32 + 64], F32); Bvecs = Buf("vecs")
        WB = [sb("WB%d" % i, [128, 8192], BF16) for i in range(NWB)]
        BWB = [Buf("WB%d" % i) for i in range(NWB)]

        ident_f = cst[:, 0:128]
        U_f = cst[:, 128:256]
        Lw_f = cst[:, 256:384]
        TU_f = cst[:, 384:512]
        TL_f = cst[:, 512:640]
        ones_f = cst[:, 640:768]
        ident_b = cbf[:, 0:128]
        U_b = cbf[:, 128:256]
        Lw_b = cbf[:, 256:384]

        def smv(name):
            o, w = SM[name]
            return smt[:, o:o + w]

        dma("sp", smt[:], small, writes=[Bsm], sem_buf=Bsm)
        dma("sp", cst[:], consts, writes=[Bcst], sem_buf=Bcst)
        op("dve", I("tensor_copy", out=cbf[:], in_=cst[:, 0:384]), reads=[Bcst], writes=[Bcbf])
        op("act", I("activation", out=vecs[:, 0:64], in_=smv("alog"), func=AF.Exp), reads=[Bsm], writes=[Bvecs])
        op("dve", I("tensor_scalar", out=vecs[:, 0:64], in0=vecs[:, 0:64], scalar1=-1.0, scalar2=None, op0=ALU.mult),
           reads=[Bvecs], writes=[Bvecs])
        op("act", I("activation", out=vecs[:, 96:128], in_=smv("sinks"), func=AF.Exp), reads=[Bsm], writes=[Bvecs])
        A_bc = vecs[:, 0:64]
        expsink = vecs[:, 96:128]

        with ExitStack() as s0:
            stg = [sb("stg%d" % i, [128, 8, 512], F32, s0) for i in range(2)]
            Bstg = [Buf("stg%d" % i) for i in range(2)]
            cvt = [sb("cvt%d" % i, [128, 8, 512], BF16, s0) for i in range(2)]
            Bcvt = [Buf("cvt%d" % i) for i in range(2)]
            screp = sb("screp", [128, 2, KC, 128], F32, s0); Bscrep = Buf("screp")
            scf = sb("scf", [128, 128], F32, s0); Bscf = Buf("scf")
            modb = sb("modb", [128, 512], F32, s0); Bmodb = Buf("modb")
            badab = [sb("badab%d" % i, [128, 512], F32, s0) for i in range(2)]
            Bbada = [Buf("badab%d" % i) for i in range(2)]
            gpost = sb("gpost", [128, 2 * D], F32, s0); Bgpost = Buf("gpost")
            tmpx = sb("tmpx", [128, 4, 128], F32, s0); Btmpx = Buf("tmpx")
            ggst = sb("ggst", [128, 512], F32, s0); Bggst = Buf("ggst")
            dma("sp", gpost[:], rowbc[:, D:3 * D], writes=[Bgpost], sem_buf=Bgpost)
            op("act", I("activation", out=scf[:], in_=smv("cfm"), func=AF.Silu), reads=[Bsm], writes=[Bscf])
            si = 0
            for r0 in range(0, NPC, 2):
                for r in range(2):
                    op("dve", I("tensor_copy",
                        out=screp[:, r], in_=scf[:, 16 * (r0 + r):16 * (r0 + r) + 16].unsqueeze(2).to_broadcast([128, KC, 128])),
                       reads=[Bscf], writes=[Bscrep])
                for blk in range(24):
                    seg, jb = blk // 4, blk % 4
                    bb = blk % 2
                    dma("sp", badab[bb][:], rowbc[:, 3 * D + blk * 512:3 * D + (blk + 1) * 512], writes=[Bbada[bb]], sem_buf=Bbada[bb])
                    for half in range(2):
                        b = si % 2
                        si += 1
                        src = W["w_ada"][half * 1024:(half + 1) * 1024, blk * 512:(blk + 1) * 512].rearrange("(k p) c -> p k c", p=128)
                        dma("sp", stg[b][:], src, writes=[Bstg[b]], sem_buf=Bstg[b])
                        for r in range(2):
                            grp("pe", [I("matmul",
                                PA[r][:, :], lhsT=screp[:, r, half * 8 + kk, :], rhs=stg[b][:, kk, :],
                                start=(half == 0 and kk == 0), stop=(half == 1 and kk == 7)) for kk in range(8)],
                                reads=[Bscrep, Bstg[b]], writes=[BPA[r]])
                    for r in range(2):
                        op("dve", I("tensor_tensor", out=modb[:], in0=PA[r][:, :], in1=badab[bb][:], op=ALU.add),
                           reads=[BPA[r], Bbada[bb]], writes=[Bmodb])
                        if seg in (2, 5):
                            j = 0 if seg == 2 else 1
                            op("dve", I("tensor_tensor",
                                out=ggst[:], in0=modb[:], in1=gpost[:, j * D + jb * 512:j * D + (jb + 1) * 512], op=ALU.mult),
                               reads=[Bmodb, Bgpost], writes=[Bggst])
                            dma("sp", st_gg[2 * (r0 + r) + j, :, jb * 512:(jb + 1) * 512], ggst[:], reads=[Bggst], sem_buf=Bggst)
                        else:
                            slot = {0: 0, 1: 1, 3: 2, 4: 3}[seg]
                            op("dve", I("tensor_tensor",
                                out=tmpx[:], in0=modb[:].rearrange("p (a b) -> p a b", a=4),
                                in1=ident_f.unsqueeze(1).to_broadcast([128, 4, 128]), op=ALU.mult),
                               reads=[Bmodb, Bcst], writes=[Btmpx])
                            op("dve", I("tensor_reduce",
                                out=fm[:, r0 + r, slot, 4 * jb:4 * jb + 4], in_=tmpx[:], axis=AX.X, op=ALU.add),
                               reads=[Btmpx], writes=[Bfm])
            for r in range(NPC):
                for slot, gname in ((1, "gpre1"), (3, "gpre2")):
                    op("dve", I("tensor_scalar", out=fm[:, r, slot, :], in0=fm[:, r, slot, :], scalar1=1.0, scalar2=None, op0=ALU.add),
                       reads=[Bfm], writes=[Bfm])
                    op("dve", I("tensor_tensor", out=fm[:, r, slot, :], in0=fm[:, r, slot, :], in1=smv(gname), op=ALU.mult),
                       reads=[Bfm, Bsm], writes=[Bfm])
            ce = 0
            for u, (mat, k0, nk, c0, ncol) in enumerate(UNITS):
                for h0 in range(0, nk, 8):
                    hk = min(8, nk - h0)
                    b = si % 2
                    si += 1
                    src = W[mat][(k0 + h0) * 128:(k0 + h0 + hk) * 128, c0:c0 + ncol].rearrange("(k p) c -> p k c", p=128)
                    dma("sp", stg[b][:, 0:hk, 0:ncol], src, writes=[Bstg[b]], sem_buf=Bstg[b])
                    eng = ("act", "dve", "pool")[ce % 3]
                    ce += 1
                    if eng == "act":
                        op("act", I("copy", out=cvt[b][:, 0:hk, 0:ncol], in_=stg[b][:, 0:hk, 0:ncol]),
                           reads=[Bstg[b]], writes=[Bcvt[b]])
                    else:
                        op(eng, I("tensor_copy", out=cvt[b][:, 0:hk, 0:ncol], in_=stg[b][:, 0:hk, 0:ncol]),
                           reads=[Bstg[b]], writes=[Bcvt[b]])
                    dst = wscr[u, :, h0 * ncol:(h0 + hk) * ncol].rearrange("p (k c) -> p k c", c=ncol)
                    dma("sp", dst, cvt[b][:, 0:hk, 0:ncol], reads=[Bcvt[b]], sem_buf=Bcvt[b])
            fw.barrier()

        wstate = {"n": 0}

        def wload(u):
            b = wstate["n"] % NWB
            wstate["n"] += 1
            mat, k0, nk, c0, ncol = UNITS[u]
            dma("sp", WB[b][:, 0:nk * ncol], wscr[u, :, 0:nk * ncol], writes=[BWB[b]], sem_buf=BWB[b])
            return b

        def wview(b, u):
            mat, k0, nk, c0, ncol = UNITS[u]
            return WB[b][:, 0:nk * ncol].rearrange("p (k c) -> p k c", c=ncol)

        _stop = os.environ.get("KSTOP", "")
        with ExitStack() as s1:
            xt = sb("xt", [128, D], F32, s1); Bxt = Buf("xt")
            xn = sb("xn", [128, D], BF16, s1); Bxn = Buf("xn")
            nst = sb("nst", [128, 4], F32, s1); Bnst = Buf("nst")
            hT = sb("hT", [128, KC, 128], BF16, s1); BhT = Buf("hT")
            qtok = sb("qtok", [128, 512], BF16, s1); Bqtok = Buf("qtok")
            rt = sb("rt", [128, 4, 64], F32, s1); Brt = Buf("rt")
            qT = [sb("qT%d" % i, [128, 4, 4, 128], BF16, s1) for i in range(2)]
            BqT = [Buf("qT%d" % i) for i in range(2)]
            kT = [sb("kT%d" % i, [128, 4, 128], BF16, s1) for i in range(4)]
            BkT = [Buf("kT%d" % i) for i in range(4)]
            vr = [sb("vr%d" % i, [128, 8, 65], BF16, s1) for i in range(4)]
            Bvr = [Buf("vr%d" % i) for i in range(4)]
            zs = sb("zs", [128, D], BF16, s1); Bzs = Buf("zs")
            gs = sb("gs", [128, D], BF16, s1); Bgs = Buf("gs")
            ga = [sb("ga%d" % i, [128, D], BF16, s1) for i in range(2)]
            Bga = [Buf("ga%d" % i) for i in range(2)]
            dtt = [sb("dtt%d" % i, [128, 64], F32, s1) for i in range(2)]
            Bdtt = [Buf("dtt%d" % i) for i in range(2)]
            xtok = sb("xtok", [128, 512], BF16, s1); Bxtok = Buf("xtok")
            RAW = [sb("RAW%d" % i, [128, 32, 132], BF16, s1) for i in range(2)]
            BRAW = [Buf("RAW%d" % i) for i in range(2)]
            cacc = [sb("cacc%d" % i, [128, 128], F32, s1) for i in range(2)]
            Bcacc = [Buf("cacc%d" % i) for i in range(2)]
            XBC = sb("XBC", [128, 32, 128], BF16, s1); BXBC = Buf("XBC")
            Xtok = sb("Xtok", [128, D], BF16, s1); BXtok = Buf("Xtok")
            Btok = sb("Btok", [128, 1024], BF16, s1); BBtok = Buf("Btok")
            Xdt = [sb("Xdt%d" % i, [128, D], BF16, s1) for i in range(2)]
            BXdt = [Buf("Xdt%d" % i) for i in range(2)]
            Xdec = [sb("Xdec%d" % i, [128, D], BF16, s1) for i in range(2)]
            BXdec = [Buf("Xdec%d" % i) for i in range(2)]
            sv = sb("sv", [128, 8, 64], F32, s1); Bsv = Buf("sv")
            vst = sb("vst", [128, 64], F32, s1); Bvst = Buf("vst")
            Rm = [sb("Rm%d" % i, [128, 4, 128], F32, s1) for i in range(2)]
            BRm = [Buf("Rm%d" % i) for i in range(2)]
            LT = [sb("LT%d" % i, [128, 4, 128], BF16, s1) for i in range(2)]
            BLT = [Buf("LT%d" % i) for i in range(2)]
            CBm = sb("CBm", [128, 2, 128], BF16, s1); BCBm = Buf("CBm")
            MT = [sb("MT%d" % i, [128, 4, 128], BF16, s1) for i in range(2)]
            BMT = [Buf("MT%d" % i) for i in range(2)]
            Sf = sb("Sf", [128, D], F32, s1); BSf = Buf("Sf")
            prevf = sb("prevf", [128, D], BF16, s1); Bprevf = Buf("prevf")
            sbst = sb("sbst", [128, D], F32, s1); Bsbst = Buf("sbst")
            ypst = sb("ypst", [128, D], F32, s1); Bypst = Buf("ypst")
            ytmp = sb("ytmp", [128, 256], F32, s1); Bytmp = Buf("ytmp")
            PTs = [sb("PTs%d" % i, [128, 4, 128], BF16, s1) for i in range(3)]
            BPTs = [Buf("PTs%d" % i) for i in range(3)]
            yast = sb("yast", [128, D], BF16, s1); Byast = Buf("yast")
            den = sb("den", [128, 8], F32, s1); Bden = Buf("den")
            rp = [sb("rp%d" % i, [128, 16], F32, s1) for i in range(2)]
            Brp = [Buf("rp%d" % i) for i in range(2)]

            wq = []
            wplan = []
            for p, n in enumerate(pieces):
                for i in range(n):
                    wplan.extend(range(N_IN_UNITS))
            wpos = {"i": 0}

            def wnext():
                while len(wq) < NWB and wpos["i"] < len(wplan):
                    u = wplan[wpos["i"]]
                    wpos["i"] += 1
                    wq.append((u, wload(u)))
                return wq.pop(0)

            def rms_to_hT(xsrc_ap, Bx, r, slots):
                sh_s, gsc_s = slots
                op("act", I("activation", out=xn[:], in_=xsrc_ap, func=AF.Square, accum_out=nst[:, 0:1]),
                   reads=[Bx], writes=[Bxn, Bnst])
                op("dve", I("tensor_scalar", out=nst[:, 1:2], in0=nst[:, 0:1], scalar1=1.0 / D, scalar2=EPS, op0=ALU.mult, op1=ALU.add),
                   reads=[Bnst], writes=[Bnst])
                op("act", I("activation", out=nst[:, 2:3], in_=nst[:, 1:2], func=AF.Sqrt), reads=[Bnst], writes=[Bnst])
                op("dve", I("reciprocal", out=nst[:, 3:4], in_=nst[:, 2:3]), reads=[Bnst], writes=[Bnst])
                op("act", I("activation", out=xn[:], in_=xsrc_ap, func=AF.Copy, scale=nst[:, 3:4]),
                   reads=[Bx, Bnst], writes=[Bxn])
                for hh in range(2):
                    for k8 in range(8):
                        kc = hh * 8 + k8
                        op("pe", I("transpose", out=PT[hh][:, k8 * 128:(k8 + 1) * 128], in_=xn[:, kc * 128:(kc + 1) * 128], identity=ident_b),
                           reads=[Bxn, Bcbf], writes=[BPT[hh]])
                    for k8 in range(8):
                        kc = hh * 8 + k8
                        op("dve", I("tensor_scalar",
                            out=hT[:, kc, :], in0=PT[hh][:, k8 * 128:(k8 + 1) * 128],
                            scalar1=fm[:, r, gsc_s, kc:kc + 1], scalar2=fm[:, r, sh_s, kc:kc + 1], op0=ALU.mult, op1=ALU.add),
                           reads=[BPT[hh], Bfm], writes=[BhT])

            def gemm(b, u, pa, lhs, Blhs):
                mat, k0, nk, c0, ncol = UNITS[u]
                wv = wview(b, u)
                grp("pe", [I("matmul", PA[pa][:, 0:ncol], lhsT=lhs(kk), rhs=wv[:, kk, :], start=(kk == 0), stop=(kk == nk - 1))
                           for kk in range(nk)], reads=[Blhs, BWB[b]], writes=[BPA[pa]])

            gchunk = 0
            for p, n in enumerate(pieces if _stop != "prepass" else []):
                g0 = gchunk
                op("pool", I("memset", Sf[:], 0.0), writes=[BSf])
                op("pool", I("memset", prevf[:], 0.0), writes=[Bprevf])
                for b_ in range(2):
                    op("pool", I("memset", RAW[b_][:], 0.0), writes=[BRAW[b_]])

                def inproj(i):
                    gc = g0 + i
                    s2, s4 = i % 2, i % 4
                    dma("sp", xt[:], xin[gc * 128:(gc + 1) * 128, :], writes=[Bxt], sem_buf=Bxt)
                    rms_to_hT(xt[:], Bxt, p, (0, 1))
                    dma("sp", rp[s2][:], rope[:, gc * 16:gc * 16 + 16], writes=[Brp[s2]], sem_buf=Brp[s2])
                    Brope = Brp[s2]
                    cosv = rp[s2][:, 0:8]
                    sinv = rp[s2][:, 8:16]
                    pa = 0
                    for u in range(N_IN_UNITS):
                        if _stop == "inproj1":
                            break
                        uu, b = wnext()
                        assert uu == u
                        gemm(b, u, pa, lambda kk: hT[:, kk, :], BhT)
                        _cat = "q" if u < 4 else "k" if u == 4 else "v" if u == 5 else "z" if u <= 9 else "xbc" if u <= 17 else "dt" if u == 18 else "ga" if u <= 22 else "gs"
                        if _stop == "inproj2" or _cat in os.environ.get("KSKIP", "").split(","):
                            pa = 1 - pa
                            continue
                        P_ = PA[pa]
                        BP = BPA[pa]
                        if u <= 4:
                            pat = "p (j g d) -> p g j d" if u < 4 else "p (g j d) -> p g j d"
                            pv = P_[:, :].rearrange("p (g j d) -> p g j d", g=2, j=4)
                            qv = qtok[:].rearrange(pat, g=2, j=4)
                            op("act", I("copy", out=qv, in_=pv), reads=[BP], writes=[Bqtok])
                            cb_ = cosv.unsqueeze(1).unsqueeze(1).to_broadcast([128, 2, 4, 8])
                            sb_ = sinv.unsqueeze(1).unsqueeze(1).to_broadcast([128, 2, 4, 8])
                            x1, x2 = pv[:, :, :, 0:8], pv[:, :, :, 8:16]
                            rv = rt[:].rearrange("p a (g j d) -> p a g j d", g=2, j=4)
                            for a_, (xx, tt) in enumerate(((x1, cb_), (x2, sb_), (x2, cb_), (x1, sb_))):
                                op("dve", I("tensor_tensor", out=rv[:, a_], in0=xx, in1=tt, op=ALU.mult),
                                   reads=[BP, Brope], writes=[Brt])
                            op("dve", I("tensor_tensor", out=qv[:, :, :, 0:8], in0=rv[:, 0], in1=rv[:, 1], op=ALU.subtract), reads=[Brt], writes=[Bqtok])
                            op("dve", I("tensor_tensor", out=qv[:, :, :, 8:16], in0=rv[:, 2], in1=rv[:, 3], op=ALU.add), reads=[Brt], writes=[Bqtok])
                            tb = u % 2
                            if u < 4:
                                for j in range(4):
                                    op("pe", I("transpose", out=PT[tb][:, j * 128:(j + 1) * 128], in_=qtok[:, j * 128:(j + 1) * 128], identity=ident_b),
                                       reads=[Bqtok, Bcbf], writes=[BPT[tb]])
                                op("act", I("copy", out=qT[s2][:, u].rearrange("p j t -> p (j t)"), in_=PT[tb][:, 0:512]),
                                   reads=[BPT[tb]], writes=[BqT[s2]])
                            else:
                                for j in range(4):
                                    op("pe", I("transpose", out=PT[tb][:, j * 128:(j + 1) * 128], in_=qtok[:, j * 128:(j + 1) * 128], identity=ident_b),
                                       reads=[Bqtok, Bcbf], writes=[BPT[tb]])
                                op("act", I("copy", out=kT[s4][:].rearrange("p j t -> p (j t)"), in_=PT[tb][:, 0:512]),
                                   reads=[BPT[tb]], writes=[BkT[s4]])
                        elif u == 5:
                            op("act", I("copy", out=vr[s4][:, :, 0:64], in_=P_[:, :].rearrange("p (h d) -> p h d", d=64)),
                               reads=[BP], writes=[Bvr[s4]])
                            op("pool", I("memset", vr[s4][:, :, 64:65], 1.0), writes=[Bvr[s4]])
                        elif u <= 9:
                            j = u - 6
                            op("act", I("activation", out=zs[:, j * 512:(j + 1) * 512], in_=P_[:, :], func=AF.Silu),
                               reads=[BP], writes=[Bzs])
                            if j == 3:
                                dma("sp", st_zs[gc], zs[:], reads=[Bzs], sem_buf=Bzs)
                        elif u <= 17:
                            j = u - 10
                            tb = u % 2
                            op("act", I("copy", out=xtok[:], in_=P_[:, :]), reads=[BP], writes=[Bxtok])
                            for q_ in range(4):
                                op("pe", I("transpose", out=PT[tb][:, q_ * 128:(q_ + 1) * 128], in_=xtok[:, q_ * 128:(q_ + 1) * 128], identity=ident_b),
                                   reads=[Bxtok, Bcbf], writes=[BPT[tb]])
                            op("dve", I("tensor_copy", out=RAW[s2][:, 4 * j:4 * j + 4, 2:130], in_=PT[tb][:, 0:512].rearrange("p (a t) -> p a t", a=4)),
                               reads=[BPT[tb]], writes=[BRAW[s2]])
                            if j == 7:
                                op("pool", I("tensor_copy", out=RAW[1 - s2][:, :, 130:132], in_=RAW[s2][:, :, 2:4]),
                                   reads=[BRAW[s2]], writes=[BRAW[1 - s2]])
                                if i > 0:
                                    op("pool", I("tensor_copy", out=RAW[s2][:, :, 0:2], in_=RAW[1 - s2][:, :, 128:130]),
                                       reads=[BRAW[1 - s2]], writes=[BRAW[s2]])
                        elif u == 18:
                            op("dve", I("tensor_tensor", out=dtt[s2][:], in0=P_[:, 0:64], in1=smv("dtb"), op=ALU.add),
                               reads=[BP, Bsm], writes=[Bdtt[s2]])
                            op("act", I("activation", out=dtt[s2][:], in_=dtt[s2][:], func=AF.Exp), reads=[Bdtt[s2]], writes=[Bdtt[s2]])
                            op("act", I("activation", out=dtt[s2][:], in_=dtt[s2][:], func=AF.Ln, bias=ones_f[:, 0:1]), reads=[Bdtt[s2], Bcst], writes=[Bdtt[s2]])
                        elif u <= 22:
                            j = u - 19
                            op("act", I("activation", out=ga[s2][:, j * 512:(j + 1) * 512], in_=P_[:, :], func=AF.Sigmoid),
                               reads=[BP], writes=[Bga[s2]])
                        else:
                            j = u - 23
                            op("act", I("activation", out=gs[:, j * 512:(j + 1) * 512], in_=P_[:, :], func=AF.Sigmoid),
                               reads=[BP], writes=[Bgs])
                            if j == 3:
                                dma("sp", st_gs[gc], gs[:], reads=[Bgs], sem_buf=Bgs)
                        pa = 1 - pa

                def mixers(m):
                    gc = g0 + m
                    s2, s4 = m % 2, m % 4
                    first, last = (m == 0), (m == n - 1)
                    if last:
                        op("pool", I("memset", RAW[s2][:, :, 130:132], 0.0), writes=[BRAW[s2]])
                    for blk in range(32):
                        cb2 = blk % 2
                        cw = smt[:, SM["convw"][0] + blk * 5: SM["convw"][0] + blk * 5 + 5]
                        cbias = smt[:, SM["convb"][0] + blk: SM["convb"][0] + blk + 1]
                        op("act", I("activation",
                            out=cacc[cb2][:], in_=RAW[s2][:, blk, 0:128], func=AF.Identity, scale=cw[:, 0:1], bias=cbias),
                           reads=[BRAW[s2], Bsm], writes=[Bcacc[cb2]])
                        for k in range(1, 5):
                            op("dve", I("scalar_tensor_tensor",
                                out=cacc[cb2][:], in0=RAW[s2][:, blk, k:k + 128], scalar=cw[:, k:k + 1], in1=cacc[cb2][:], op0=ALU.mult, op1=ALU.add),
                               reads=[BRAW[s2], Bsm, Bcacc[cb2]], writes=[Bcacc[cb2]])
                        op("act", I("activation", out=XBC[:, blk, :], in_=cacc[cb2][:], func=AF.Silu),
                           reads=[Bcacc[cb2]], writes=[BXBC])
                    dma("sp", st_ct[gc], XBC[:, 24:32, :].rearrange("p a t -> p (a t)"), reads=[BXBC], sem_buf=BXBC)
                    for hh in range(2):
                        for k8 in range(8):
                            op("pe", I("transpose", out=PT[hh][:, k8 * 128:(k8 + 1) * 128], in_=XBC[:, hh * 8 + k8, :], identity=ident_b),
                               reads=[BXBC, Bcbf], writes=[BPT[hh]])
                        op("act", I("copy", out=Xtok[:, hh * 1024:(hh + 1) * 1024], in_=PT[hh][:, :]), reads=[BPT[hh]], writes=[BXtok])
                    for k8 in range(8):
                        op("pe", I("transpose", out=PT[0][:, k8 * 128:(k8 + 1) * 128], in_=XBC[:, 16 + k8, :], identity=ident_b),
                           reads=[BXBC, Bcbf], writes=[BPT[0]])
                    op("act", I("copy", out=Btok[:], in_=PT[0][:, :]), reads=[BPT[0]], writes=[BBtok])
                    dA, At, eAt, dec, Tot, cd, tmpv, Wb = [sv[:, i_, :] for i_ in range(8)]
                    op("dve", I("tensor_tensor", out=dA, in0=dtt[s2][:], in1=A_bc, op=ALU.mult), reads=[Bdtt[s2], Bvecs], writes=[Bsv])
                    grp("pe", [I("matmul", PM[:, 0:32], lhsT=U_f, rhs=dA[:, 0:32], start=True, stop=True),
                               I("matmul", PM[:, 32:64], lhsT=Lw_f, rhs=dA[:, 32:64], start=True, stop=True),
                               I("matmul", PM[:, 64:128], lhsT=ones_f, rhs=dA, start=True, stop=True)],
                        reads=[Bsv, Bcst], writes=[BPMa])
                    op("dve", I("tensor_copy", out=sv[:, 1, :], in_=PM[:, 0:64]), reads=[BPMa], writes=[Bsv])
                    op("dve", I("tensor_copy", out=sv[:, 4, :], in_=PM[:, 64:128]), reads=[BPMa], writes=[Bsv])
                    op("act", I("activation", out=eAt, in_=At, func=AF.Exp), reads=[Bsv], writes=[Bsv])
                    op("dve", I("tensor_tensor", out=tmpv, in0=Tot, in1=At, op=ALU.subtract), reads=[Bsv], writes=[Bsv])
                    op("act", I("activation", out=dec, in_=tmpv, func=AF.Exp), reads=[Bsv], writes=[Bsv])
                    op("act", I("activation", out=cd, in_=Tot, func=AF.Exp), reads=[Bsv], writes=[Bsv])
                    op("pool", I("tensor_copy", out=vst[:, 0:32], in_=eAt[:, 32:64]), reads=[Bsv], writes=[Bvst])
                    op("pool", I("tensor_copy", out=vst[:, 32:64], in_=cd[:, 32:64]), reads=[Bsv], writes=[Bvst])
                    dma("sp", st_vec[gc], vst[:], reads=[Bvst], sem_buf=Bvst)
                    X3 = Xtok[:].rearrange("p (h d) -> p h d", d=64)
                    for d_ in range(2):
                        op("dve", I("tensor_tensor", out=Xdt[d_][:].rearrange("p (h d) -> p h d", d=64), in0=X3,
                                                                   in1=dtt[s2][:, d_ * 32:(d_ + 1) * 32].unsqueeze(2).to_broadcast([128, 32, 64]), op=ALU.mult),
                           reads=[BXtok, Bdtt[s2]], writes=[BXdt[d_]])
                        op("pool", I("tensor_tensor", out=Xdec[d_][:].rearrange("p (h d) -> p h d", d=64), in0=Xdt[d_][:].rearrange("p (h d) -> p h d", d=64),
                                                                    in1=dec[:, d_ * 32:(d_ + 1) * 32].unsqueeze(2).to_broadcast([128, 32, 64]), op=ALU.mult),
                           reads=[BXdt[d_], Bsv], writes=[BXdec[d_]])
                    op("pool", I("tensor_tensor", out=X3, in0=X3,
                                                         in1=smv("dskip").unsqueeze(2).to_broadcast([128, 32, 64]), op=ALU.mult),
                       reads=[BXtok, Bsm], writes=[BXtok])
                    DX, BDX = Xtok, BXtok
                    for g in range(8):
                        BT_g = XBC[:, 16 + g, :]
                        CT_g = XBC[:, 24 + g, :]
                        op("pe", I("matmul", PM[:, 128:256], lhsT=BT_g, rhs=CT_g, start=True, stop=True),
                           reads=[BXBC], writes=[BPMb])
                        op("dve", I("tensor_tensor", out=CBm[:, 0, :], in0=PM[:, 128:256], in1=U_f, op=ALU.mult), reads=[BPMb, Bcst], writes=[BCBm])
                        op("dve", I("tensor_tensor", out=CBm[:, 1, :], in0=PM[:, 128:256], in1=Lw_f, op=ALU.mult), reads=[BPMb, Bcst], writes=[BCBm])
                        for d_ in range(2):
                            msk = U_f if d_ == 0 else Lw_f
                            tri = TU_f if d_ == 0 else TL_f
                            op("pool", I("tensor_tensor",
                                out=Rm[d_][:], in0=msk.unsqueeze(1).to_broadcast([128, 4, 128]),
                                in1=dA[:, d_ * 32 + 4 * g:d_ * 32 + 4 * g + 4].unsqueeze(2).to_broadcast([128, 4, 128]), op=ALU.mult),
                               reads=[Bcst, Bsv], writes=[BRm[d_]])
                            op("pe", I("matmul", PX1[:, :], lhsT=tri, rhs=Rm[d_][:].rearrange("p a t -> p (a t)"), start=True, stop=True),
                               reads=[Bcst, BRm[d_]], writes=[BPX1])
                            op("act", I("activation", out=LT[d_][:].rearrange("p a t -> p (a t)"), in_=PX1[:, :], func=AF.Exp),
                               reads=[BPX1], writes=[BLT[d_]])
                            op("dve", I("tensor_tensor", out=MT[d_][:], in0=LT[d_][:], in1=CBm[:, d_, :].unsqueeze(1).to_broadcast([128, 4, 128]), op=ALU.mult),
                               reads=[BLT[d_], BCBm], writes=[BMT[d_]])
                        fns = []
                        for r_ in range(4):
                            h = 4 * g + r_
                            fns.append(I("matmul", PX2[:, r_ * 64:(r_ + 1) * 64], lhsT=MT[0][:, r_, :], rhs=Xdt[0][:, h * 64:(h + 1) * 64], start=True, stop=False))
                            fns.append(I("matmul", PX2[:, r_ * 64:(r_ + 1) * 64], lhsT=MT[1][:, r_, :], rhs=Xdt[1][:, h * 64:(h + 1) * 64], start=False, stop=False))
                            fns.append(I("matmul", PX2[:, r_ * 64:(r_ + 1) * 64], lhsT=ident_b, rhs=DX[:, h * 64:(h + 1) * 64], start=False, stop=True))
                        grp("pe", fns, reads=[BMT[0], BMT[1], BXdt[0], BXdt[1], BDX, Bcbf], writes=[BPX2[0]])
                        op("pe", I("matmul", PX2[:, 256:512], lhsT=CT_g, rhs=prevf[:, g * 256:(g + 1) * 256], start=True, stop=True),
                           reads=[BXBC, Bprevf], writes=[BPX2[1]])
                        op("dve", I("tensor_tensor", out=ytmp[:].rearrange("p (h d) -> p h d", d=64), in0=PX2[:, 256:512].rearrange("p (h d) -> p h d", d=64),
                                                                 in1=eAt[:, 4 * g:4 * g + 4].unsqueeze(2).to_broadcast([128, 4, 64]), op=ALU.mult),
                           reads=[BPX2[1], Bsv], writes=[Bytmp])
                        op("dve", I("tensor_tensor", out=ypst[:, g * 256:(g + 1) * 256], in0=ytmp[:], in1=PX2[:, 0:256], op=ALU.add),
                           reads=[Bytmp, BPX2[0]], writes=[Bypst])
                        for d_ in range(2):
                            op("pe", I("matmul", PX3[:, d_ * 256:(d_ + 1) * 256], lhsT=Btok[:, g * 128:(g + 1) * 128], rhs=Xdec[d_][:, g * 256:(g + 1) * 256], start=True, stop=True),
                               reads=[BBtok, BXdec[d_]], writes=[BPX3[d_]])
                        Sg = Sf[:, g * 256:(g + 1) * 256].rearrange("p (h d) -> p h d", d=64)
                        op("dve", I("tensor_tensor", out=Sg, in0=Sg, in1=cd[:, 4 * g:4 * g + 4].unsqueeze(2).to_broadcast([128, 4, 64]), op=ALU.mult),
                           reads=[BSf, Bsv, Bprevf], writes=[BSf])
                        op("dve", I("tensor_tensor", out=Sf[:, g * 256:(g + 1) * 256], in0=Sf[:, g * 256:(g + 1) * 256], in1=PX3[:, 0:256], op=ALU.add),
                           reads=[BSf, BPX3[0]], writes=[BSf])
                        op("act", I("copy", out=sbst[:, g * 256:(g + 1) * 256], in_=PX3[:, 256:512]), reads=[BPX3[1]], writes=[Bsbst])
                    op("act", I("copy", out=prevf[:], in_=Sf[:]), reads=[BSf], writes=[Bprevf])
                    dma("sp", st_sb[gc], sbst[:], reads=[Bsbst], sem_buf=Bsbst)
                    dma("sp", st_ypart[gc], ypst[:], reads=[Bypst], sem_buf=Bypst)
                    kbs = [kb for kb in (m - 1, m, m + 1) if 0 <= kb < n]
                    for g in range(8):
                        e_, i_ = g % 2, g // 2
                        rhs_q = qT[s2][e_ * 64:(e_ + 1) * 64, i_].rearrange("p j t -> p (j t)")
                        for t_, kb in enumerate(kbs):
                            pb = (g * 3 + t_) % 3
                            op("pe", I("matmul", PX1[:, :], lhsT=kT[kb % 4][e_ * 64:(e_ + 1) * 64, i_, :], rhs=rhs_q, start=True, stop=True),
                               reads=[BkT[kb % 4], BqT[s2]], writes=[BPX1])
                            op("act", I("activation", out=PTs[pb][:].rearrange("p a t -> p (a t)"), in_=PX1[:, :], func=AF.Exp, scale=0.125),
                               reads=[BPX1], writes=[BPTs[pb]])
                            if kb != m:
                                mk = Lw_b if kb < m else U_b
                                op("pool", I("tensor_tensor", out=PTs[pb][:], in0=PTs[pb][:], in1=mk.unsqueeze(1).to_broadcast([128, 4, 128]), op=ALU.mult),
                                   reads=[BPTs[pb], Bcbf], writes=[BPTs[pb]])
                        fns = []
                        for j in range(4):
                            for t_, kb in enumerate(kbs):
                                pb = (g * 3 + t_) % 3
                                fns.append(I("matmul", PX3[:, j * 65:(j + 1) * 65], lhsT=PTs[pb][:, j, :], rhs=vr[kb % 4][:, g, :],
                                                                                      start=(t_ == 0), stop=(t_ == len(kbs) - 1)))
                        grp("pe", fns, reads=[BPTs[0], BPTs[1], BPTs[2]] + [Bvr[kb % 4] for kb in kbs], writes=[BPX3[0], BPX3[1]])
                        o3 = PX3[:, 0:260].rearrange("p (j d) -> p j d", d=65)
                        op("dve", I("tensor_tensor", out=den[:, 0:4].unsqueeze(2), in0=o3[:, :, 64:65], in1=expsink[:, 4 * g:4 * g + 4].unsqueeze(2), op=ALU.add),
                           reads=[BPX3[0], BPX3[1], Bvecs], writes=[Bden])
                        op("dve", I("reciprocal", out=den[:, 4:8], in_=den[:, 0:4]), reads=[Bden], writes=[Bden])
                        op("dve", I("tensor_tensor", out=ytmp[:].rearrange("p (j d) -> p j d", d=64), in0=o3[:, :, 0:64],
                                                                   in1=den[:, 4:8].unsqueeze(2).to_broadcast([128, 4, 64]), op=ALU.mult),
                           reads=[BPX3[0], BPX3[1], Bden], writes=[Bytmp])
                        op("dve", I("tensor_tensor", out=yast[:, g * 256:(g + 1) * 256], in0=ytmp[:], in1=ga[s2][:, g * 256:(g + 1) * 256], op=ALU.mult),
                           reads=[Bytmp, Bga[s2]], writes=[Byast])
                    dma("sp", st_ya[gc], yast[:], reads=[Byast], sem_buf=Byast)

                for i in range(n):
                    inproj(i)
                    if i > 0 and not _stop.startswith("inproj"):
                        mixers(i - 1)
                if not _stop.startswith("inproj"):
                    mixers(n - 1)
                gchunk += n
            fw.barrier()

        with ExitStack() as s2_:
            yp = sb("yp", [128, D], F32, s2_); Byp = Buf("yp")
            zs2 = sb("zs2", [128, D], BF16, s2_); Bzs2 = Buf("zs2")
            gs2 = sb("gs2", [128, D], BF16, s2_); Bgs2 = Buf("gs2")
            ya2 = sb("ya2", [128, D], BF16, s2_); Bya2 = Buf("ya2")
            ct2 = sb("ct2", [128, 8, 128], BF16, s2_); Bct2 = Buf("ct2")
            sb2 = sb("sb2", [128, D], F32, s2_); Bsb2 = Buf("sb2")
            vc2 = sb("vc2", [128, 64], F32, s2_); Bvc2 = Buf("vc2")
            xt2 = sb("xt2", [128, D], F32, s2_); Bxt2 = Buf("xt2")
            Sb = sb("Sb", [128, D], F32, s2_); BSb = Buf("Sb")
            prevb = sb("prevb", [128, D], BF16, s2_); Bprevb = Buf("prevb")
            yw = sb("yw", [128, D], F32, s2_); Byw = Buf("yw")
            junk2 = sb("junk2", [128, D], BF16, s2_); Bjunk2 = Buf("junk2")
            gst = sb("gst", [128, 24], F32, s2_); Bgst = Buf("gst")
            mg = sb("mg", [128, D], BF16, s2_); Bmg = Buf("mg")
            mT = sb("mT", [128, KC, 128], BF16, s2_); BmT = Buf("mT")
            x1 = sb("x1", [128, D], F32, s2_); Bx1 = Buf("x1")
            xn2 = sb("xn2", [128, D], BF16, s2_); Bxn2 = Buf("xn2")
            nst2 = sb("nst2", [128, 4], F32, s2_); Bnst2 = Buf("nst2")
            h2T = sb("h2T", [128, KC, 128], BF16, s2_); Bh2T = Buf("h2T")
            sg = sb("sg", [128, 512], BF16, s2_); Bsg = Buf("sg")
            acttok = sb("acttok", [128, 512], BF16, s2_); Bacttok = Buf("acttok")
            actT = sb("actT", [128, FC, 128], BF16, s2_); BactT = Buf("actT")
            fo = sb("fo", [128, D], F32, s2_); Bfo = Buf("fo")
            nwbc = sb("nwbc", [128, D], F32, s2_); Bnw = Buf("nwbc", const=True)
            GGt = [sb("GG_%d" % j, [128, D], F32, s2_) for j in range(2)]
            BGGt = [Buf("GG_%d" % j) for j in range(2)]
            dma("sp", nwbc[:], rowbc[:, 0:D], writes=[Bnw], sem_buf=Bnw)
            PB = [PA[0], PA[1], PX1, PX2]
            BPB = [BPA[0], BPA[1], BPX1, BPX2[0]]

            P2_UNITS = list(range(U_OUT0, U_OUT0 + 4)) + list(range(U_GU0, U_GU0 + 22)) + list(range(U_DN0, U_DN0 + 16))
            wq2 = []
            wplan2 = P2_UNITS * NCH
            wpos2 = {"i": 0}

            def wnext2():
                while len(wq2) < NWB and wpos2["i"] < len(wplan2):
                    u = wplan2[wpos2["i"]]
                    wpos2["i"] += 1
                    wq2.append((u, wload(u)))
                return wq2.pop(0)

            def rstd_of(ss_ap, out_ap, Bst, n_el):
                op("dve", I("tensor_scalar", out=out_ap, in0=ss_ap, scalar1=1.0 / n_el, scalar2=EPS, op0=ALU.mult, op1=ALU.add), reads=[Bst], writes=[Bst])
                op("act", I("activation", out=out_ap, in_=out_ap, func=AF.Sqrt), reads=[Bst], writes=[Bst])
                op("dve", I("reciprocal", out=out_ap, in_=out_ap), reads=[Bst], writes=[Bst])

            goff = [sum(pieces[:k]) for k in range(NPC)]
            for p, n in enumerate(pieces if _stop not in ("prepass", "pass1", "inproj") else []):
                op("pool", I("memset", Sb[:], 0.0), writes=[BSb])
                for j in range(2):
                    dma("sp", GGt[j][:], st_gg[2 * p + j], writes=[BGGt[j]], sem_buf=BGGt[j])
                GG = {p: GGt}
                BGG = {p: BGGt}
                for m in range(n - 1, -1, -1):
                    gc = goff[p] + m
                    last = (m == n - 1)
                    dma("sp", yp[:], st_ypart[gc], writes=[Byp], sem_buf=Byp)
                    dma("sp", zs2[:], st_zs[gc], writes=[Bzs2], sem_buf=Bzs2)
                    dma("sp", gs2[:], st_gs[gc], writes=[Bgs2], sem_buf=Bgs2)
                    dma("sp", ya2[:], st_ya[gc], writes=[Bya2], sem_buf=Bya2)
                    dma("sp", ct2[:].rearrange("p a t -> p (a t)"), st_ct[gc], writes=[Bct2], sem_buf=Bct2)
                    dma("sp", sb2[:], st_sb[gc], writes=[Bsb2], sem_buf=Bsb2)
                    dma("sp", vc2[:], st_vec[gc], writes=[Bvc2], sem_buf=Bvc2)
                    dma("sp", xt2[:], xin[gc * 128:(gc + 1) * 128, :], writes=[Bxt2], sem_buf=Bxt2)
                    if not last:
                        op("act", I("copy", out=prevb[:], in_=Sb[:]), reads=[BSb], writes=[Bprevb])
                        for g in range(8):
                            bk = g // 2
                            op("pe", I("matmul", PB[bk][:, (g % 2) * 256:(g % 2 + 1) * 256], lhsT=ct2[:, g, :], rhs=prevb[:, g * 256:(g + 1) * 256], start=True, stop=True),
                               reads=[Bct2, Bprevb], writes=[BPB[bk]])
                        for bk in range(4):
                            op("dve", I("tensor_tensor", out=yw[:, bk * 512:(bk + 1) * 512].rearrange("p (h d) -> p h d", d=64),
                                                                       in0=PB[bk][:, :].rearrange("p (h d) -> p h d", d=64),
                                                                       in1=vc2[:, 8 * bk:8 * bk + 8].unsqueeze(2).to_broadcast([128, 8, 64]), op=ALU.mult),
                               reads=[BPB[bk], Bvc2], writes=[Byw])
                        op("dve", I("tensor_tensor", out=yp[:], in0=yp[:], in1=yw[:], op=ALU.add), reads=[Byp, Byw], writes=[Byp])
                    S3 = Sb[:].rearrange("p (h d) -> p h d", d=64)
                    op("dve", I("tensor_tensor", out=S3, in0=S3, in1=vc2[:, 32:64].unsqueeze(2).to_broadcast([128, 32, 64]), op=ALU.mult),
                       reads=[BSb, Bvc2, Bprevb], writes=[BSb])
                    op("dve", I("tensor_tensor", out=Sb[:], in0=Sb[:], in1=sb2[:], op=ALU.add), reads=[BSb, Bsb2], writes=[BSb])
                    op("dve", I("tensor_tensor", out=yp[:], in0=yp[:], in1=zs2[:], op=ALU.mult), reads=[Byp, Bzs2], writes=[Byp])
                    for g in range(8):
                        op("act", I("activation", out=junk2[:, g * 256:(g + 1) * 256], in_=yp[:, g * 256:(g + 1) * 256], func=AF.Square, accum_out=gst[:, g:g + 1]),
                           reads=[Byp], writes=[Bjunk2, Bgst])
                    rstd_of(gst[:, 0:8], gst[:, 8:16], Bgst, 256.0)
                    op("dve", I("tensor_tensor", out=yp[:].rearrange("p (g d) -> p g d", d=256), in0=yp[:].rearrange("p (g d) -> p g d", d=256),
                                                        in1=gst[:, 8:16].unsqueeze(2).to_broadcast([128, 8, 256]), op=ALU.mult), reads=[Byp, Bgst], writes=[Byp])
                    op("dve", I("tensor_tensor", out=yp[:], in0=yp[:], in1=nwbc[:], op=ALU.mult), reads=[Byp, Bnw], writes=[Byp])
                    op("dve", I("tensor_tensor", out=yp[:], in0=yp[:], in1=gs2[:], op=ALU.mult), reads=[Byp, Bgs2], writes=[Byp])
                    op("dve", I("tensor_tensor", out=mg[:], in0=yp[:], in1=ya2[:], op=ALU.add), reads=[Byp, Bya2], writes=[Bmg])
                    for hh in range(2):
                        for k8 in range(8):
                            op("pe", I("transpose", out=PT[hh][:, k8 * 128:(k8 + 1) * 128], in_=mg[:, (hh * 8 + k8) * 128:(hh * 8 + k8 + 1) * 128], identity=ident_b),
                               reads=[Bmg, Bcbf], writes=[BPT[hh]])
                        op("act", I("copy", out=mT[:, hh * 8:(hh + 1) * 8, :].rearrange("p a t -> p (a t)"), in_=PT[hh][:, :]), reads=[BPT[hh]], writes=[BmT])
                    for cb_ in range(4):
                        u, b = wnext2()
                        wv = wview(b, u)
                        grp("pe", [I("matmul", PB[cb_][:, :], lhsT=mT[:, kk, :], rhs=wv[:, kk, :], start=(kk == 0), stop=(kk == KC - 1)) for kk in range(KC)],
                            reads=[BmT, BWB[b]], writes=[BPB[cb_]])
                    for cb_ in range(4):
                        op("act", I("activation", out=junk2[:, cb_ * 512:(cb_ + 1) * 512], in_=PB[cb_][:, :], func=AF.Square, accum_out=gst[:, 16 + cb_:17 + cb_]),
                           reads=[BPB[cb_]], writes=[Bjunk2, Bgst])
                    op("dve", I("tensor_reduce", out=nst2[:, 0:1], in_=gst[:, 16:20], axis=AX.X, op=ALU.add), reads=[Bgst], writes=[Bnst2])
                    rstd_of(nst2[:, 0:1], nst2[:, 1:2], Bnst2, float(D))
                    for cb_ in range(4):
                        op("dve", I("scalar_tensor_tensor", out=x1[:, cb_ * 512:(cb_ + 1) * 512], in0=PB[cb_][:, :], scalar=nst2[:, 1:2],
                                                                            in1=GG[p][0][:, cb_ * 512:(cb_ + 1) * 512], op0=ALU.mult, op1=ALU.mult),
                           reads=[BPB[cb_], Bnst2, BGG[p][0]], writes=[Bx1])
                    op("dve", I("tensor_tensor", out=x1[:], in0=x1[:], in1=xt2[:], op=ALU.add), reads=[Bx1, Bxt2], writes=[Bx1])
                    op("act", I("activation", out=junk2[:], in_=x1[:], func=AF.Square, accum_out=nst2[:, 2:3]), reads=[Bx1], writes=[Bjunk2, Bnst2])
                    rstd_of(nst2[:, 2:3], nst2[:, 3:4], Bnst2, float(D))
                    op("act", I("activation", out=xn2[:], in_=x1[:], func=AF.Copy, scale=nst2[:, 3:4]), reads=[Bx1, Bnst2], writes=[Bxn2])
                    for hh in range(2):
                        for k8 in range(8):
                            kc = hh * 8 + k8
                            op("pe", I("transpose", out=PT[hh][:, k8 * 128:(k8 + 1) * 128], in_=xn2[:, kc * 128:(kc + 1) * 128], identity=ident_b),
                               reads=[Bxn2, Bcbf], writes=[BPT[hh]])
                        for k8 in range(8):
                            kc = hh * 8 + k8
                            op("dve", I("tensor_scalar", out=h2T[:, kc, :], in0=PT[hh][:, k8 * 128:(k8 + 1) * 128],
                                                                                    scalar1=fm[:, p, 3, kc:kc + 1], scalar2=fm[:, p, 2, kc:kc + 1], op0=ALU.mult, op1=ALU.add),
                               reads=[BPT[hh], Bfm], writes=[Bh2T])
                    for j in range(11):
                        for w_ in range(2):
                            u, b = wnext2()
                            wv = wview(b, u)
                            grp("pe", [I("matmul", PB[w_][:, :], lhsT=h2T[:, kk, :], rhs=wv[:, kk, :], start=(kk == 0), stop=(kk == KC - 1)) for kk in range(KC)],
                                reads=[Bh2T, BWB[b]], writes=[BPB[w_]])
                        op("act", I("activation", out=sg[:], in_=PB[0][:, :], func=AF.Silu), reads=[BPB[0]], writes=[Bsg])
                        op("dve", I("tensor_tensor", out=acttok[:], in0=sg[:], in1=PB[1][:, :], op=ALU.mult), reads=[Bsg, BPB[1]], writes=[Bacttok])
                        tb = j % 2
                        for q_ in range(4):
                            op("pe", I("transpose", out=PT[tb][:, q_ * 128:(q_ + 1) * 128], in_=acttok[:, q_ * 128:(q_ + 1) * 128], identity=ident_b),
                               reads=[Bacttok, Bcbf], writes=[BPT[tb]])
                        op("act", I("copy", out=actT[:, 4 * j:4 * j + 4, :].rearrange("p a t -> p (a t)"), in_=PT[tb][:, 0:512]), reads=[BPT[tb]], writes=[BactT])
                    for kg in range(4):
                        for cb_ in range(4):
                            u, b = wnext2()
                            wv = wview(b, u)
                            grp("pe", [I("matmul", PB[cb_][:, :], lhsT=actT[:, 11 * kg + kk, :], rhs=wv[:, kk, :],
                                                                                      start=(kg == 0 and kk == 0), stop=(kg == 3 and kk == 10)) for kk in range(11)],
                                reads=[BactT, BWB[b]], writes=[BPB[cb_]])
                    for cb_ in range(4):
                        op("act", I("activation", out=junk2[:, cb_ * 512:(cb_ + 1) * 512], in_=PB[cb_][:, :], func=AF.Square, accum_out=gst[:, 20 + cb_:21 + cb_]),
                           reads=[BPB[cb_]], writes=[Bjunk2, Bgst])
                    op("dve", I("tensor_reduce", out=nst2[:, 0:1], in_=gst[:, 20:24], axis=AX.X, op=ALU.add), reads=[Bgst], writes=[Bnst2])
                    rstd_of(nst2[:, 0:1], nst2[:, 1:2], Bnst2, float(D))
                    for cb_ in range(4):
                        op("dve", I("scalar_tensor_tensor", out=fo[:, cb_ * 512:(cb_ + 1) * 512], in0=PB[cb_][:, :], scalar=nst2[:, 1:2],
                                                                            in1=GG[p][1][:, cb_ * 512:(cb_ + 1) * 512], op0=ALU.mult, op1=ALU.mult),
                           reads=[BPB[cb_], Bnst2, BGG[p][1]], writes=[Bfo])
                    op("dve", I("tensor_tensor", out=fo[:], in0=fo[:], in1=x1[:], op=ALU.add), reads=[Bfo, Bx1], writes=[Bfo])
                    dma("sp", yout[gc * 128:(gc + 1) * 128, :], fo[:], reads=[Bfo], sem_buf=Bfo, final=True)
            fw._need("pool", fw.final_waits)
            fw._need("sp", fw.final_waits)
        fw.emit()
    return nc


def host_consts():
    t = np.arange(128)
    ident = np.eye(128, dtype=np.float32)
    U = (t[:, None] <= t[None, :]).astype(np.float32)
    Lw = (t[:, None] >= t[None, :]).astype(np.float32)
    TU = (t[:, None] > t[None, :]).astype(np.float32)
    TL = (t[:, None] < t[None, :]).astype(np.float32)
    ones = np.ones((128, 128), np.float32)
    return np.concatenate([ident, U, Lw, TU, TL, ones], axis=1)


def rope_table(pos0_list):
    half = 8
    inv = (np.float32(500000.0) ** (-np.arange(half, dtype=np.float32) * np.float32(2.0) / np.float32(16))).astype(np.float32)
    cols = []
    for p0 in pos0_list:
        pos = (p0 + np.arange(128)).astype(np.float32)
        ang = (pos[:, None] * inv[None, :]).astype(np.float32)
        cols.append(np.cos(ang).astype(np.float32))
        cols.append(np.sin(ang).astype(np.float32))
    return np.ascontiguousarray(np.concatenate(cols, axis=1))


def fmaj(v, kc):
    return np.ascontiguousarray(np.asarray(v, np.float32).reshape(kc, 128).T)


def bc128(v):
    return np.ascontiguousarray(np.broadcast_to(np.asarray(v, np.float32)[None, :], (128, np.asarray(v).shape[0])))


def make_in_map(xs, cs, P):
    sm = np.zeros((128, SM_W), np.float32)

    def put(name, arr):
        o, w = SM[name]
        sm[:, o:o + arr.shape[1]] = arr

    put("cfm", np.concatenate([fmaj(c, 16) for c in cs], axis=1))
    put("gpre1", fmaj(P["g_pre1"][0], 16))
    put("gpre2", fmaj(P["g_pre2"][0], 16))
    cw = np.asarray(P["conv_w"][0], np.float32)
    put("convw", np.ascontiguousarray(cw.reshape(5, 32, 128).transpose(2, 1, 0)).reshape(128, 160))
    put("convb", fmaj(P["conv_b"][0], 32))
    put("alog", bc128(np.concatenate([P["a_log_f"][0], P["a_log_b"][0]])))
    put("dtb", bc128(np.concatenate([P["dt_bias_f"][0], P["dt_bias_b"][0]])))
    put("dskip", bc128(P["d_skip"][0]))
    put("sinks", bc128(P["sinks"][0]))
    rowbc = np.concatenate([bc128(P["ssd_norm_w"][0]), bc128(P["g_post1"][0]), bc128(P["g_post2"][0]), bc128(P["b_ada"][0])], axis=1)
    pos0 = []
    for x in xs:
        pos0 += [128 * i for i in range(x.shape[0] // 128)]
    return {
        "xin": np.ascontiguousarray(np.concatenate(xs, axis=0), dtype=np.float32),
        "small": sm, "consts": host_consts(), "rope": rope_table(pos0), "rowbc": np.ascontiguousarray(rowbc),
        "w_ada": np.ascontiguousarray(P["w_ada"][0], dtype=np.float32), "w_in": np.ascontiguousarray(P["w_in"][0], dtype=np.float32),
        "w_out": np.ascontiguousarray(P["w_out"][0], dtype=np.float32), "w_gu": np.ascontiguousarray(P["w_gu"][0], dtype=np.float32),
        "w_down": np.ascontiguousarray(P["w_down"][0], dtype=np.float32),
    }


def kernel(x_prompt, x_sample, c_prompt, c_sample, **P):
    x_prompt = np.asarray(x_prompt, np.float32)
    x_sample = np.asarray(x_sample, np.float32)
    c_prompt = np.asarray(c_prompt, np.float32)
    c_sample = np.asarray(c_sample, np.float32)
    P = {k: np.asarray(v, np.float32) for k, v in P.items()}
    B, L, _ = x_prompt.shape
    Bs, Ls, _ = x_sample.shape
    ncores = 8
    zP, zS, zc = np.zeros((L, D), np.float32), np.zeros((Ls, D), np.float32), np.zeros((D,), np.float32)
    maps = []
    for c in range(ncores):
        xa, ca = (x_prompt[c], c_prompt[c]) if c < B else (zP, zc)
        xb, cb = (x_sample[c], c_sample[c]) if c < Bs else (zS, zc)
        maps.append(make_in_map([xa, xb], [ca, cb], P))
    nc = build([L // 128, Ls // 128])
    res = run_bass_kernel_spmd(nc, maps, core_ids=list(range(ncores)))
    y_p = np.stack([res.results[c]["yout"][:L] for c in range(B)], axis=0)
    y_s = np.stack([res.results[c]["yout"][L:L + Ls] for c in range(Bs)], axis=0)
    return (np.ascontiguousarray(y_p, dtype=np.float32), np.ascontiguousarray(y_s, dtype=np.float32))
```

```python
import os
import numpy as np
from contextlib import ExitStack
import concourse.bass as bass
import concourse.mybir as mybir
from concourse.bass_utils import run_bass_kernel_spmd

F32 = mybir.dt.float32
BF16 = mybir.dt.bfloat16
AF = mybir.ActivationFunctionType
ALU = mybir.AluOpType
AX = mybir.AxisListType

D = 2048
KC = 16
DFF = 5632
FC = 44
NH = 32
EPS = 1e-6
IN_W = 13376
NWB1 = int(os.environ.get("KNWB1", "4"))
NWB2 = int(os.environ.get("KNWB2", "5"))


class Buf:
    def __init__(self, name, const=False, excl=False):
        self.name = name
        self.excl = excl
        self.w = None
        self.r = []
        self.dsem = None
        self.dcount = 0
        self.const = const


class Eng:
    def __init__(self, name):
        self.name = name
        self.ops = []
        self.count = 0
        self.waited = {}


class FW:
    def __init__(self, nc, stack):
        self.nc = nc
        self.stack = stack
        self.E = {n: Eng(n) for n in ("pe", "dve", "act", "pool", "sp")}
        self.sems = {}
        for n in self.E:
            self.sems[n] = stack.enter_context(nc.semaphore("s_" + n))
        self.nd = 0
        self.final_waits = []
        self.dma_bufs = []

    def dma_sem(self, buf):
        if buf.dsem is None:
            key = "d%d" % self.nd
            self.nd += 1
            self.sems[key] = self.stack.enter_context(self.nc.semaphore(key))
            buf.dsem = key
            self.dma_bufs.append(buf)
        return buf.dsem

    def _need(self, eng, deps):
        e = self.E[eng]
        best = {}
        for k, v in deps:
            if v > best.get(k, 0):
                best[k] = v
        for k, v in best.items():
            if e.waited.get(k, 0) >= v:
                continue
            e.waited[k] = v
            h = self.sems[k]
            e.ops.append(lambda en, h=h, v=v: en.wait_ge(h, v))

    @staticmethod
    def _deps(reads, writes):
        deps = []
        for b in reads:
            if b.w is not None:
                deps.append(b.w)
            if b.excl:
                deps.extend(b.r)
        for b in writes:
            if b.w is not None:
                deps.append(b.w)
            deps.extend(b.r)
        return deps

    def _mark(self, ev, reads, writes):
        for b in reads:
            if not b.const:
                b.r.append(ev)
        for b in writes:
            b.w = ev
            b.r = []

    def op(self, eng, fn, reads=(), writes=()):
        if eng == "pool" and os.environ.get("KPOOL"):
            eng = os.environ["KPOOL"]
        e = self.E[eng]
        self._need(eng, self._deps(reads, writes))
        e.count += 1
        ev = (eng, e.count)
        h = self.sems[eng]
        e.ops.append(lambda en, fn=fn, h=h: fn(en).then_inc(h, 1))
        self._mark(ev, reads, writes)
        return ev

    def group(self, eng, fns, reads=(), writes=()):
        e = self.E[eng]
        self._need(eng, self._deps(reads, writes))
        e.count += 1
        ev = (eng, e.count)
        h = self.sems[eng]
        n = len(fns)
        for i, fn in enumerate(fns):
            if i == n - 1:
                e.ops.append(lambda en, fn=fn, h=h: fn(en).then_inc(h, 1))
            else:
                e.ops.append(lambda en, fn=fn: fn(en))
        self._mark(ev, reads, writes)
        return ev

    def dma(self, q, out_ap, in_ap, reads=(), writes=(), sem_buf=None, final=False, **kw):
        if q == "pool" and os.environ.get("KDMAQ"):
            q = os.environ["KDMAQ"]
        e = self.E[q]
        self._need(q, self._deps(reads, writes))
        key = self.dma_sem(sem_buf)
        sem_buf.dcount += 16
        ev = (key, sem_buf.dcount)
        h = self.sems[key]
        e.ops.append(lambda en, o=out_ap, i=in_ap, h=h, kw=kw: en.dma_start(out=o, in_=i, **kw).then_inc(h, 16))
        self._mark(ev, reads, writes)
        if final:
            self.final_waits.append(ev)
        return ev

    def barrier(self):
        evs = [(n, e.count) for n, e in self.E.items() if e.count > 0]
        evs += [(b.dsem, b.dcount) for b in self.dma_bufs if b.dcount > 0]
        for n in self.E:
            self._need(n, evs)

    def emit(self):
        nc = self.nc
        E = self.E
        with nc.Block() as block:
            @block.tensor
            def _(en):
                for f in E["pe"].ops:
                    f(en)

            @block.vector
            def _(en):
                for f in E["dve"].ops:
                    f(en)

            @block.scalar
            def _(en):
                for f in E["act"].ops:
                    f(en)

            @block.gpsimd
            def _(en):
                for f in E["pool"].ops:
                    f(en)

            @block.sync
            def _(en):
                for f in E["sp"].ops:
                    f(en)


def I(name, *args, **kwargs):
    return lambda en: getattr(en, name)(*args, **kwargs)


def weight_units():
    U = []
    for i in range(4):
        U.append(("w_in", 0, 16, 512 * i, 512))
    U.append(("w_in", 0, 16, 2048, 512))
    U.append(("w_in", 0, 16, 2560, 512))
    for i in range(4):
        U.append(("w_in", 0, 16, 3072 + 512 * i, 512))
    for i in range(8):
        U.append(("w_in", 0, 16, 5120 + 512 * i, 512))
    U.append(("w_in", 0, 16, 9216, 64))
    for i in range(8):
        U.append(("w_in", 0, 16, 9280 + 512 * i, 512))
    n_in = len(U)
    for i in range(4):
        U.append(("w_out", 0, 16, 512 * i, 512))
    for j in range(11):
        U.append(("w_gu", 0, 16, 512 * j, 512))
        U.append(("w_gu", 0, 16, DFF + 512 * j, 512))
    for kg in range(4):
        for cb in range(4):
            U.append(("w_down", 11 * kg, 11, 512 * cb, 512))
    return U, n_in


UNITS, N_IN_UNITS = weight_units()
U_OUT0 = N_IN_UNITS
U_GU0 = U_OUT0 + 4
U_DN0 = U_GU0 + 22

SM = {}
_o = 0
for _n, _w in [("cfm", 128), ("gpre1", 16), ("gpre2", 16), ("convw", 160), ("convb", 32), ("alog", 64),
               ("dtb", 64), ("dskip", 32), ("sinks", 32)]:
    SM[_n] = (_o, _w)
    _o += _w
SM_W = _o
CONST_W = 6 * 128


def build(pieces):
    pieces = list(pieces)
    NPC = len(pieces)
    assert NPC % 2 == 0 and NPC <= 8
    NCH = sum(pieces)
    NT = NCH * 128
    nc = bass.Bass("TRN2", target_bir_lowering=False)

    def din(name, shape):
        return nc.dram_tensor(name, shape, F32, kind="ExternalInput").ap()

    xin = din("xin", [NT, D])
    small = din("small", [128, SM_W])
    consts = din("consts", [128, CONST_W])
    rope = din("rope", [128, NCH * 16])
    rowbc = din("rowbc", [128, 3 * D + 6 * D])
    W = {"w_ada": din("w_ada", [D, 6 * D]), "w_in": din("w_in", [D, IN_W]), "w_out": din("w_out", [D, D]),
         "w_gu": din("w_gu", [D, 2 * DFF]), "w_down": din("w_down", [DFF, D])}
    yout = nc.dram_tensor("yout", [NT, D], F32, kind="ExternalOutput").ap()

    def dscr(name, shape, dt):
        return nc.dram_tensor(name, shape, dt, kind="Internal").ap()

    wscr = dscr("wscr", [len(UNITS), 128, 8192], BF16)
    class PieceScr:
        def __init__(self, name, width, dt):
            self.t = [dscr("%s_%d" % (name, k), [n, 128, width], dt) for k, n in enumerate(pieces)]
            self.map = []
            for k, n in enumerate(pieces):
                self.map += [(k, i) for i in range(n)]

        def __getitem__(self, gc):
            k, i = self.map[gc]
            return self.t[k][i]

    st_ypart = PieceScr("st_ypart", D, BF16)
    st_zs = PieceScr("st_zs", D, BF16)
    st_gs = PieceScr("st_gs", D, BF16)
    st_ya = PieceScr("st_ya", D, BF16)
    st_ct = PieceScr("st_ct", 1024, BF16)
    st_sb = PieceScr("st_sb", D, BF16)
    st_vec = PieceScr("st_vec", 64, F32)
    st_gg = dscr("st_gg", [2 * NPC, 128, D], F32)

    with ExitStack() as st:
        fw = FW(nc, st)
        op, grp, dma = fw.op, fw.group, fw.dma

        def sb(name, shape, dt, stack=st):
            return stack.enter_context(nc.sbuf_tensor(name, shape, dt))

        def ps(name, shape, dt, stack=st):
            return stack.enter_context(nc.psum_tensor(name, shape, dt))

        PA = [ps("PA%d" % i, [128, 512], F32) for i in range(2)]
        BPA = [Buf("PA%d" % i, excl=True) for i in range(2)]
        PT = [ps("PT%d" % i, [128, 1024], BF16) for i in range(2)]
        BPT = [Buf("PT%d" % i, excl=True) for i in range(2)]
        PX1 = ps("PX1", [128, 512], F32); BPX1 = Buf("PX1", excl=True)
        PX2 = ps("PX2", [128, 512], F32); _b2 = Buf("PX2", excl=True); BPX2 = [_b2, _b2]
        PX3 = ps("PX3", [128, 512], F32); _b3 = Buf("PX3", excl=True); BPX3 = [_b3, _b3]
        PM = ps("PM", [128, 512], F32); BPMa = Buf("PM", excl=True); BPMb = BPMa

        smt = sb("smt", [128, SM_W], F32); Bsm = Buf("smt", const=True)
        cst = sb("cst", [128, CONST_W], F32); Bcst = Buf("cst", const=True)
        cbf = sb("cbf", [128, 3 * 128], BF16); Bcbf = Buf("cbf", const=True)
        fm = sb("fm", [128, NPC, 4, 16], F32); Bfm = Buf("fm")
        vecs = sb("vecs", [128, 4 * 32 + 64], F32); Bvecs = Buf("vecs")
        WB, BWB = [], []
        wstate = {"n": 0}

        def alloc_wb(nb, stack):
            WB[:] = [sb("WB%d_%d" % (nb, i), [128, 8192], BF16, stack) for i in range(nb)]
            BWB[:] = [Buf("WB%d" % i) for i in range(nb)]
            wstate["n"] = 0

        ident_f = cst[:, 0:128]
        U_f = cst[:, 128:256]
        Lw_f = cst[:, 256:384]
        TU_f = cst[:, 384:512]
        TL_f = cst[:, 512:640]
        ones_f = cst[:, 640:768]
        ident_b = cbf[:, 0:128]
        U_b = cbf[:, 128:256]
        Lw_b = cbf[:, 256:384]

        def smv(name):
            o, w = SM[name]
            return smt[:, o:o + w]

        dma("sp", smt[:], small, writes=[Bsm], sem_buf=Bsm)
        dma("sp", cst[:], consts, writes=[Bcst], sem_buf=Bcst)
        op("dve", I("tensor_copy", out=cbf[:], in_=cst[:, 0:384]), reads=[Bcst], writes=[Bcbf])
        op("act", I("activation", out=vecs[:, 0:64], in_=smv("alog"), func=AF.Exp), reads=[Bsm], writes=[Bvecs])
        op("dve", I("tensor_scalar", out=vecs[:, 0:64], in0=vecs[:, 0:64], scalar1=-1.0, scalar2=None, op0=ALU.mult),
           reads=[Bvecs], writes=[Bvecs])
        op("act", I("activation", out=vecs[:, 96:128], in_=smv("sinks"), func=AF.Exp), reads=[Bsm], writes=[Bvecs])
        A_bc = vecs[:, 0:64]
        expsink = vecs[:, 96:128]

        with ExitStack() as s0:
            stg = [sb("stg%d" % i, [128, 8, 512], F32, s0) for i in range(2)]
            Bstg = [Buf("stg%d" % i) for i in range(2)]
            cvt = [sb("cvt%d" % i, [128, 8, 512], BF16, s0) for i in range(2)]
            Bcvt = [Buf("cvt%d" % i) for i in range(2)]
            screp = sb("screp", [128, 2, KC, 128], F32, s0); Bscrep = Buf("screp")
            scf = sb("scf", [128, 128], F32, s0); Bscf = Buf("scf")
            modb = sb("modb", [128, 512], F32, s0); Bmodb = Buf("modb")
            badab = [sb("badab%d" % i, [128, 512], F32, s0) for i in range(2)]
            Bbada = [Buf("badab%d" % i) for i in range(2)]
            gpost = sb("gpost", [128, 2 * D], F32, s0); Bgpost = Buf("gpost")
            tmpx = sb("tmpx", [128, 4, 128], F32, s0); Btmpx = Buf("tmpx")
            ggst = sb("ggst", [128, 512], F32, s0); Bggst = Buf("ggst")
            dma("sp", gpost[:], rowbc[:, D:3 * D], writes=[Bgpost], sem_buf=Bgpost)
            op("act", I("activation", out=scf[:], in_=smv("cfm"), func=AF.Silu), reads=[Bsm], writes=[Bscf])
            si = 0
            for r0 in range(0, NPC, 2):
                for r in range(2):
                    op("dve", I("tensor_copy",
                        out=screp[:, r], in_=scf[:, 16 * (r0 + r):16 * (r0 + r) + 16].unsqueeze(2).to_broadcast([128, KC, 128])),
                       reads=[Bscf], writes=[Bscrep])
                for blk in range(24):
                    seg, jb = blk // 4, blk % 4
                    bb = blk % 2
                    dma("sp", badab[bb][:], rowbc[:, 3 * D + blk * 512:3 * D + (blk + 1) * 512], writes=[Bbada[bb]], sem_buf=Bbada[bb])
                    for half in range(2):
                        b = si % 2
                        si += 1
                        src = W["w_ada"][half * 1024:(half + 1) * 1024, blk * 512:(blk + 1) * 512].rearrange("(k p) c -> p k c", p=128)
                        dma("sp", stg[b][:], src, writes=[Bstg[b]], sem_buf=Bstg[b])
                        for r in range(2):
                            grp("pe", [I("matmul",
                                PA[r][:, :], lhsT=screp[:, r, half * 8 + kk, :], rhs=stg[b][:, kk, :],
                                start=(half == 0 and kk == 0), stop=(half == 1 and kk == 7)) for kk in range(8)],
                                reads=[Bscrep, Bstg[b]], writes=[BPA[r]])
                    for r in range(2):
                        op("dve", I("tensor_tensor", out=modb[:], in0=PA[r][:, :], in1=badab[bb][:], op=ALU.add),
                           reads=[BPA[r], Bbada[bb]], writes=[Bmodb])
                        if seg in (2, 5):
                            j = 0 if seg == 2 else 1
                            op("dve", I("tensor_tensor",
                                out=ggst[:], in0=modb[:], in1=gpost[:, j * D + jb * 512:j * D + (jb + 1) * 512], op=ALU.mult),
                               reads=[Bmodb, Bgpost], writes=[Bggst])
                            dma("sp", st_gg[2 * (r0 + r) + j, :, jb * 512:(jb + 1) * 512], ggst[:], reads=[Bggst], sem_buf=Bggst)
                        else:
                            slot = {0: 0, 1: 1, 3: 2, 4: 3}[seg]
                            op("dve", I("tensor_tensor",
                                out=tmpx[:], in0=modb[:].rearrange("p (a b) -> p a b", a=4),
                                in1=ident_f.unsqueeze(1).to_broadcast([128, 4, 128]), op=ALU.mult),
                               reads=[Bmodb, Bcst], writes=[Btmpx])
                            op("dve", I("tensor_reduce",
                                out=fm[:, r0 + r, slot, 4 * jb:4 * jb + 4], in_=tmpx[:], axis=AX.X, op=ALU.add),
                               reads=[Btmpx], writes=[Bfm])
            for r in range(NPC):
                for slot, gname in ((1, "gpre1"), (3, "gpre2")):
                    op("dve", I("tensor_scalar", out=fm[:, r, slot, :], in0=fm[:, r, slot, :], scalar1=1.0, scalar2=None, op0=ALU.add),
                       reads=[Bfm], writes=[Bfm])
                    op("dve", I("tensor_tensor", out=fm[:, r, slot, :], in0=fm[:, r, slot, :], in1=smv(gname), op=ALU.mult),
                       reads=[Bfm, Bsm], writes=[Bfm])
            ce = 0
            for u, (mat, k0, nk, c0, ncol) in enumerate(UNITS):
                for h0 in range(0, nk, 8):
                    hk = min(8, nk - h0)
                    b = si % 2
                    si += 1
                    src = W[mat][(k0 + h0) * 128:(k0 + h0 + hk) * 128, c0:c0 + ncol].rearrange("(k p) c -> p k c", p=128)
                    dma("sp", stg[b][:, 0:hk, 0:ncol], src, writes=[Bstg[b]], sem_buf=Bstg[b])
                    eng = ("act", "dve", "pool")[ce % 3]
                    ce += 1
                    if eng == "act":
                        op("act", I("copy", out=cvt[b][:, 0:hk, 0:ncol], in_=stg[b][:, 0:hk, 0:ncol]),
                           reads=[Bstg[b]], writes=[Bcvt[b]])
                    else:
                        op(eng, I("tensor_copy", out=cvt[b][:, 0:hk, 0:ncol], in_=stg[b][:, 0:hk, 0:ncol]),
                           reads=[Bstg[b]], writes=[Bcvt[b]])
                    dst = wscr[u, :, h0 * ncol:(h0 + hk) * ncol].rearrange("p (k c) -> p k c", c=ncol)
                    dma("sp", dst, cvt[b][:, 0:hk, 0:ncol], reads=[Bcvt[b]], sem_buf=Bcvt[b])
            fw.barrier()

        def wload(u):
            b = wstate["n"] % len(WB)
            wstate["n"] += 1
            mat, k0, nk, c0, ncol = UNITS[u]
            dma("sp", WB[b][:, 0:nk * ncol], wscr[u, :, 0:nk * ncol], writes=[BWB[b]], sem_buf=BWB[b])
            return b

        def wview(b, u):
            mat, k0, nk, c0, ncol = UNITS[u]
            return WB[b][:, 0:nk * ncol].rearrange("p (k c) -> p k c", c=ncol)

        _stop = os.environ.get("KSTOP", "")
        with ExitStack() as s1:
            alloc_wb(NWB1, s1)
            xt = sb("xt", [128, D], F32, s1); Bxt = Buf("xt")
            xn = sb("xn", [128, D], BF16, s1); Bxn = Buf("xn")
            nst = sb("nst", [128, 4], F32, s1); Bnst = Buf("nst")
            hT = sb("hT", [128, KC, 128], BF16, s1); BhT = Buf("hT")
            qtok = sb("qtok", [128, 512], BF16, s1); Bqtok = Buf("qtok")
            rt = sb("rt", [128, 4, 64], F32, s1); Brt = Buf("rt")
            qT = [sb("qT%d" % i, [128, 4, 4, 128], BF16, s1) for i in range(2)]
            BqT = [Buf("qT%d" % i) for i in range(2)]
            kT = [sb("kT%d" % i, [128, 4, 128], BF16, s1) for i in range(4)]
            BkT = [Buf("kT%d" % i) for i in range(4)]
            vr = [sb("vr%d" % i, [128, 8, 65], BF16, s1) for i in range(4)]
            Bvr = [Buf("vr%d" % i) for i in range(4)]
            zs = sb("zs", [128, D], BF16, s1); Bzs = Buf("zs")
            gs, Bgs = zs, Bzs
            ga = [sb("ga%d" % i, [128, D], BF16, s1) for i in range(2)]
            Bga = [Buf("ga%d" % i) for i in range(2)]
            dtt = [sb("dtt%d" % i, [128, 64], F32, s1) for i in range(2)]
            Bdtt = [Buf("dtt%d" % i) for i in range(2)]
            xtok = sb("xtok", [128, 512], BF16, s1); Bxtok = Buf("xtok")
            RAW = [sb("RAW%d" % i, [128, 32, 132], BF16, s1) for i in range(2)]
            BRAW = [Buf("RAW%d" % i) for i in range(2)]
            cacc = [sb("cacc%d" % i, [128, 128], F32, s1) for i in range(6)]
            Bcacc = [Buf("cacc%d" % i) for i in range(6)]
            XBC = sb("XBC", [128, 32, 128], BF16, s1); BXBC = Buf("XBC")
            Xtok = sb("Xtok", [128, D], BF16, s1); BXtok = Buf("Xtok")
            Btok = sb("Btok", [128, 1024], BF16, s1); BBtok = Buf("Btok")
            Xdt = [sb("Xdt%d" % i, [128, D], BF16, s1) for i in range(2)]
            BXdt = [Buf("Xdt%d" % i) for i in range(2)]
            Xdec = [sb("Xdec%d" % i, [128, D], BF16, s1) for i in range(2)]
            BXdec = [Buf("Xdec%d" % i) for i in range(2)]
            sv = sb("sv", [128, 8, 64], F32, s1); Bsv = Buf("sv")
            vst = sb("vst", [128, 64], F32, s1); Bvst = Buf("vst")
            Rm = [sb("Rm%d" % i, [128, 4, 128], F32, s1) for i in range(2)]
            BRm = [Buf("Rm%d" % i) for i in range(2)]
            LT = [sb("LT%d" % i, [128, 4, 128], BF16, s1) for i in range(2)]
            BLT = [Buf("LT%d" % i) for i in range(2)]
            CBm = sb("CBm", [128, 2, 128], BF16, s1); BCBm = Buf("CBm")
            MT = [sb("MT%d" % i, [128, 4, 128], BF16, s1) for i in range(2)]
            BMT = [Buf("MT%d" % i) for i in range(2)]
            Sf = sb("Sf", [128, D], F32, s1); BSf = Buf("Sf")
            prevf = sb("prevf", [128, D], BF16, s1); Bprevf = Buf("prevf")
            sbst = sb("sbst", [128, D], BF16, s1); Bsbst = Buf("sbst")
            ypst = sb("ypst", [128, D], BF16, s1); Bypst = Buf("ypst")
            ytmp = sb("ytmp", [128, 256], F32, s1); Bytmp = Buf("ytmp")
            PTs = [sb("PTs%d" % i, [128, 4, 128], BF16, s1) for i in range(3)]
            BPTs = [Buf("PTs%d" % i) for i in range(3)]
            yast = sb("yast", [128, D], BF16, s1); Byast = Buf("yast")
            den = sb("den", [128, 8], F32, s1); Bden = Buf("den")
            rp = [sb("rp%d" % i, [128, 16], F32, s1) for i in range(2)]
            Brp = [Buf("rp%d" % i) for i in range(2)]

            wq = []
            wplan = []
            for p, n in enumerate(pieces):
                for i in range(n):
                    wplan.extend(range(N_IN_UNITS))
            wpos = {"i": 0}

            def wnext():
                while len(wq) < len(WB) and wpos["i"] < len(wplan):
                    u = wplan[wpos["i"]]
                    wpos["i"] += 1
                    wq.append((u, wload(u)))
                return wq.pop(0)

            def rms_to_hT(xsrc_ap, Bx, r, slots):
                sh_s, gsc_s = slots
                op("act", I("activation", out=xn[:], in_=xsrc_ap, func=AF.Square, accum_out=nst[:, 0:1]),
                   reads=[Bx], writes=[Bxn, Bnst])
                op("dve", I("tensor_scalar", out=nst[:, 1:2], in0=nst[:, 0:1], scalar1=1.0 / D, scalar2=EPS, op0=ALU.mult, op1=ALU.add),
                   reads=[Bnst], writes=[Bnst])
                op("act", I("activation", out=nst[:, 2:3], in_=nst[:, 1:2], func=AF.Sqrt), reads=[Bnst], writes=[Bnst])
                op("dve", I("reciprocal", out=nst[:, 3:4], in_=nst[:, 2:3]), reads=[Bnst], writes=[Bnst])
                op("act", I("activation", out=xn[:], in_=xsrc_ap, func=AF.Copy, scale=nst[:, 3:4]),
                   reads=[Bx, Bnst], writes=[Bxn])
                for hh in range(2):
                    for k8 in range(8):
                        kc = hh * 8 + k8
                        op("pe", I("transpose", out=PT[hh][:, k8 * 128:(k8 + 1) * 128], in_=xn[:, kc * 128:(kc + 1) * 128], identity=ident_b),
                           reads=[Bxn, Bcbf], writes=[BPT[hh]])
                    for k8 in range(8):
                        kc = hh * 8 + k8
                        op("dve", I("tensor_scalar",
                            out=hT[:, kc, :], in0=PT[hh][:, k8 * 128:(k8 + 1) * 128],
                            scalar1=fm[:, r, gsc_s, kc:kc + 1], scalar2=fm[:, r, sh_s, kc:kc + 1], op0=ALU.mult, op1=ALU.add),
                           reads=[BPT[hh], Bfm], writes=[BhT])

            def gemm(b, u, pa, lhs, Blhs):
                mat, k0, nk, c0, ncol = UNITS[u]
                wv = wview(b, u)
                grp("pe", [I("matmul", PA[pa][:, 0:ncol], lhsT=lhs(kk), rhs=wv[:, kk, :], start=(kk == 0), stop=(kk == nk - 1))
                           for kk in range(nk)], reads=[Blhs, BWB[b]], writes=[BPA[pa]])

            gchunk = 0
            for p, n in enumerate(pieces if _stop != "prepass" else []):
                g0 = gchunk
                op("pool", I("memset", Sf[:], 0.0), writes=[BSf])
                op("pool", I("memset", prevf[:], 0.0), writes=[Bprevf])
                for b_ in range(2):
                    op("pool", I("memset", RAW[b_][:], 0.0), writes=[BRAW[b_]])

                def inproj(i):
                    gc = g0 + i
                    s2, s4 = i % 2, i % 4
                    dma("sp", xt[:], xin[gc * 128:(gc + 1) * 128, :], writes=[Bxt], sem_buf=Bxt)
                    rms_to_hT(xt[:], Bxt, p, (0, 1))
                    dma("sp", rp[s2][:], rope[:, gc * 16:gc * 16 + 16], writes=[Brp[s2]], sem_buf=Brp[s2])
                    Brope = Brp[s2]
                    cosv = rp[s2][:, 0:8]
                    sinv = rp[s2][:, 8:16]
                    pa = 0
                    for u in range(N_IN_UNITS):
                        if _stop == "inproj1":
                            break
                        uu, b = wnext()
                        assert uu == u
                        gemm(b, u, pa, lambda kk: hT[:, kk, :], BhT)
                        _cat = "q" if u < 4 else "k" if u == 4 else "v" if u == 5 else "z" if u <= 9 else "xbc" if u <= 17 else "dt" if u == 18 else "ga" if u <= 22 else "gs"
                        if _stop == "inproj2" or _cat in os.environ.get("KSKIP", "").split(","):
                            pa = 1 - pa
                            continue
                        P_ = PA[pa]
                        BP = BPA[pa]
                        if u <= 4:
                            pat = "p (j g d) -> p g j d" if u < 4 else "p (g j d) -> p g j d"
                            pv = P_[:, :].rearrange("p (g j d) -> p g j d", g=2, j=4)
                            qv = qtok[:].rearrange(pat, g=2, j=4)
                            op("act", I("copy", out=qv, in_=pv), reads=[BP], writes=[Bqtok])
                            cb_ = cosv.unsqueeze(1).unsqueeze(1).to_broadcast([128, 2, 4, 8])
                            sb_ = sinv.unsqueeze(1).unsqueeze(1).to_broadcast([128, 2, 4, 8])
                            x1, x2 = pv[:, :, :, 0:8], pv[:, :, :, 8:16]
                            rv = rt[:].rearrange("p a (g j d) -> p a g j d", g=2, j=4)
                            for a_, (xx, tt) in enumerate(((x1, cb_), (x2, sb_), (x2, cb_), (x1, sb_))):
                                op("dve", I("tensor_tensor", out=rv[:, a_], in0=xx, in1=tt, op=ALU.mult),
                                   reads=[BP, Brope], writes=[Brt])
                            op("dve", I("tensor_tensor", out=qv[:, :, :, 0:8], in0=rv[:, 0], in1=rv[:, 1], op=ALU.subtract), reads=[Brt], writes=[Bqtok])
                            op("dve", I("tensor_tensor", out=qv[:, :, :, 8:16], in0=rv[:, 2], in1=rv[:, 3], op=ALU.add), reads=[Brt], writes=[Bqtok])
                            tb = u % 2
                            if u < 4:
                                for j in range(4):
                                    op("pe", I("transpose", out=PT[tb][:, j * 128:(j + 1) * 128], in_=qtok[:, j * 128:(j + 1) * 128], identity=ident_b),
                                       reads=[Bqtok, Bcbf], writes=[BPT[tb]])
                                op("act", I("copy", out=qT[s2][:, u].rearrange("p j t -> p (j t)"), in_=PT[tb][:, 0:512]),
                                   reads=[BPT[tb]], writes=[BqT[s2]])
                            else:
                                for j in range(4):
                                    op("pe", I("transpose", out=PT[tb][:, j * 128:(j + 1) * 128], in_=qtok[:, j * 128:(j + 1) * 128], identity=ident_b),
                                       reads=[Bqtok, Bcbf], writes=[BPT[tb]])
                                op("act", I("copy", out=kT[s4][:].rearrange("p j t -> p (j t)"), in_=PT[tb][:, 0:512]),
                                   reads=[BPT[tb]], writes=[BkT[s4]])
                        elif u == 5:
                            op("act", I("copy", out=vr[s4][:, :, 0:64], in_=P_[:, :].rearrange("p (h d) -> p h d", d=64)),
                               reads=[BP], writes=[Bvr[s4]])
                            op("pool", I("memset", vr[s4][:, :, 64:65], 1.0), writes=[Bvr[s4]])
                        elif u <= 9:
                            j = u - 6
                            op("act", I("activation", out=zs[:, j * 512:(j + 1) * 512], in_=P_[:, :], func=AF.Silu),
                               reads=[BP], writes=[Bzs])
                            if j == 3:
                                dma("sp", st_zs[gc], zs[:], reads=[Bzs], sem_buf=Bzs)
                        elif u <= 17:
                            j = u - 10
                            tb = u % 2
                            op("act", I("copy", out=xtok[:], in_=P_[:, :]), reads=[BP], writes=[Bxtok])
                            for q_ in range(4):
                                op("pe", I("transpose", out=PT[tb][:, q_ * 128:(q_ + 1) * 128], in_=xtok[:, q_ * 128:(q_ + 1) * 128], identity=ident_b),
                                   reads=[Bxtok, Bcbf], writes=[BPT[tb]])
                            op("dve", I("tensor_copy", out=RAW[s2][:, 4 * j:4 * j + 4, 2:130], in_=PT[tb][:, 0:512].rearrange("p (a t) -> p a t", a=4)),
                               reads=[BPT[tb]], writes=[BRAW[s2]])
                            if j == 7:
                                op("pool", I("tensor_copy", out=RAW[1 - s2][:, :, 130:132], in_=RAW[s2][:, :, 2:4]),
                                   reads=[BRAW[s2]], writes=[BRAW[1 - s2]])
                                if i > 0:
                                    op("pool", I("tensor_copy", out=RAW[s2][:, :, 0:2], in_=RAW[1 - s2][:, :, 128:130]),
                                       reads=[BRAW[1 - s2]], writes=[BRAW[s2]])
                        elif u == 18:
                            op("dve", I("tensor_tensor", out=dtt[s2][:], in0=P_[:, 0:64], in1=smv("dtb"), op=ALU.add),
                               reads=[BP, Bsm], writes=[Bdtt[s2]])
                            op("act", I("activation", out=dtt[s2][:], in_=dtt[s2][:], func=AF.Exp), reads=[Bdtt[s2]], writes=[Bdtt[s2]])
                            op("act", I("activation", out=dtt[s2][:], in_=dtt[s2][:], func=AF.Ln, bias=ones_f[:, 0:1]), reads=[Bdtt[s2], Bcst], writes=[Bdtt[s2]])
                        elif u <= 22:
                            j = u - 19
                            op("act", I("activation", out=ga[s2][:, j * 512:(j + 1) * 512], in_=P_[:, :], func=AF.Sigmoid),
                               reads=[BP], writes=[Bga[s2]])
                        else:
                            j = u - 23
                            op("act", I("activation", out=gs[:, j * 512:(j + 1) * 512], in_=P_[:, :], func=AF.Sigmoid),
                               reads=[BP], writes=[Bgs])
                            if j == 3:
                                dma("sp", st_gs[gc], gs[:], reads=[Bgs], sem_buf=Bgs)
                        pa = 1 - pa

                def mixers(m):
                    gc = g0 + m
                    s2, s4 = m % 2, m % 4
                    first, last = (m == 0), (m == n - 1)
                    if last:
                        op("pool", I("memset", RAW[s2][:, :, 130:132], 0.0), writes=[BRAW[s2]])
                    for blk in range(32):
                        cb2 = blk % 6
                        cw = smt[:, SM["convw"][0] + blk * 5: SM["convw"][0] + blk * 5 + 5]
                        cbias = smt[:, SM["convb"][0] + blk: SM["convb"][0] + blk + 1]
                        op("act", I("activation",
                            out=cacc[cb2][:], in_=RAW[s2][:, blk, 0:128], func=AF.Identity, scale=cw[:, 0:1], bias=cbias),
                           reads=[BRAW[s2], Bsm], writes=[Bcacc[cb2]])
                        for k in range(1, 5):
                            op("dve", I("scalar_tensor_tensor",
                                out=cacc[cb2][:], in0=RAW[s2][:, blk, k:k + 128], scalar=cw[:, k:k + 1], in1=cacc[cb2][:], op0=ALU.mult, op1=ALU.add),
                               reads=[BRAW[s2], Bsm, Bcacc[cb2]], writes=[Bcacc[cb2]])
                        op("act", I("activation", out=XBC[:, blk, :], in_=cacc[cb2][:], func=AF.Silu),
                           reads=[Bcacc[cb2]], writes=[BXBC])
                    dma("sp", st_ct[gc], XBC[:, 24:32, :].rearrange("p a t -> p (a t)"), reads=[BXBC], sem_buf=BXBC)
                    for hh in range(2):
                        for k8 in range(8):
                            op("pe", I("transpose", out=PT[hh][:, k8 * 128:(k8 + 1) * 128], in_=XBC[:, hh * 8 + k8, :], identity=ident_b),
                               reads=[BXBC, Bcbf], writes=[BPT[hh]])
                        op("act", I("copy", out=Xtok[:, hh * 1024:(hh + 1) * 1024], in_=PT[hh][:, :]), reads=[BPT[hh]], writes=[BXtok])
                    for k8 in range(8):
                        op("pe", I("transpose", out=PT[0][:, k8 * 128:(k8 + 1) * 128], in_=XBC[:, 16 + k8, :], identity=ident_b),
                           reads=[BXBC, Bcbf], writes=[BPT[0]])
                    op("act", I("copy", out=Btok[:], in_=PT[0][:, :]), reads=[BPT[0]], writes=[BBtok])
                    dA, At, eAt, dec, Tot, cd, tmpv, Wb = [sv[:, i_, :] for i_ in range(8)]
                    op("dve", I("tensor_tensor", out=dA, in0=dtt[s2][:], in1=A_bc, op=ALU.mult), reads=[Bdtt[s2], Bvecs], writes=[Bsv])
                    grp("pe", [I("matmul", PM[:, 0:32], lhsT=U_f, rhs=dA[:, 0:32], start=True, stop=True),
                               I("matmul", PM[:, 32:64], lhsT=Lw_f, rhs=dA[:, 32:64], start=True, stop=True),
                               I("matmul", PM[:, 64:128], lhsT=ones_f, rhs=dA, start=True, stop=True)],
                        reads=[Bsv, Bcst], writes=[BPMa])
                    op("dve", I("tensor_copy", out=sv[:, 1, :], in_=PM[:, 0:64]), reads=[BPMa], writes=[Bsv])
                    op("dve", I("tensor_copy", out=sv[:, 4, :], in_=PM[:, 64:128]), reads=[BPMa], writes=[Bsv])
                    op("act", I("activation", out=eAt, in_=At, func=AF.Exp), reads=[Bsv], writes=[Bsv])
                    op("dve", I("tensor_tensor", out=tmpv, in0=Tot, in1=At, op=ALU.subtract), reads=[Bsv], writes=[Bsv])
                    op("act", I("activation", out=dec, in_=tmpv, func=AF.Exp), reads=[Bsv], writes=[Bsv])
                    op("act", I("activation", out=cd, in_=Tot, func=AF.Exp), reads=[Bsv], writes=[Bsv])
                    op("pool", I("tensor_copy", out=vst[:, 0:32], in_=eAt[:, 32:64]), reads=[Bsv], writes=[Bvst])
                    op("pool", I("tensor_copy", out=vst[:, 32:64], in_=cd[:, 32:64]), reads=[Bsv], writes=[Bvst])
                    dma("sp", st_vec[gc], vst[:], reads=[Bvst], sem_buf=Bvst)
                    X3 = Xtok[:].rearrange("p (h d) -> p h d", d=64)
                    for d_ in range(2):
                        op("dve", I("tensor_tensor", out=Xdt[d_][:].rearrange("p (h d) -> p h d", d=64), in0=X3,
                                                                   in1=dtt[s2][:, d_ * 32:(d_ + 1) * 32].unsqueeze(2).to_broadcast([128, 32, 64]), op=ALU.mult),
                           reads=[BXtok, Bdtt[s2]], writes=[BXdt[d_]])
                        op("pool", I("tensor_tensor", out=Xdec[d_][:].rearrange("p (h d) -> p h d", d=64), in0=Xdt[d_][:].rearrange("p (h d) -> p h d", d=64),
                                                                    in1=dec[:, d_ * 32:(d_ + 1) * 32].unsqueeze(2).to_broadcast([128, 32, 64]), op=ALU.mult),
                           reads=[BXdt[d_], Bsv], writes=[BXdec[d_]])
                    op("pool", I("tensor_tensor", out=X3, in0=X3,
                                                         in1=smv("dskip").unsqueeze(2).to_broadcast([128, 32, 64]), op=ALU.mult),
                       reads=[BXtok, Bsm], writes=[BXtok])
                    DX, BDX = Xtok, BXtok
                    for g in range(8):
                        BT_g = XBC[:, 16 + g, :]
                        CT_g = XBC[:, 24 + g, :]
                        op("pe", I("matmul", PM[:, 128:256], lhsT=BT_g, rhs=CT_g, start=True, stop=True),
                           reads=[BXBC], writes=[BPMb])
                        op("dve", I("tensor_tensor", out=CBm[:, 0, :], in0=PM[:, 128:256], in1=U_f, op=ALU.mult), reads=[BPMb, Bcst], writes=[BCBm])
                        op("dve", I("tensor_tensor", out=CBm[:, 1, :], in0=PM[:, 128:256], in1=Lw_f, op=ALU.mult), reads=[BPMb, Bcst], writes=[BCBm])
                        for d_ in range(2):
                            msk = U_f if d_ == 0 else Lw_f
                            tri = TU_f if d_ == 0 else TL_f
                            op("pool", I("tensor_tensor",
                                out=Rm[d_][:], in0=msk.unsqueeze(1).to_broadcast([128, 4, 128]),
                                in1=dA[:, d_ * 32 + 4 * g:d_ * 32 + 4 * g + 4].unsqueeze(2).to_broadcast([128, 4, 128]), op=ALU.mult),
                               reads=[Bcst, Bsv], writes=[BRm[d_]])
                            op("pe", I("matmul", PA[d_][:, :], lhsT=tri, rhs=Rm[d_][:].rearrange("p a t -> p (a t)"), start=True, stop=True),
                               reads=[Bcst, BRm[d_]], writes=[BPA[d_]])
                            op("act", I("activation", out=LT[d_][:].rearrange("p a t -> p (a t)"), in_=PA[d_][:, :], func=AF.Exp),
                               reads=[BPA[d_]], writes=[BLT[d_]])
                            op("dve", I("tensor_tensor", out=MT[d_][:], in0=LT[d_][:], in1=CBm[:, d_, :].unsqueeze(1).to_broadcast([128, 4, 128]), op=ALU.mult),
                               reads=[BLT[d_], BCBm], writes=[BMT[d_]])
                        fns = []
                        for r_ in range(4):
                            h = 4 * g + r_
                            fns.append(I("matmul", PX2[:, r_ * 64:(r_ + 1) * 64], lhsT=MT[0][:, r_, :], rhs=Xdt[0][:, h * 64:(h + 1) * 64], start=True, stop=False))
                            fns.append(I("matmul", PX2[:, r_ * 64:(r_ + 1) * 64], lhsT=MT[1][:, r_, :], rhs=Xdt[1][:, h * 64:(h + 1) * 64], start=False, stop=False))
                            fns.append(I("matmul", PX2[:, r_ * 64:(r_ + 1) * 64], lhsT=ident_b, rhs=DX[:, h * 64:(h + 1) * 64], start=False, stop=True))
                        grp("pe", fns, reads=[BMT[0], BMT[1], BXdt[0], BXdt[1], BDX, Bcbf], writes=[BPX2[0]])
                        op("pe", I("matmul", PX2[:, 256:512], lhsT=CT_g, rhs=prevf[:, g * 256:(g + 1) * 256], start=True, stop=True),
                           reads=[BXBC, Bprevf], writes=[BPX2[1]])
                        op("dve", I("tensor_tensor", out=ytmp[:].rearrange("p (h d) -> p h d", d=64), in0=PX2[:, 256:512].rearrange("p (h d) -> p h d", d=64),
                                                                 in1=eAt[:, 4 * g:4 * g + 4].unsqueeze(2).to_broadcast([128, 4, 64]), op=ALU.mult),
                           reads=[BPX2[1], Bsv], writes=[Bytmp])
                        op("dve", I("tensor_tensor", out=ypst[:, g * 256:(g + 1) * 256], in0=ytmp[:], in1=PX2[:, 0:256], op=ALU.add),
                           reads=[Bytmp, BPX2[0]], writes=[Bypst])
                        for d_ in range(2):
                            op("pe", I("matmul", PX3[:, d_ * 256:(d_ + 1) * 256], lhsT=Btok[:, g * 128:(g + 1) * 128], rhs=Xdec[d_][:, g * 256:(g + 1) * 256], start=True, stop=True),
                               reads=[BBtok, BXdec[d_]], writes=[BPX3[d_]])
                        Sg = Sf[:, g * 256:(g + 1) * 256].rearrange("p (h d) -> p h d", d=64)
                        op("dve", I("tensor_tensor", out=Sg, in0=Sg, in1=cd[:, 4 * g:4 * g + 4].unsqueeze(2).to_broadcast([128, 4, 64]), op=ALU.mult),
                           reads=[BSf, Bsv, Bprevf], writes=[BSf])
                        op("dve", I("tensor_tensor", out=Sf[:, g * 256:(g + 1) * 256], in0=Sf[:, g * 256:(g + 1) * 256], in1=PX3[:, 0:256], op=ALU.add),
                           reads=[BSf, BPX3[0]], writes=[BSf])
                        op("act", I("copy", out=sbst[:, g * 256:(g + 1) * 256], in_=PX3[:, 256:512]), reads=[BPX3[1]], writes=[Bsbst])
                    op("act", I("copy", out=prevf[:], in_=Sf[:]), reads=[BSf], writes=[Bprevf])
                    dma("sp", st_sb[gc], sbst[:], reads=[Bsbst], sem_buf=Bsbst)
                    dma("sp", st_ypart[gc], ypst[:], reads=[Bypst], sem_buf=Bypst)
                    kbs = [kb for kb in (m - 1, m, m + 1) if 0 <= kb < n]
                    for g in range(8):
                        e_, i_ = g % 2, g // 2
                        rhs_q = qT[s2][e_ * 64:(e_ + 1) * 64, i_].rearrange("p j t -> p (j t)")
                        for t_, kb in enumerate(kbs):
                            pb = (g * 3 + t_) % 3
                            QB, BQB = ((PX1, BPX1), (PA[0], BPA[0]), (PA[1], BPA[1]))[pb]
                            op("pe", I("matmul", QB[:, :], lhsT=kT[kb % 4][e_ * 64:(e_ + 1) * 64, i_, :], rhs=rhs_q, start=True, stop=True),
                               reads=[BkT[kb % 4], BqT[s2]], writes=[BQB])
                            op("act", I("activation", out=PTs[pb][:].rearrange("p a t -> p (a t)"), in_=QB[:, :], func=AF.Exp, scale=0.125),
                               reads=[BQB], writes=[BPTs[pb]])
                            if kb != m:
                                mk = Lw_b if kb < m else U_b
                                op("pool", I("tensor_tensor", out=PTs[pb][:], in0=PTs[pb][:], in1=mk.unsqueeze(1).to_broadcast([128, 4, 128]), op=ALU.mult),
                                   reads=[BPTs[pb], Bcbf], writes=[BPTs[pb]])
                        fns = []
                        OB, BOB = ((PX3, BPX3[0]), (PX2, BPX2[0]))[g % 2]
                        for j in range(4):
                            for t_, kb in enumerate(kbs):
                                pb = (g * 3 + t_) % 3
                                fns.append(I("matmul", OB[:, j * 65:(j + 1) * 65], lhsT=PTs[pb][:, j, :], rhs=vr[kb % 4][:, g, :],
                                                                                      start=(t_ == 0), stop=(t_ == len(kbs) - 1)))
                        grp("pe", fns, reads=[BPTs[0], BPTs[1], BPTs[2]] + [Bvr[kb % 4] for kb in kbs], writes=[BOB])
                        o3 = OB[:, 0:260].rearrange("p (j d) -> p j d", d=65)
                        op("dve", I("tensor_tensor", out=den[:, 0:4].unsqueeze(2), in0=o3[:, :, 64:65], in1=expsink[:, 4 * g:4 * g + 4].unsqueeze(2), op=ALU.add),
                           reads=[BOB, Bvecs], writes=[Bden])
                        op("dve", I("reciprocal", out=den[:, 4:8], in_=den[:, 0:4]), reads=[Bden], writes=[Bden])
                        op("dve", I("tensor_tensor", out=ytmp[:].rearrange("p (j d) -> p j d", d=64), in0=o3[:, :, 0:64],
                                                                   in1=den[:, 4:8].unsqueeze(2).to_broadcast([128, 4, 64]), op=ALU.mult),
                           reads=[BOB, Bden], writes=[Bytmp])
                        op("dve", I("tensor_tensor", out=yast[:, g * 256:(g + 1) * 256], in0=ytmp[:], in1=ga[s2][:, g * 256:(g + 1) * 256], op=ALU.mult),
                           reads=[Bytmp, Bga[s2]], writes=[Byast])
                    dma("sp", st_ya[gc], yast[:], reads=[Byast], sem_buf=Byast)

                for i in range(n):
                    inproj(i)
                    if i > 0 and not _stop.startswith("inproj"):
                        mixers(i - 1)
                if not _stop.startswith("inproj"):
                    mixers(n - 1)
                gchunk += n
            fw.barrier()

        with ExitStack() as s2_:
            alloc_wb(NWB2, s2_)
            yp = sb("yp", [128, D], F32, s2_); Byp = Buf("yp")
            zs2 = sb("zs2", [128, D], BF16, s2_); Bzs2 = Buf("zs2")
            gs2 = sb("gs2", [128, D], BF16, s2_); Bgs2 = Buf("gs2")
            ya2 = sb("ya2", [128, D], BF16, s2_); Bya2 = Buf("ya2")
            ct2 = sb("ct2", [128, 8, 128], BF16, s2_); Bct2 = Buf("ct2")
            sb2 = sb("sb2", [128, D], BF16, s2_); Bsb2 = Buf("sb2")
            vc2 = sb("vc2", [128, 64], F32, s2_); Bvc2 = Buf("vc2")
            xt2 = sb("xt2", [128, D], F32, s2_); Bxt2 = Buf("xt2")
            Sb = sb("Sb", [128, D], F32, s2_); BSb = Buf("Sb")
            prevb = sb("prevb", [128, D], BF16, s2_); Bprevb = Buf("prevb")
            gst = sb("gst", [128, 24], F32, s2_); Bgst = Buf("gst")
            mg = sb("mg", [128, D], BF16, s2_); Bmg = Buf("mg")
            junk2, Bjunk2 = mg, Bmg
            mT = sb("mT", [128, KC, 128], BF16, s2_); BmT = Buf("mT")
            x1 = sb("x1", [128, D], F32, s2_); Bx1 = Buf("x1")
            xn2 = sb("xn2", [128, D], BF16, s2_); Bxn2 = Buf("xn2")
            ypl, Bypl = xn2, Bxn2
            nst2 = sb("nst2", [128, 4], F32, s2_); Bnst2 = Buf("nst2")
            h2T = sb("h2T", [128, KC, 128], BF16, s2_); Bh2T = Buf("h2T")
            sg = sb("sg", [128, 512], BF16, s2_); Bsg = Buf("sg")
            acttok = sb("acttok", [128, 512], BF16, s2_); Bacttok = Buf("acttok")
            actT = sb("actT", [128, FC, 128], BF16, s2_); BactT = Buf("actT")
            fo = sb("fo", [128, D], F32, s2_); Bfo = Buf("fo")
            yw, Byw = fo, Bfo
            nwbc = sb("nwbc", [128, D], F32, s2_); Bnw = Buf("nwbc", const=True)
            GGt = [sb("GG_%d" % j, [128, D], F32, s2_) for j in range(2)]
            BGGt = [Buf("GG_%d" % j) for j in range(2)]
            dma("sp", nwbc[:], rowbc[:, 0:D], writes=[Bnw], sem_buf=Bnw)
            PB = [PA[0], PA[1], PX1, PX2]
            BPB = [BPA[0], BPA[1], BPX1, BPX2[0]]

            P2_UNITS = list(range(U_OUT0, U_OUT0 + 4)) + list(range(U_GU0, U_GU0 + 22)) + list(range(U_DN0, U_DN0 + 16))
            wq2 = []
            wplan2 = P2_UNITS * NCH
            wpos2 = {"i": 0}

            def wnext2():
                while len(wq2) < len(WB) and wpos2["i"] < len(wplan2):
                    u = wplan2[wpos2["i"]]
                    wpos2["i"] += 1
                    wq2.append((u, wload(u)))
                return wq2.pop(0)

            def rstd_of(ss_ap, out_ap, Bst, n_el):
                op("dve", I("tensor_scalar", out=out_ap, in0=ss_ap, scalar1=1.0 / n_el, scalar2=EPS, op0=ALU.mult, op1=ALU.add), reads=[Bst], writes=[Bst])
                op("act", I("activation", out=out_ap, in_=out_ap, func=AF.Sqrt), reads=[Bst], writes=[Bst])
                op("dve", I("reciprocal", out=out_ap, in_=out_ap), reads=[Bst], writes=[Bst])

            goff = [sum(pieces[:k]) for k in range(NPC)]
            for p, n in enumerate(pieces if _stop not in ("prepass", "pass1", "inproj") else []):
                op("pool", I("memset", Sb[:], 0.0), writes=[BSb])
                for j in range(2):
                    dma("sp", GGt[j][:], st_gg[2 * p + j], writes=[BGGt[j]], sem_buf=BGGt[j])
                GG = {p: GGt}
                BGG = {p: BGGt}
                for m in range(n - 1, -1, -1):
                    gc = goff[p] + m
                    last = (m == n - 1)
                    dma("sp", ypl[:], st_ypart[gc], writes=[Bypl], sem_buf=Bypl)
                    op("pool", I("tensor_copy", out=yp[:], in_=ypl[:]), reads=[Bypl], writes=[Byp])
                    dma("sp", zs2[:], st_zs[gc], writes=[Bzs2], sem_buf=Bzs2)
                    dma("sp", gs2[:], st_gs[gc], writes=[Bgs2], sem_buf=Bgs2)
                    dma("sp", ya2[:], st_ya[gc], writes=[Bya2], sem_buf=Bya2)
                    dma("sp", ct2[:].rearrange("p a t -> p (a t)"), st_ct[gc], writes=[Bct2], sem_buf=Bct2)
                    dma("sp", sb2[:], st_sb[gc], writes=[Bsb2], sem_buf=Bsb2)
                    dma("sp", vc2[:], st_vec[gc], writes=[Bvc2], sem_buf=Bvc2)
                    dma("sp", xt2[:], xin[gc * 128:(gc + 1) * 128, :], writes=[Bxt2], sem_buf=Bxt2)
                    if not last:
                        op("act", I("copy", out=prevb[:], in_=Sb[:]), reads=[BSb], writes=[Bprevb])
                        for g in range(8):
                            bk = g // 2
                            op("pe", I("matmul", PB[bk][:, (g % 2) * 256:(g % 2 + 1) * 256], lhsT=ct2[:, g, :], rhs=prevb[:, g * 256:(g + 1) * 256], start=True, stop=True),
                               reads=[Bct2, Bprevb], writes=[BPB[bk]])
                        for bk in range(4):
                            op("dve", I("tensor_tensor", out=yw[:, bk * 512:(bk + 1) * 512].rearrange("p (h d) -> p h d", d=64),
                                                                       in0=PB[bk][:, :].rearrange("p (h d) -> p h d", d=64),
                                                                       in1=vc2[:, 8 * bk:8 * bk + 8].unsqueeze(2).to_broadcast([128, 8, 64]), op=ALU.mult),
                               reads=[BPB[bk], Bvc2], writes=[Byw])
                        op("dve", I("tensor_tensor", out=yp[:], in0=yp[:], in1=yw[:], op=ALU.add), reads=[Byp, Byw], writes=[Byp])
                    S3 = Sb[:].rearrange("p (h d) -> p h d", d=64)
                    op("dve", I("tensor_tensor", out=S3, in0=S3, in1=vc2[:, 32:64].unsqueeze(2).to_broadcast([128, 32, 64]), op=ALU.mult),
                       reads=[BSb, Bvc2, Bprevb], writes=[BSb])
                    op("dve", I("tensor_tensor", out=Sb[:], in0=Sb[:], in1=sb2[:], op=ALU.add), reads=[BSb, Bsb2], writes=[BSb])
                    op("dve", I("tensor_tensor", out=yp[:], in0=yp[:], in1=zs2[:], op=ALU.mult), reads=[Byp, Bzs2], writes=[Byp])
                    for g in range(8):
                        op("act", I("activation", out=junk2[:, g * 256:(g + 1) * 256], in_=yp[:, g * 256:(g + 1) * 256], func=AF.Square, accum_out=gst[:, g:g + 1]),
                           reads=[Byp], writes=[Bjunk2, Bgst])
                    rstd_of(gst[:, 0:8], gst[:, 8:16], Bgst, 256.0)
                    op("dve", I("tensor_tensor", out=yp[:].rearrange("p (g d) -> p g d", d=256), in0=yp[:].rearrange("p (g d) -> p g d", d=256),
                                                        in1=gst[:, 8:16].unsqueeze(2).to_broadcast([128, 8, 256]), op=ALU.mult), reads=[Byp, Bgst], writes=[Byp])
                    op("dve", I("tensor_tensor", out=yp[:], in0=yp[:], in1=nwbc[:], op=ALU.mult), reads=[Byp, Bnw], writes=[Byp])
                    op("dve", I("tensor_tensor", out=yp[:], in0=yp[:], in1=gs2[:], op=ALU.mult), reads=[Byp, Bgs2], writes=[Byp])
                    op("dve", I("tensor_tensor", out=mg[:], in0=yp[:], in1=ya2[:], op=ALU.add), reads=[Byp, Bya2], writes=[Bmg])
                    for hh in range(2):
                        for k8 in range(8):
                            op("pe", I("transpose", out=PT[hh][:, k8 * 128:(k8 + 1) * 128], in_=mg[:, (hh * 8 + k8) * 128:(hh * 8 + k8 + 1) * 128], identity=ident_b),
                               reads=[Bmg, Bcbf], writes=[BPT[hh]])
                        op("act", I("copy", out=mT[:, hh * 8:(hh + 1) * 8, :].rearrange("p a t -> p (a t)"), in_=PT[hh][:, :]), reads=[BPT[hh]], writes=[BmT])
                    for cb_ in range(4):
                        u, b = wnext2()
                        wv = wview(b, u)
                        grp("pe", [I("matmul", PB[cb_][:, :], lhsT=mT[:, kk, :], rhs=wv[:, kk, :], start=(kk == 0), stop=(kk == KC - 1)) for kk in range(KC)],
                            reads=[BmT, BWB[b]], writes=[BPB[cb_]])
                    for cb_ in range(4):
                        op("act", I("activation", out=junk2[:, cb_ * 512:(cb_ + 1) * 512], in_=PB[cb_][:, :], func=AF.Square, accum_out=gst[:, 16 + cb_:17 + cb_]),
                           reads=[BPB[cb_]], writes=[Bjunk2, Bgst])
                    op("dve", I("tensor_reduce", out=nst2[:, 0:1], in_=gst[:, 16:20], axis=AX.X, op=ALU.add), reads=[Bgst], writes=[Bnst2])
                    rstd_of(nst2[:, 0:1], nst2[:, 1:2], Bnst2, float(D))
                    for cb_ in range(4):
                        op("dve", I("scalar_tensor_tensor", out=x1[:, cb_ * 512:(cb_ + 1) * 512], in0=PB[cb_][:, :], scalar=nst2[:, 1:2],
                                                                            in1=GG[p][0][:, cb_ * 512:(cb_ + 1) * 512], op0=ALU.mult, op1=ALU.mult),
                           reads=[BPB[cb_], Bnst2, BGG[p][0]], writes=[Bx1])
                    op("dve", I("tensor_tensor", out=x1[:], in0=x1[:], in1=xt2[:], op=ALU.add), reads=[Bx1, Bxt2], writes=[Bx1])
                    op("act", I("activation", out=junk2[:], in_=x1[:], func=AF.Square, accum_out=nst2[:, 2:3]), reads=[Bx1], writes=[Bjunk2, Bnst2])
                    rstd_of(nst2[:, 2:3], nst2[:, 3:4], Bnst2, float(D))
                    op("act", I("activation", out=xn2[:], in_=x1[:], func=AF.Copy, scale=nst2[:, 3:4]), reads=[Bx1, Bnst2], writes=[Bxn2])
                    for hh in range(2):
                        for k8 in range(8):
                            kc = hh * 8 + k8
                            op("pe", I("transpose", out=PT[hh][:, k8 * 128:(k8 + 1) * 128], in_=xn2[:, kc * 128:(kc + 1) * 128], identity=ident_b),
                               reads=[Bxn2, Bcbf], writes=[BPT[hh]])
                        for k8 in range(8):
                            kc = hh * 8 + k8
                            op("dve", I("tensor_scalar", out=h2T[:, kc, :], in0=PT[hh][:, k8 * 128:(k8 + 1) * 128],
                                                                                    scalar1=fm[:, p, 3, kc:kc + 1], scalar2=fm[:, p, 2, kc:kc + 1], op0=ALU.mult, op1=ALU.add),
                               reads=[BPT[hh], Bfm], writes=[Bh2T])
                    for j in range(11):
                        for w_ in range(2):
                            u, b = wnext2()
                            wv = wview(b, u)
                            grp("pe", [I("matmul", PB[w_][:, :], lhsT=h2T[:, kk, :], rhs=wv[:, kk, :], start=(kk == 0), stop=(kk == KC - 1)) for kk in range(KC)],
                                reads=[Bh2T, BWB[b]], writes=[BPB[w_]])
                        op("act", I("activation", out=sg[:], in_=PB[0][:, :], func=AF.Silu), reads=[BPB[0]], writes=[Bsg])
                        op("dve", I("tensor_tensor", out=acttok[:], in0=sg[:], in1=PB[1][:, :], op=ALU.mult), reads=[Bsg, BPB[1]], writes=[Bacttok])
                        tb = j % 2
                        for q_ in range(4):
                            op("pe", I("transpose", out=PT[tb][:, q_ * 128:(q_ + 1) * 128], in_=acttok[:, q_ * 128:(q_ + 1) * 128], identity=ident_b),
                               reads=[Bacttok, Bcbf], writes=[BPT[tb]])
                        op("act", I("copy", out=actT[:, 4 * j:4 * j + 4, :].rearrange("p a t -> p (a t)"), in_=PT[tb][:, 0:512]), reads=[BPT[tb]], writes=[BactT])
                    for kg in range(4):
                        for cb_ in range(4):
                            u, b = wnext2()
                            wv = wview(b, u)
                            grp("pe", [I("matmul", PB[cb_][:, :], lhsT=actT[:, 11 * kg + kk, :], rhs=wv[:, kk, :],
                                                                                      start=(kg == 0 and kk == 0), stop=(kg == 3 and kk == 10)) for kk in range(11)],
                                reads=[BactT, BWB[b]], writes=[BPB[cb_]])
                    for cb_ in range(4):
                        op("act", I("activation", out=junk2[:, cb_ * 512:(cb_ + 1) * 512], in_=PB[cb_][:, :], func=AF.Square, accum_out=gst[:, 20 + cb_:21 + cb_]),
                           reads=[BPB[cb_]], writes=[Bjunk2, Bgst])
                    op("dve", I("tensor_reduce", out=nst2[:, 0:1], in_=gst[:, 20:24], axis=AX.X, op=ALU.add), reads=[Bgst], writes=[Bnst2])
                    rstd_of(nst2[:, 0:1], nst2[:, 1:2], Bnst2, float(D))
                    for cb_ in range(4):
                        op("dve", I("scalar_tensor_tensor", out=fo[:, cb_ * 512:(cb_ + 1) * 512], in0=PB[cb_][:, :], scalar=nst2[:, 1:2],
                                                                            in1=GG[p][1][:, cb_ * 512:(cb_ + 1) * 512], op0=ALU.mult, op1=ALU.mult),
                           reads=[BPB[cb_], Bnst2, BGG[p][1]], writes=[Bfo])
                    op("dve", I("tensor_tensor", out=fo[:], in0=fo[:], in1=x1[:], op=ALU.add), reads=[Bfo, Bx1], writes=[Bfo])
                    dma("sp", yout[gc * 128:(gc + 1) * 128, :], fo[:], reads=[Bfo], sem_buf=Bfo, final=True)
            fw._need("pool", fw.final_waits)
            fw._need("sp", fw.final_waits)
        fw.emit()
    return nc


def host_consts():
    t = np.arange(128)
    ident = np.eye(128, dtype=np.float32)
    U = (t[:, None] <= t[None, :]).astype(np.float32)
    Lw = (t[:, None] >= t[None, :]).astype(np.float32)
    TU = (t[:, None] > t[None, :]).astype(np.float32)
    TL = (t[:, None] < t[None, :]).astype(np.float32)
    ones = np.ones((128, 128), np.float32)
    return np.concatenate([ident, U, Lw, TU, TL, ones], axis=1)


def rope_table(pos0_list):
    half = 8
    inv = (np.float32(500000.0) ** (-np.arange(half, dtype=np.float32) * np.float32(2.0) / np.float32(16))).astype(np.float32)
    cols = []
    for p0 in pos0_list:
        pos = (p0 + np.arange(128)).astype(np.float32)
        ang = (pos[:, None] * inv[None, :]).astype(np.float32)
        cols.append(np.cos(ang).astype(np.float32))
        cols.append(np.sin(ang).astype(np.float32))
    return np.ascontiguousarray(np.concatenate(cols, axis=1))


def fmaj(v, kc):
    return np.ascontiguousarray(np.asarray(v, np.float32).reshape(kc, 128).T)


def bc128(v):
    return np.ascontiguousarray(np.broadcast_to(np.asarray(v, np.float32)[None, :], (128, np.asarray(v).shape[0])))


def make_in_map(xs, cs, P):
    sm = np.zeros((128, SM_W), np.float32)

    def put(name, arr):
        o, w = SM[name]
        sm[:, o:o + arr.shape[1]] = arr

    put("cfm", np.concatenate([fmaj(c, 16) for c in cs], axis=1))
    put("gpre1", fmaj(P["g_pre1"][0], 16))
    put("gpre2", fmaj(P["g_pre2"][0], 16))
    cw = np.asarray(P["conv_w"][0], np.float32)
    put("convw", np.ascontiguousarray(cw.reshape(5, 32, 128).transpose(2, 1, 0)).reshape(128, 160))
    put("convb", fmaj(P["conv_b"][0], 32))
    put("alog", bc128(np.concatenate([P["a_log_f"][0], P["a_log_b"][0]])))
    put("dtb", bc128(np.concatenate([P["dt_bias_f"][0], P["dt_bias_b"][0]])))
    put("dskip", bc128(P["d_skip"][0]))
    put("sinks", bc128(P["sinks"][0]))
    rowbc = np.concatenate([bc128(P["ssd_norm_w"][0]), bc128(P["g_post1"][0]), bc128(P["g_post2"][0]), bc128(P["b_ada"][0])], axis=1)
    pos0 = []
    for x in xs:
        pos0 += [128 * i for i in range(x.shape[0] // 128)]
    return {
        "xin": np.ascontiguousarray(np.concatenate(xs, axis=0), dtype=np.float32),
        "small": sm, "consts": host_consts(), "rope": rope_table(pos0), "rowbc": np.ascontiguousarray(rowbc),
        "w_ada": np.ascontiguousarray(P["w_ada"][0], dtype=np.float32), "w_in": np.ascontiguousarray(P["w_in"][0], dtype=np.float32),
        "w_out": np.ascontiguousarray(P["w_out"][0], dtype=np.float32), "w_gu": np.ascontiguousarray(P["w_gu"][0], dtype=np.float32),
        "w_down": np.ascontiguousarray(P["w_down"][0], dtype=np.float32),
    }


def kernel(x_prompt, x_sample, c_prompt, c_sample, **P):
    x_prompt = np.asarray(x_prompt, np.float32)
    x_sample = np.asarray(x_sample, np.float32)
    c_prompt = np.asarray(c_prompt, np.float32)
    c_sample = np.asarray(c_sample, np.float32)
    P = {k: np.asarray(v, np.float32) for k, v in P.items()}
    B, L, _ = x_prompt.shape
    Bs, Ls, _ = x_sample.shape
    ncores = 8
    zP, zS, zc = np.zeros((L, D), np.float32), np.zeros((Ls, D), np.float32), np.zeros((D,), np.float32)
    maps = []
    for c in range(ncores):
        xa, ca = (x_prompt[c], c_prompt[c]) if c < B else (zP, zc)
        xb, cb = (x_sample[c], c_sample[c]) if c < Bs else (zS, zc)
        maps.append(make_in_map([xa, xb], [ca, cb], P))
    nc = build([L // 128, Ls // 128])
    res = run_bass_kernel_spmd(nc, maps, core_ids=list(range(ncores)))
    y_p = np.stack([res.results[c]["yout"][:L] for c in range(B)], axis=0)
    y_s = np.stack([res.results[c]["yout"][L:L + Ls] for c in range(Bs)], axis=0)
    return (np.ascontiguousarray(y_p, dtype=np.float32), np.ascontiguousarray(y_s, dtype=np.float32))
```

```python
import os
import numpy as np
from contextlib import ExitStack
import concourse.bass as bass
import concourse.mybir as mybir
from concourse.bass_utils import run_bass_kernel_spmd

F32 = mybir.dt.float32
BF16 = mybir.dt.bfloat16
AF = mybir.ActivationFunctionType
ALU = mybir.AluOpType
AX = mybir.AxisListType

D = 2048
KC = 16
DFF = 5632
FC = 44
NH = 32
EPS = 1e-6
IN_W = 13376
NWB1 = int(os.environ.get("KNWB1", "4"))
NWB2 = int(os.environ.get("KNWB2", "5"))


class Buf:
    def __init__(self, name, const=False, excl=False):
        self.name = name
        self.excl = excl
        self.w = None
        self.r = []
        self.dsem = None
        self.dcount = 0
        self.const = const


class Eng:
    def __init__(self, name):
        self.name = name
        self.ops = []
        self.count = 0
        self.waited = {}


class FW:
    def __init__(self, nc, stack):
        self.nc = nc
        self.stack = stack
        self.E = {n: Eng(n) for n in ("pe", "dve", "act", "pool", "sp")}
        self.sems = {}
        for n in self.E:
            self.sems[n] = stack.enter_context(nc.semaphore("s_" + n))
        self.nd = 0
        self.final_waits = []
        self.dma_bufs = []

    def dma_sem(self, buf):
        if buf.dsem is None:
            key = "d%d" % self.nd
            self.nd += 1
            self.sems[key] = self.stack.enter_context(self.nc.semaphore(key))
            buf.dsem = key
            self.dma_bufs.append(buf)
        return buf.dsem

    def _need(self, eng, deps):
        e = self.E[eng]
        best = {}
        for k, v in deps:
            if v > best.get(k, 0):
                best[k] = v
        for k, v in best.items():
            if e.waited.get(k, 0) >= v:
                continue
            e.waited[k] = v
            h = self.sems[k]
            e.ops.append(lambda en, h=h, v=v: en.wait_ge(h, v))

    @staticmethod
    def _deps(reads, writes):
        deps = []
        for b in reads:
            if b.w is not None:
                deps.append(b.w)
            if b.excl:
                deps.extend(b.r)
        for b in writes:
            if b.w is not None:
                deps.append(b.w)
            deps.extend(b.r)
        return deps

    def _mark(self, ev, reads, writes):
        for b in reads:
            if not b.const:
                b.r.append(ev)
        for b in writes:
            b.w = ev
            b.r = []

    def op(self, eng, fn, reads=(), writes=()):
        if eng == "pool" and os.environ.get("KPOOL"):
            eng = os.environ["KPOOL"]
        e = self.E[eng]
        self._need(eng, self._deps(reads, writes))
        e.count += 1
        ev = (eng, e.count)
        h = self.sems[eng]
        e.ops.append(lambda en, fn=fn, h=h: fn(en).then_inc(h, 1))
        self._mark(ev, reads, writes)
        return ev

    def group(self, eng, fns, reads=(), writes=()):
        e = self.E[eng]
        self._need(eng, self._deps(reads, writes))
        e.count += 1
        ev = (eng, e.count)
        h = self.sems[eng]
        n = len(fns)
        for i, fn in enumerate(fns):
            if i == n - 1:
                e.ops.append(lambda en, fn=fn, h=h: fn(en).then_inc(h, 1))
            else:
                e.ops.append(lambda en, fn=fn: fn(en))
        self._mark(ev, reads, writes)
        return ev

    def dma(self, q, out_ap, in_ap, reads=(), writes=(), sem_buf=None, final=False, **kw):
        if q == "pool" and os.environ.get("KDMAQ"):
            q = os.environ["KDMAQ"]
        e = self.E[q]
        self._need(q, self._deps(reads, writes))
        key = self.dma_sem(sem_buf)
        sem_buf.dcount += 16
        ev = (key, sem_buf.dcount)
        h = self.sems[key]
        e.ops.append(lambda en, o=out_ap, i=in_ap, h=h, kw=kw: en.dma_start(out=o, in_=i, **kw).then_inc(h, 16))
        self._mark(ev, reads, writes)
        if final:
            self.final_waits.append(ev)
        return ev

    def barrier(self):
        evs = [(n, e.count) for n, e in self.E.items() if e.count > 0]
        evs += [(b.dsem, b.dcount) for b in self.dma_bufs if b.dcount > 0]
        for n in self.E:
            self._need(n, evs)

    def emit(self):
        nc = self.nc
        E = self.E
        with nc.Block() as block:
            @block.tensor
            def _(en):
                for f in E["pe"].ops:
                    f(en)

            @block.vector
            def _(en):
                for f in E["dve"].ops:
                    f(en)

            @block.scalar
            def _(en):
                for f in E["act"].ops:
                    f(en)

            @block.gpsimd
            def _(en):
                for f in E["pool"].ops:
                    f(en)

            @block.sync
            def _(en):
                for f in E["sp"].ops:
                    f(en)


def I(name, *args, **kwargs):
    return lambda en: getattr(en, name)(*args, **kwargs)


def weight_units():
    U = []
    for i in range(4):
        U.append(("w_in", 0, 16, 512 * i, 512))
    U.append(("w_in", 0, 16, 2048, 512))
    U.append(("w_in", 0, 16, 2560, 512))
    for i in range(4):
        U.append(("w_in", 0, 16, 3072 + 512 * i, 512))
    for i in range(8):
        U.append(("w_in", 0, 16, 5120 + 512 * i, 512))
    U.append(("w_in", 0, 16, 9216, 64))
    for i in range(8):
        U.append(("w_in", 0, 16, 9280 + 512 * i, 512))
    n_in = len(U)
    for i in range(4):
        U.append(("w_out", 0, 16, 512 * i, 512))
    for j in range(11):
        U.append(("w_gu", 0, 16, 512 * j, 512))
        U.append(("w_gu", 0, 16, DFF + 512 * j, 512))
    for kg in range(4):
        for cb in range(4):
            U.append(("w_down", 11 * kg, 11, 512 * cb, 512))
    return U, n_in


UNITS, N_IN_UNITS = weight_units()
U_OUT0 = N_IN_UNITS
U_GU0 = U_OUT0 + 4
U_DN0 = U_GU0 + 22

SM = {}
_o = 0
for _n, _w in [("cfm", 128), ("gpre1", 16), ("gpre2", 16), ("convw", 160), ("convb", 32), ("alog", 64),
               ("dtb", 64), ("dskip", 32), ("sinks", 32)]:
    SM[_n] = (_o, _w)
    _o += _w
SM_W = _o
CONST_W = 6 * 128


def build(pieces):
    pieces = list(pieces)
    NPC = len(pieces)
    assert NPC % 2 == 0 and NPC <= 8
    NCH = sum(pieces)
    NT = NCH * 128
    nc = bass.Bass("TRN2", target_bir_lowering=False)

    def din(name, shape):
        return nc.dram_tensor(name, shape, F32, kind="ExternalInput").ap()

    xin = din("xin", [NT, D])
    small = din("small", [128, SM_W])
    consts = din("consts", [128, CONST_W])
    rope = din("rope", [128, NCH * 16])
    rowbc = din("rowbc", [128, 3 * D + 6 * D])
    W = {"w_ada": din("w_ada", [D, 6 * D]), "w_in": din("w_in", [D, IN_W]), "w_out": din("w_out", [D, D]),
         "w_gu": din("w_gu", [D, 2 * DFF]), "w_down": din("w_down", [DFF, D])}
    yout = nc.dram_tensor("yout", [NT, D], F32, kind="ExternalOutput").ap()

    def dscr(name, shape, dt):
        return nc.dram_tensor(name, shape, dt, kind="Internal").ap()

    wscr = dscr("wscr", [len(UNITS), 128, 8192], BF16)
    class PieceScr:
        def __init__(self, name, width, dt):
            self.t = [dscr("%s_%d" % (name, k), [n, 128, width], dt) for k, n in enumerate(pieces)]
            self.map = []
            for k, n in enumerate(pieces):
                self.map += [(k, i) for i in range(n)]

        def __getitem__(self, gc):
            k, i = self.map[gc]
            return self.t[k][i]

    st_ypart = PieceScr("st_ypart", D, BF16)
    st_zs = PieceScr("st_zs", D, BF16)
    st_gs = PieceScr("st_gs", D, BF16)
    st_ya = PieceScr("st_ya", D, BF16)
    st_ct = PieceScr("st_ct", 1024, BF16)
    st_sb = PieceScr("st_sb", D, BF16)
    st_vec = PieceScr("st_vec", 64, F32)
    st_gg = dscr("st_gg", [2 * NPC, 128, D], F32)

    with ExitStack() as st:
        fw = FW(nc, st)
        op, grp, dma = fw.op, fw.group, fw.dma

        def sb(name, shape, dt, stack=st):
            return stack.enter_context(nc.sbuf_tensor(name, shape, dt))

        def ps(name, shape, dt, stack=st):
            return stack.enter_context(nc.psum_tensor(name, shape, dt))

        PA = [ps("PA%d" % i, [128, 512], F32) for i in range(2)]
        BPA = [Buf("PA%d" % i, excl=True) for i in range(2)]
        PT = [ps("PT%d" % i, [128, 1024], BF16) for i in range(2)]
        BPT = [Buf("PT%d" % i, excl=True) for i in range(2)]
        PX1 = ps("PX1", [128, 512], F32); BPX1 = Buf("PX1", excl=True)
        PX2 = ps("PX2", [128, 512], F32); _b2 = Buf("PX2", excl=True); BPX2 = [_b2, _b2]
        PX3 = ps("PX3", [128, 512], F32); _b3 = Buf("PX3", excl=True); BPX3 = [_b3, _b3]
        PM = ps("PM", [128, 512], F32); BPMa = Buf("PM", excl=True); BPMb = BPMa

        smt = sb("smt", [128, SM_W], F32); Bsm = Buf("smt", const=True)
        cst = sb("cst", [128, CONST_W], F32); Bcst = Buf("cst", const=True)
        cbf = sb("cbf", [128, 3 * 128], BF16); Bcbf = Buf("cbf", const=True)
        fm = sb("fm", [128, NPC, 4, 16], F32); Bfm = Buf("fm")
        vecs = sb("vecs", [128, 4 * 32 + 64], F32); Bvecs = Buf("vecs")
        WB, BWB = [], []
        wstate = {"n": 0}

        def alloc_wb(nb, stack):
            WB[:] = [sb("WB%d_%d" % (nb, i), [128, 8192], BF16, stack) for i in range(nb)]
            BWB[:] = [Buf("WB%d" % i) for i in range(nb)]
            wstate["n"] = 0

        ident_f = cst[:, 0:128]
        U_f = cst[:, 128:256]
        Lw_f = cst[:, 256:384]
        TU_f = cst[:, 384:512]
        TL_f = cst[:, 512:640]
        ones_f = cst[:, 640:768]
        ident_b = cbf[:, 0:128]
        U_b = cbf[:, 128:256]
        Lw_b = cbf[:, 256:384]

        def smv(name):
            o, w = SM[name]
            return smt[:, o:o + w]

        dma("sp", smt[:], small, writes=[Bsm], sem_buf=Bsm)
        dma("sp", cst[:], consts, writes=[Bcst], sem_buf=Bcst)
        op("dve", I("tensor_copy", out=cbf[:], in_=cst[:, 0:384]), reads=[Bcst], writes=[Bcbf])
        op("act", I("activation", out=vecs[:, 0:64], in_=smv("alog"), func=AF.Exp), reads=[Bsm], writes=[Bvecs])
        op("dve", I("tensor_scalar", out=vecs[:, 0:64], in0=vecs[:, 0:64], scalar1=-1.0, scalar2=None, op0=ALU.mult),
           reads=[Bvecs], writes=[Bvecs])
        op("act", I("activation", out=vecs[:, 96:128], in_=smv("sinks"), func=AF.Exp), reads=[Bsm], writes=[Bvecs])
        A_bc = vecs[:, 0:64]
        expsink = vecs[:, 96:128]

        with ExitStack() as s0:
            stg = [sb("stg%d" % i, [128, 8, 512], F32, s0) for i in range(2)]
            Bstg = [Buf("stg%d" % i) for i in range(2)]
            cvt = [sb("cvt%d" % i, [128, 8, 512], BF16, s0) for i in range(2)]
            Bcvt = [Buf("cvt%d" % i) for i in range(2)]
            screp = sb("screp", [128, 2, KC, 128], F32, s0); Bscrep = Buf("screp")
            scf = sb("scf", [128, 128], F32, s0); Bscf = Buf("scf")
            modb = sb("modb", [128, 512], F32, s0); Bmodb = Buf("modb")
            badab = [sb("badab%d" % i, [128, 512], F32, s0) for i in range(2)]
            Bbada = [Buf("badab%d" % i) for i in range(2)]
            gpost = sb("gpost", [128, 2 * D], F32, s0); Bgpost = Buf("gpost")
            tmpx = sb("tmpx", [128, 4, 128], F32, s0); Btmpx = Buf("tmpx")
            ggst = sb("ggst", [128, 512], F32, s0); Bggst = Buf("ggst")
            dma("sp", gpost[:], rowbc[:, D:3 * D], writes=[Bgpost], sem_buf=Bgpost)
            op("act", I("activation", out=scf[:], in_=smv("cfm"), func=AF.Silu), reads=[Bsm], writes=[Bscf])
            si = 0
            for r0 in range(0, NPC, 2):
                for r in range(2):
                    op("dve", I("tensor_copy",
                        out=screp[:, r], in_=scf[:, 16 * (r0 + r):16 * (r0 + r) + 16].unsqueeze(2).to_broadcast([128, KC, 128])),
                       reads=[Bscf], writes=[Bscrep])
                for blk in range(24):
                    seg, jb = blk // 4, blk % 4
                    bb = blk % 2
                    dma("sp", badab[bb][:], rowbc[:, 3 * D + blk * 512:3 * D + (blk + 1) * 512], writes=[Bbada[bb]], sem_buf=Bbada[bb])
                    for half in range(2):
                        b = si % 2
                        si += 1
                        src = W["w_ada"][half * 1024:(half + 1) * 1024, blk * 512:(blk + 1) * 512].rearrange("(k p) c -> p k c", p=128)
                        dma("sp", stg[b][:], src, writes=[Bstg[b]], sem_buf=Bstg[b])
                        for r in range(2):
                            grp("pe", [I("matmul",
                                PA[r][:, :], lhsT=screp[:, r, half * 8 + kk, :], rhs=stg[b][:, kk, :],
                                start=(half == 0 and kk == 0), stop=(half == 1 and kk == 7)) for kk in range(8)],
                                reads=[Bscrep, Bstg[b]], writes=[BPA[r]])
                    for r in range(2):
                        op("dve", I("tensor_tensor", out=modb[:], in0=PA[r][:, :], in1=badab[bb][:], op=ALU.add),
                           reads=[BPA[r], Bbada[bb]], writes=[Bmodb])
                        if seg in (2, 5):
                            j = 0 if seg == 2 else 1
                            op("dve", I("tensor_tensor",
                                out=ggst[:], in0=modb[:], in1=gpost[:, j * D + jb * 512:j * D + (jb + 1) * 512], op=ALU.mult),
                               reads=[Bmodb, Bgpost], writes=[Bggst])
                            dma("sp", st_gg[2 * (r0 + r) + j, :, jb * 512:(jb + 1) * 512], ggst[:], reads=[Bggst], sem_buf=Bggst)
                        else:
                            slot = {0: 0, 1: 1, 3: 2, 4: 3}[seg]
                            op("dve", I("tensor_tensor",
                                out=tmpx[:], in0=modb[:].rearrange("p (a b) -> p a b", a=4),
                                in1=ident_f.unsqueeze(1).to_broadcast([128, 4, 128]), op=ALU.mult),
                               reads=[Bmodb, Bcst], writes=[Btmpx])
                            op("dve", I("tensor_reduce",
                                out=fm[:, r0 + r, slot, 4 * jb:4 * jb + 4], in_=tmpx[:], axis=AX.X, op=ALU.add),
                               reads=[Btmpx], writes=[Bfm])
            for r in range(NPC):
                for slot, gname in ((1, "gpre1"), (3, "gpre2")):
                    op("dve", I("tensor_scalar", out=fm[:, r, slot, :], in0=fm[:, r, slot, :], scalar1=1.0, scalar2=None, op0=ALU.add),
                       reads=[Bfm], writes=[Bfm])
                    op("dve", I("tensor_tensor", out=fm[:, r, slot, :], in0=fm[:, r, slot, :], in1=smv(gname), op=ALU.mult),
                       reads=[Bfm, Bsm], writes=[Bfm])
            ce = 0
            for u, (mat, k0, nk, c0, ncol) in enumerate(UNITS):
                for h0 in range(0, nk, 8):
                    hk = min(8, nk - h0)
                    b = si % 2
                    si += 1
                    src = W[mat][(k0 + h0) * 128:(k0 + h0 + hk) * 128, c0:c0 + ncol].rearrange("(k p) c -> p k c", p=128)
                    dma("sp", stg[b][:, 0:hk, 0:ncol], src, writes=[Bstg[b]], sem_buf=Bstg[b])
                    eng = ("act", "dve", "pool")[ce % 3]
                    ce += 1
                    if eng == "act":
                        op("act", I("copy", out=cvt[b][:, 0:hk, 0:ncol], in_=stg[b][:, 0:hk, 0:ncol]),
                           reads=[Bstg[b]], writes=[Bcvt[b]])
                    else:
                        op(eng, I("tensor_copy", out=cvt[b][:, 0:hk, 0:ncol], in_=stg[b][:, 0:hk, 0:ncol]),
                           reads=[Bstg[b]], writes=[Bcvt[b]])
                    dst = wscr[u, :, h0 * ncol:(h0 + hk) * ncol].rearrange("p (k c) -> p k c", c=ncol)
                    dma("sp", dst, cvt[b][:, 0:hk, 0:ncol], reads=[Bcvt[b]], sem_buf=Bcvt[b])
            fw.barrier()

        def wload(u):
            b = wstate["n"] % len(WB)
            wstate["n"] += 1
            mat, k0, nk, c0, ncol = UNITS[u]
            dma("sp", WB[b][:, 0:nk * ncol], wscr[u, :, 0:nk * ncol], writes=[BWB[b]], sem_buf=BWB[b])
            return b

        def wview(b, u):
            mat, k0, nk, c0, ncol = UNITS[u]
            return WB[b][:, 0:nk * ncol].rearrange("p (k c) -> p k c", c=ncol)

        _stop = os.environ.get("KSTOP", "")
        with ExitStack() as s1:
            alloc_wb(NWB1, s1)
            xt = sb("xt", [128, D], F32, s1); Bxt = Buf("xt")
            xn = sb("xn", [128, D], BF16, s1); Bxn = Buf("xn")
            nst = sb("nst", [128, 4], F32, s1); Bnst = Buf("nst")
            hT = sb("hT", [128, KC, 128], BF16, s1); BhT = Buf("hT")
            qtok = sb("qtok", [128, 512], BF16, s1); Bqtok = Buf("qtok")
            rt = sb("rt", [128, 4, 64], F32, s1); Brt = Buf("rt")
            qT = [sb("qT%d" % i, [128, 4, 4, 128], BF16, s1) for i in range(2)]
            BqT = [Buf("qT%d" % i) for i in range(2)]
            kT = [sb("kT%d" % i, [128, 4, 128], BF16, s1) for i in range(4)]
            BkT = [Buf("kT%d" % i) for i in range(4)]
            vr = [sb("vr%d" % i, [128, 8, 65], BF16, s1) for i in range(4)]
            Bvr = [Buf("vr%d" % i) for i in range(4)]
            zs = sb("zs", [128, D], BF16, s1); Bzs = Buf("zs")
            gs, Bgs = zs, Bzs
            ga = [sb("ga%d" % i, [128, D], BF16, s1) for i in range(2)]
            Bga = [Buf("ga%d" % i) for i in range(2)]
            dtt = [sb("dtt%d" % i, [128, 64], F32, s1) for i in range(2)]
            Bdtt = [Buf("dtt%d" % i) for i in range(2)]
            xtok = sb("xtok", [128, 512], BF16, s1); Bxtok = Buf("xtok")
            RAW = [sb("RAW%d" % i, [128, 32, 132], BF16, s1) for i in range(2)]
            BRAW = [Buf("RAW%d" % i) for i in range(2)]
            cacc = [sb("cacc%d" % i, [128, 128], F32, s1) for i in range(8)]
            Bcacc = [Buf("cacc%d" % i) for i in range(8)]
            XBC = sb("XBC", [128, 32, 128], BF16, s1); BXBC = Buf("XBC")
            Xtok = sb("Xtok", [128, D], BF16, s1); BXtok = Buf("Xtok")
            Btok = sb("Btok", [128, 1024], BF16, s1); BBtok = Buf("Btok")
            Xdt = [sb("Xdt%d" % i, [128, D], BF16, s1) for i in range(2)]
            BXdt = [Buf("Xdt%d" % i) for i in range(2)]
            Xdec = [sb("Xdec%d" % i, [128, D], BF16, s1) for i in range(2)]
            BXdec = [Buf("Xdec%d" % i) for i in range(2)]
            sv = sb("sv", [128, 8, 64], F32, s1); Bsv = Buf("sv")
            vst = sb("vst", [128, 64], F32, s1); Bvst = Buf("vst")
            Rm = [sb("Rm%d" % i, [128, 4, 128], F32, s1) for i in range(2)]
            BRm = [Buf("Rm%d" % i) for i in range(2)]
            LT = [sb("LT%d" % i, [128, 4, 128], BF16, s1) for i in range(2)]
            BLT = [Buf("LT%d" % i) for i in range(2)]
            CBm = sb("CBm", [128, 2, 128], BF16, s1); BCBm = Buf("CBm")
            MT = [sb("MT%d" % i, [128, 4, 128], BF16, s1) for i in range(2)]
            BMT = [Buf("MT%d" % i) for i in range(2)]
            Sf = sb("Sf", [128, D], F32, s1); BSf = Buf("Sf")
            prevf = sb("prevf", [128, D], BF16, s1); Bprevf = Buf("prevf")
            sbst = sb("sbst", [128, D], BF16, s1); Bsbst = Buf("sbst")
            ypst = sb("ypst", [128, D], BF16, s1); Bypst = Buf("ypst")
            ytmp = sb("ytmp", [128, 256], F32, s1); Bytmp = Buf("ytmp")
            PTs = [sb("PTs%d" % i, [128, 4, 128], BF16, s1) for i in range(3)]
            BPTs = [Buf("PTs%d" % i) for i in range(3)]
            yast, Byast = xn, Bxn
            den = sb("den", [128, 8], F32, s1); Bden = Buf("den")
            rp = [sb("rp%d" % i, [128, 16], F32, s1) for i in range(2)]
            Brp = [Buf("rp%d" % i) for i in range(2)]

            wq = []
            wplan = []
            for p, n in enumerate(pieces):
                for i in range(n):
                    wplan.extend(range(N_IN_UNITS))
            wpos = {"i": 0}

            def wnext():
                while len(wq) < len(WB) and wpos["i"] < len(wplan):
                    u = wplan[wpos["i"]]
                    wpos["i"] += 1
                    wq.append((u, wload(u)))
                return wq.pop(0)

            def rms_to_hT(xsrc_ap, Bx, r, slots):
                sh_s, gsc_s = slots
                op("act", I("activation", out=xn[:], in_=xsrc_ap, func=AF.Square, accum_out=nst[:, 0:1]),
                   reads=[Bx], writes=[Bxn, Bnst])
                op("dve", I("tensor_scalar", out=nst[:, 1:2], in0=nst[:, 0:1], scalar1=1.0 / D, scalar2=EPS, op0=ALU.mult, op1=ALU.add),
                   reads=[Bnst], writes=[Bnst])
                op("act", I("activation", out=nst[:, 2:3], in_=nst[:, 1:2], func=AF.Sqrt), reads=[Bnst], writes=[Bnst])
                op("dve", I("reciprocal", out=nst[:, 3:4], in_=nst[:, 2:3]), reads=[Bnst], writes=[Bnst])
                op("act", I("activation", out=xn[:], in_=xsrc_ap, func=AF.Copy, scale=nst[:, 3:4]),
                   reads=[Bx, Bnst], writes=[Bxn])
                for hh in range(2):
                    for k8 in range(8):
                        kc = hh * 8 + k8
                        op("pe", I("transpose", out=PT[hh][:, k8 * 128:(k8 + 1) * 128], in_=xn[:, kc * 128:(kc + 1) * 128], identity=ident_b),
                           reads=[Bxn, Bcbf], writes=[BPT[hh]])
                    for k8 in range(8):
                        kc = hh * 8 + k8
                        op("dve", I("tensor_scalar",
                            out=hT[:, kc, :], in0=PT[hh][:, k8 * 128:(k8 + 1) * 128],
                            scalar1=fm[:, r, gsc_s, kc:kc + 1], scalar2=fm[:, r, sh_s, kc:kc + 1], op0=ALU.mult, op1=ALU.add),
                           reads=[BPT[hh], Bfm], writes=[BhT])

            def gemm(b, u, pa, lhs, Blhs):
                mat, k0, nk, c0, ncol = UNITS[u]
                wv = wview(b, u)
                grp("pe", [I("matmul", PA[pa][:, 0:ncol], lhsT=lhs(kk), rhs=wv[:, kk, :], start=(kk == 0), stop=(kk == nk - 1))
                           for kk in range(nk)], reads=[Blhs, BWB[b]], writes=[BPA[pa]])

            gchunk = 0
            for p, n in enumerate(pieces if _stop != "prepass" else []):
                g0 = gchunk
                op("pool", I("memset", Sf[:], 0.0), writes=[BSf])
                op("pool", I("memset", prevf[:], 0.0), writes=[Bprevf])
                for b_ in range(2):
                    op("pool", I("memset", RAW[b_][:], 0.0), writes=[BRAW[b_]])

                def inproj(i):
                    gc = g0 + i
                    s2, s4 = i % 2, i % 4
                    dma("sp", xt[:], xin[gc * 128:(gc + 1) * 128, :], writes=[Bxt], sem_buf=Bxt)
                    rms_to_hT(xt[:], Bxt, p, (0, 1))
                    dma("sp", rp[s2][:], rope[:, gc * 16:gc * 16 + 16], writes=[Brp[s2]], sem_buf=Brp[s2])
                    Brope = Brp[s2]
                    cosv = rp[s2][:, 0:8]
                    sinv = rp[s2][:, 8:16]
                    pa = 0
                    for u in range(N_IN_UNITS):
                        if _stop == "inproj1":
                            break
                        uu, b = wnext()
                        assert uu == u
                        gemm(b, u, pa, lambda kk: hT[:, kk, :], BhT)
                        _cat = "q" if u < 4 else "k" if u == 4 else "v" if u == 5 else "z" if u <= 9 else "xbc" if u <= 17 else "dt" if u == 18 else "ga" if u <= 22 else "gs"
                        if _stop == "inproj2" or _cat in os.environ.get("KSKIP", "").split(","):
                            pa = 1 - pa
                            continue
                        P_ = PA[pa]
                        BP = BPA[pa]
                        if u <= 4:
                            pat = "p (j g d) -> p g j d" if u < 4 else "p (g j d) -> p g j d"
                            pv = P_[:, :].rearrange("p (g j d) -> p g j d", g=2, j=4)
                            qv = qtok[:].rearrange(pat, g=2, j=4)
                            op("act", I("copy", out=qv, in_=pv), reads=[BP], writes=[Bqtok])
                            cb_ = cosv.unsqueeze(1).unsqueeze(1).to_broadcast([128, 2, 4, 8])
                            sb_ = sinv.unsqueeze(1).unsqueeze(1).to_broadcast([128, 2, 4, 8])
                            x1, x2 = pv[:, :, :, 0:8], pv[:, :, :, 8:16]
                            rv = rt[:].rearrange("p a (g j d) -> p a g j d", g=2, j=4)
                            for a_, (xx, tt) in enumerate(((x1, cb_), (x2, sb_), (x2, cb_), (x1, sb_))):
                                op("dve", I("tensor_tensor", out=rv[:, a_], in0=xx, in1=tt, op=ALU.mult),
                                   reads=[BP, Brope], writes=[Brt])
                            op("dve", I("tensor_tensor", out=qv[:, :, :, 0:8], in0=rv[:, 0], in1=rv[:, 1], op=ALU.subtract), reads=[Brt], writes=[Bqtok])
                            op("dve", I("tensor_tensor", out=qv[:, :, :, 8:16], in0=rv[:, 2], in1=rv[:, 3], op=ALU.add), reads=[Brt], writes=[Bqtok])
                            tb = u % 2
                            if u < 4:
                                for j in range(4):
                                    op("pe", I("transpose", out=PT[tb][:, j * 128:(j + 1) * 128], in_=qtok[:, j * 128:(j + 1) * 128], identity=ident_b),
                                       reads=[Bqtok, Bcbf], writes=[BPT[tb]])
                                op("act", I("copy", out=qT[s2][:, u].rearrange("p j t -> p (j t)"), in_=PT[tb][:, 0:512]),
                                   reads=[BPT[tb]], writes=[BqT[s2]])
                            else:
                                for j in range(4):
                                    op("pe", I("transpose", out=PT[tb][:, j * 128:(j + 1) * 128], in_=qtok[:, j * 128:(j + 1) * 128], identity=ident_b),
                                       reads=[Bqtok, Bcbf], writes=[BPT[tb]])
                                op("act", I("copy", out=kT[s4][:].rearrange("p j t -> p (j t)"), in_=PT[tb][:, 0:512]),
                                   reads=[BPT[tb]], writes=[BkT[s4]])
                        elif u == 5:
                            op("act", I("copy", out=vr[s4][:, :, 0:64], in_=P_[:, :].rearrange("p (h d) -> p h d", d=64)),
                               reads=[BP], writes=[Bvr[s4]])
                            op("pool", I("memset", vr[s4][:, :, 64:65], 1.0), writes=[Bvr[s4]])
                        elif u <= 9:
                            j = u - 6
                            op("act", I("activation", out=zs[:, j * 512:(j + 1) * 512], in_=P_[:, :], func=AF.Silu),
                               reads=[BP], writes=[Bzs])
                            if j == 3:
                                dma("sp", st_zs[gc], zs[:], reads=[Bzs], sem_buf=Bzs)
                        elif u <= 17:
                            j = u - 10
                            tb = u % 2
                            op("act", I("copy", out=xtok[:], in_=P_[:, :]), reads=[BP], writes=[Bxtok])
                            for q_ in range(4):
                                op("pe", I("transpose", out=PT[tb][:, q_ * 128:(q_ + 1) * 128], in_=xtok[:, q_ * 128:(q_ + 1) * 128], identity=ident_b),
                                   reads=[Bxtok, Bcbf], writes=[BPT[tb]])
                            op("dve", I("tensor_copy", out=RAW[s2][:, 4 * j:4 * j + 4, 2:130], in_=PT[tb][:, 0:512].rearrange("p (a t) -> p a t", a=4)),
                               reads=[BPT[tb]], writes=[BRAW[s2]])
                            if j == 7:
                                op("pool", I("tensor_copy", out=RAW[1 - s2][:, :, 130:132], in_=RAW[s2][:, :, 2:4]),
                                   reads=[BRAW[s2]], writes=[BRAW[1 - s2]])
                                if i > 0:
                                    op("pool", I("tensor_copy", out=RAW[s2][:, :, 0:2], in_=RAW[1 - s2][:, :, 128:130]),
                                       reads=[BRAW[1 - s2]], writes=[BRAW[s2]])
                        elif u == 18:
                            op("dve", I("tensor_tensor", out=dtt[s2][:], in0=P_[:, 0:64], in1=smv("dtb"), op=ALU.add),
                               reads=[BP, Bsm], writes=[Bdtt[s2]])
                            op("act", I("activation", out=dtt[s2][:], in_=dtt[s2][:], func=AF.Exp), reads=[Bdtt[s2]], writes=[Bdtt[s2]])
                            op("act", I("activation", out=dtt[s2][:], in_=dtt[s2][:], func=AF.Ln, bias=ones_f[:, 0:1]), reads=[Bdtt[s2], Bcst], writes=[Bdtt[s2]])
                        elif u <= 22:
                            j = u - 19
                            op("act", I("activation", out=ga[s2][:, j * 512:(j + 1) * 512], in_=P_[:, :], func=AF.Sigmoid),
                               reads=[BP], writes=[Bga[s2]])
                        else:
                            j = u - 23
                            op("act", I("activation", out=gs[:, j * 512:(j + 1) * 512], in_=P_[:, :], func=AF.Sigmoid),
                               reads=[BP], writes=[Bgs])
                            if j == 3:
                                dma("sp", st_gs[gc], gs[:], reads=[Bgs], sem_buf=Bgs)
                        pa = 1 - pa

                def mixers(m):
                    gc = g0 + m
                    s2, s4 = m % 2, m % 4
                    first, last = (m == 0), (m == n - 1)
                    if last:
                        op("pool", I("memset", RAW[s2][:, :, 130:132], 0.0), writes=[BRAW[s2]])
                    def cwv(blk):
                        return smt[:, SM["convw"][0] + blk * 5: SM["convw"][0] + blk * 5 + 5]
                    for g8 in range(0, 32, 8):
                        for blk in range(g8, g8 + 8):
                            cb2 = blk % 8
                            cbias = smt[:, SM["convb"][0] + blk: SM["convb"][0] + blk + 1]
                            op("act", I("activation", out=cacc[cb2][:], in_=RAW[s2][:, blk, 0:128], func=AF.Identity, scale=cwv(blk)[:, 0:1], bias=cbias),
                               reads=[BRAW[s2], Bsm], writes=[Bcacc[cb2]])
                        for k in range(1, 5):
                            for blk in range(g8, g8 + 8):
                                cb2 = blk % 8
                                op("dve", I("scalar_tensor_tensor", out=cacc[cb2][:], in0=RAW[s2][:, blk, k:k + 128], scalar=cwv(blk)[:, k:k + 1], in1=cacc[cb2][:], op0=ALU.mult, op1=ALU.add),
                                   reads=[BRAW[s2], Bsm, Bcacc[cb2]], writes=[Bcacc[cb2]])
                        for blk in range(g8, g8 + 8):
                            cb2 = blk % 8
                            op("act", I("activation", out=XBC[:, blk, :], in_=cacc[cb2][:], func=AF.Silu),
                               reads=[Bcacc[cb2]], writes=[BXBC])
                    dma("sp", st_ct[gc], XBC[:, 24:32, :].rearrange("p a t -> p (a t)"), reads=[BXBC], sem_buf=BXBC)
                    for hh in range(2):
                        for k8 in range(8):
                            op("pe", I("transpose", out=PT[hh][:, k8 * 128:(k8 + 1) * 128], in_=XBC[:, hh * 8 + k8, :], identity=ident_b),
                               reads=[BXBC, Bcbf], writes=[BPT[hh]])
                        op("act", I("copy", out=Xtok[:, hh * 1024:(hh + 1) * 1024], in_=PT[hh][:, :]), reads=[BPT[hh]], writes=[BXtok])
                    for k8 in range(8):
                        op("pe", I("transpose", out=PT[0][:, k8 * 128:(k8 + 1) * 128], in_=XBC[:, 16 + k8, :], identity=ident_b),
                           reads=[BXBC, Bcbf], writes=[BPT[0]])
                    op("act", I("copy", out=Btok[:], in_=PT[0][:, :]), reads=[BPT[0]], writes=[BBtok])
                    dA, At, eAt, dec, Tot, cd, tmpv, Wb = [sv[:, i_, :] for i_ in range(8)]
                    op("dve", I("tensor_tensor", out=dA, in0=dtt[s2][:], in1=A_bc, op=ALU.mult), reads=[Bdtt[s2], Bvecs], writes=[Bsv])
                    grp("pe", [I("matmul", PM[:, 0:32], lhsT=U_f, rhs=dA[:, 0:32], start=True, stop=True),
                               I("matmul", PM[:, 32:64], lhsT=Lw_f, rhs=dA[:, 32:64], start=True, stop=True),
                               I("matmul", PM[:, 64:128], lhsT=ones_f, rhs=dA, start=True, stop=True)],
                        reads=[Bsv, Bcst], writes=[BPMa])
                    op("dve", I("tensor_copy", out=sv[:, 1, :], in_=PM[:, 0:64]), reads=[BPMa], writes=[Bsv])
                    op("dve", I("tensor_copy", out=sv[:, 4, :], in_=PM[:, 64:128]), reads=[BPMa], writes=[Bsv])
                    op("act", I("activation", out=eAt, in_=At, func=AF.Exp), reads=[Bsv], writes=[Bsv])
                    op("dve", I("tensor_tensor", out=tmpv, in0=Tot, in1=At, op=ALU.subtract), reads=[Bsv], writes=[Bsv])
                    op("act", I("activation", out=dec, in_=tmpv, func=AF.Exp), reads=[Bsv], writes=[Bsv])
                    op("act", I("activation", out=cd, in_=Tot, func=AF.Exp), reads=[Bsv], writes=[Bsv])
                    op("pool", I("tensor_copy", out=vst[:, 0:32], in_=eAt[:, 32:64]), reads=[Bsv], writes=[Bvst])
                    op("pool", I("tensor_copy", out=vst[:, 32:64], in_=cd[:, 32:64]), reads=[Bsv], writes=[Bvst])
                    dma("sp", st_vec[gc], vst[:], reads=[Bvst], sem_buf=Bvst)
                    X3 = Xtok[:].rearrange("p (h d) -> p h d", d=64)
                    for d_ in range(2):
                        op("dve", I("tensor_tensor", out=Xdt[d_][:].rearrange("p (h d) -> p h d", d=64), in0=X3,
                                                                   in1=dtt[s2][:, d_ * 32:(d_ + 1) * 32].unsqueeze(2).to_broadcast([128, 32, 64]), op=ALU.mult),
                           reads=[BXtok, Bdtt[s2]], writes=[BXdt[d_]])
                        op("pool", I("tensor_tensor", out=Xdec[d_][:].rearrange("p (h d) -> p h d", d=64), in0=Xdt[d_][:].rearrange("p (h d) -> p h d", d=64),
                                                                    in1=dec[:, d_ * 32:(d_ + 1) * 32].unsqueeze(2).to_broadcast([128, 32, 64]), op=ALU.mult),
                           reads=[BXdt[d_], Bsv], writes=[BXdec[d_]])
                    op("pool", I("tensor_tensor", out=X3, in0=X3,
                                                         in1=smv("dskip").unsqueeze(2).to_broadcast([128, 32, 64]), op=ALU.mult),
                       reads=[BXtok, Bsm], writes=[BXtok])
                    DX, BDX = Xtok, BXtok
                    for g in range(8):
                        BT_g = XBC[:, 16 + g, :]
                        CT_g = XBC[:, 24 + g, :]
                        op("pe", I("matmul", PM[:, 128:256], lhsT=BT_g, rhs=CT_g, start=True, stop=True),
                           reads=[BXBC], writes=[BPMb])
                        op("dve", I("tensor_tensor", out=CBm[:, 0, :], in0=PM[:, 128:256], in1=U_f, op=ALU.mult), reads=[BPMb, Bcst], writes=[BCBm])
                        op("dve", I("tensor_tensor", out=CBm[:, 1, :], in0=PM[:, 128:256], in1=Lw_f, op=ALU.mult), reads=[BPMb, Bcst], writes=[BCBm])
                        for d_ in range(2):
                            msk = U_f if d_ == 0 else Lw_f
                            tri = TU_f if d_ == 0 else TL_f
                            op("pool", I("tensor_tensor",
                                out=Rm[d_][:], in0=msk.unsqueeze(1).to_broadcast([128, 4, 128]),
                                in1=dA[:, d_ * 32 + 4 * g:d_ * 32 + 4 * g + 4].unsqueeze(2).to_broadcast([128, 4, 128]), op=ALU.mult),
                               reads=[Bcst, Bsv], writes=[BRm[d_]])
                            op("pe", I("matmul", PA[d_][:, :], lhsT=tri, rhs=Rm[d_][:].rearrange("p a t -> p (a t)"), start=True, stop=True),
                               reads=[Bcst, BRm[d_]], writes=[BPA[d_]])
                            op("act", I("activation", out=LT[d_][:].rearrange("p a t -> p (a t)"), in_=PA[d_][:, :], func=AF.Exp),
                               reads=[BPA[d_]], writes=[BLT[d_]])
                            op("dve", I("tensor_tensor", out=MT[d_][:], in0=LT[d_][:], in1=CBm[:, d_, :].unsqueeze(1).to_broadcast([128, 4, 128]), op=ALU.mult),
                               reads=[BLT[d_], BCBm], writes=[BMT[d_]])
                        fns = []
                        for r_ in range(4):
                            h = 4 * g + r_
                            fns.append(I("matmul", PX2[:, r_ * 64:(r_ + 1) * 64], lhsT=MT[0][:, r_, :], rhs=Xdt[0][:, h * 64:(h + 1) * 64], start=True, stop=False))
                            fns.append(I("matmul", PX2[:, r_ * 64:(r_ + 1) * 64], lhsT=MT[1][:, r_, :], rhs=Xdt[1][:, h * 64:(h + 1) * 64], start=False, stop=False))
                            fns.append(I("matmul", PX2[:, r_ * 64:(r_ + 1) * 64], lhsT=ident_b, rhs=DX[:, h * 64:(h + 1) * 64], start=False, stop=True))
                        grp("pe", fns, reads=[BMT[0], BMT[1], BXdt[0], BXdt[1], BDX, Bcbf], writes=[BPX2[0]])
                        op("pe", I("matmul", PX2[:, 256:512], lhsT=CT_g, rhs=prevf[:, g * 256:(g + 1) * 256], start=True, stop=True),
                           reads=[BXBC, Bprevf], writes=[BPX2[1]])
                        op("dve", I("tensor_tensor", out=ytmp[:].rearrange("p (h d) -> p h d", d=64), in0=PX2[:, 256:512].rearrange("p (h d) -> p h d", d=64),
                                                                 in1=eAt[:, 4 * g:4 * g + 4].unsqueeze(2).to_broadcast([128, 4, 64]), op=ALU.mult),
                           reads=[BPX2[1], Bsv], writes=[Bytmp])
                        op("dve", I("tensor_tensor", out=ypst[:, g * 256:(g + 1) * 256], in0=ytmp[:], in1=PX2[:, 0:256], op=ALU.add),
                           reads=[Bytmp, BPX2[0]], writes=[Bypst])
                        for d_ in range(2):
                            op("pe", I("matmul", PX3[:, d_ * 256:(d_ + 1) * 256], lhsT=Btok[:, g * 128:(g + 1) * 128], rhs=Xdec[d_][:, g * 256:(g + 1) * 256], start=True, stop=True),
                               reads=[BBtok, BXdec[d_]], writes=[BPX3[d_]])
                        Sg = Sf[:, g * 256:(g + 1) * 256].rearrange("p (h d) -> p h d", d=64)
                        op("dve", I("tensor_tensor", out=Sg, in0=Sg, in1=cd[:, 4 * g:4 * g + 4].unsqueeze(2).to_broadcast([128, 4, 64]), op=ALU.mult),
                           reads=[BSf, Bsv, Bprevf], writes=[BSf])
                        op("dve", I("tensor_tensor", out=Sf[:, g * 256:(g + 1) * 256], in0=Sf[:, g * 256:(g + 1) * 256], in1=PX3[:, 0:256], op=ALU.add),
                           reads=[BSf, BPX3[0]], writes=[BSf])
                        op("act", I("copy", out=sbst[:, g * 256:(g + 1) * 256], in_=PX3[:, 256:512]), reads=[BPX3[1]], writes=[Bsbst])
                    op("act", I("copy", out=prevf[:], in_=Sf[:]), reads=[BSf], writes=[Bprevf])
                    dma("sp", st_sb[gc], sbst[:], reads=[Bsbst], sem_buf=Bsbst)
                    dma("sp", st_ypart[gc], ypst[:], reads=[Bypst], sem_buf=Bypst)
                    kbs = [kb for kb in (m - 1, m, m + 1) if 0 <= kb < n]
                    for g in range(8):
                        e_, i_ = g % 2, g // 2
                        rhs_q = qT[s2][e_ * 64:(e_ + 1) * 64, i_].rearrange("p j t -> p (j t)")
                        for t_, kb in enumerate(kbs):
                            pb = (g * 3 + t_) % 3
                            QB, BQB = ((PX1, BPX1), (PA[0], BPA[0]), (PA[1], BPA[1]))[pb]
                            op("pe", I("matmul", QB[:, :], lhsT=kT[kb % 4][e_ * 64:(e_ + 1) * 64, i_, :], rhs=rhs_q, start=True, stop=True),
                               reads=[BkT[kb % 4], BqT[s2]], writes=[BQB])
                            op("act", I("activation", out=PTs[pb][:].rearrange("p a t -> p (a t)"), in_=QB[:, :], func=AF.Exp, scale=0.125),
                               reads=[BQB], writes=[BPTs[pb]])
                            if kb != m:
                                mk = Lw_b if kb < m else U_b
                                op("pool", I("tensor_tensor", out=PTs[pb][:], in0=PTs[pb][:], in1=mk.unsqueeze(1).to_broadcast([128, 4, 128]), op=ALU.mult),
                                   reads=[BPTs[pb], Bcbf], writes=[BPTs[pb]])
                        fns = []
                        OB, BOB = ((PX3, BPX3[0]), (PX2, BPX2[0]))[g % 2]
                        for j in range(4):
                            for t_, kb in enumerate(kbs):
                                pb = (g * 3 + t_) % 3
                                fns.append(I("matmul", OB[:, j * 65:(j + 1) * 65], lhsT=PTs[pb][:, j, :], rhs=vr[kb % 4][:, g, :],
                                                                                      start=(t_ == 0), stop=(t_ == len(kbs) - 1)))
                        grp("pe", fns, reads=[BPTs[0], BPTs[1], BPTs[2]] + [Bvr[kb % 4] for kb in kbs], writes=[BOB])
                        o3 = OB[:, 0:260].rearrange("p (j d) -> p j d", d=65)
                        op("dve", I("tensor_tensor", out=den[:, 0:4].unsqueeze(2), in0=o3[:, :, 64:65], in1=expsink[:, 4 * g:4 * g + 4].unsqueeze(2), op=ALU.add),
                           reads=[BOB, Bvecs], writes=[Bden])
                        op("dve", I("reciprocal", out=den[:, 4:8], in_=den[:, 0:4]), reads=[Bden], writes=[Bden])
                        op("dve", I("tensor_tensor", out=ytmp[:].rearrange("p (j d) -> p j d", d=64), in0=o3[:, :, 0:64],
                                                                   in1=den[:, 4:8].unsqueeze(2).to_broadcast([128, 4, 64]), op=ALU.mult),
                           reads=[BOB, Bden], writes=[Bytmp])
                        op("dve", I("tensor_tensor", out=yast[:, g * 256:(g + 1) * 256], in0=ytmp[:], in1=ga[s2][:, g * 256:(g + 1) * 256], op=ALU.mult),
                           reads=[Bytmp, Bga[s2]], writes=[Byast])
                    dma("sp", st_ya[gc], yast[:], reads=[Byast], sem_buf=Byast)

                for i in range(n):
                    inproj(i)
                    if i > 0 and not _stop.startswith("inproj"):
                        mixers(i - 1)
                if not _stop.startswith("inproj"):
                    mixers(n - 1)
                gchunk += n
            fw.barrier()

        with ExitStack() as s2_:
            alloc_wb(NWB2, s2_)
            yp = sb("yp", [128, D], F32, s2_); Byp = Buf("yp")
            zs2 = sb("zs2", [128, D], BF16, s2_); Bzs2 = Buf("zs2")
            gs2 = sb("gs2", [128, D], BF16, s2_); Bgs2 = Buf("gs2")
            ya2 = sb("ya2", [128, D], BF16, s2_); Bya2 = Buf("ya2")
            ct2 = sb("ct2", [128, 8, 128], BF16, s2_); Bct2 = Buf("ct2")
            sb2 = sb("sb2", [128, D], BF16, s2_); Bsb2 = Buf("sb2")
            vc2 = sb("vc2", [128, 64], F32, s2_); Bvc2 = Buf("vc2")
            xt2 = sb("xt2", [128, D], F32, s2_); Bxt2 = Buf("xt2")
            Sb = sb("Sb", [128, D], F32, s2_); BSb = Buf("Sb")
            prevb = sb("prevb", [128, D], BF16, s2_); Bprevb = Buf("prevb")
            gst = sb("gst", [128, 24], F32, s2_); Bgst = Buf("gst")
            mg = sb("mg", [128, D], BF16, s2_); Bmg = Buf("mg")
            junk2, Bjunk2 = mg, Bmg
            mT = sb("mT", [128, KC, 128], BF16, s2_); BmT = Buf("mT")
            x1 = sb("x1", [128, D], F32, s2_); Bx1 = Buf("x1")
            xn2 = sb("xn2", [128, D], BF16, s2_); Bxn2 = Buf("xn2")
            ypl, Bypl = xn2, Bxn2
            nst2 = sb("nst2", [128, 4], F32, s2_); Bnst2 = Buf("nst2")
            h2T = sb("h2T", [128, KC, 128], BF16, s2_); Bh2T = Buf("h2T")
            sg = sb("sg", [128, 512], BF16, s2_); Bsg = Buf("sg")
            acttok = sb("acttok", [128, 512], BF16, s2_); Bacttok = Buf("acttok")
            actT = sb("actT", [128, FC, 128], BF16, s2_); BactT = Buf("actT")
            fo = sb("fo", [128, D], F32, s2_); Bfo = Buf("fo")
            yw, Byw = fo, Bfo
            nwbc = sb("nwbc", [128, D], F32, s2_); Bnw = Buf("nwbc", const=True)
            GGt = [sb("GG_%d" % j, [128, D], F32, s2_) for j in range(2)]
            BGGt = [Buf("GG_%d" % j) for j in range(2)]
            dma("sp", nwbc[:], rowbc[:, 0:D], writes=[Bnw], sem_buf=Bnw)
            PB = [PA[0], PA[1], PX1, PX2]
            BPB = [BPA[0], BPA[1], BPX1, BPX2[0]]

            P2_UNITS = list(range(U_OUT0, U_OUT0 + 4)) + list(range(U_GU0, U_GU0 + 22)) + list(range(U_DN0, U_DN0 + 16))
            wq2 = []
            wplan2 = P2_UNITS * NCH
            wpos2 = {"i": 0}

            def wnext2():
                while len(wq2) < len(WB) and wpos2["i"] < len(wplan2):
                    u = wplan2[wpos2["i"]]
                    wpos2["i"] += 1
                    wq2.append((u, wload(u)))
                return wq2.pop(0)

            def rstd_of(ss_ap, out_ap, Bst, n_el):
                op("dve", I("tensor_scalar", out=out_ap, in0=ss_ap, scalar1=1.0 / n_el, scalar2=EPS, op0=ALU.mult, op1=ALU.add), reads=[Bst], writes=[Bst])
                op("act", I("activation", out=out_ap, in_=out_ap, func=AF.Sqrt), reads=[Bst], writes=[Bst])
                op("dve", I("reciprocal", out=out_ap, in_=out_ap), reads=[Bst], writes=[Bst])

            goff = [sum(pieces[:k]) for k in range(NPC)]
            for p, n in enumerate(pieces if _stop not in ("prepass", "pass1", "inproj") else []):
                op("pool", I("memset", Sb[:], 0.0), writes=[BSb])
                for j in range(2):
                    dma("sp", GGt[j][:], st_gg[2 * p + j], writes=[BGGt[j]], sem_buf=BGGt[j])
                GG = {p: GGt}
                BGG = {p: BGGt}
                for m in range(n - 1, -1, -1):
                    gc = goff[p] + m
                    last = (m == n - 1)
                    dma("sp", ypl[:], st_ypart[gc], writes=[Bypl], sem_buf=Bypl)
                    op("pool", I("tensor_copy", out=yp[:], in_=ypl[:]), reads=[Bypl], writes=[Byp])
                    dma("sp", zs2[:], st_zs[gc], writes=[Bzs2], sem_buf=Bzs2)
                    dma("sp", gs2[:], st_gs[gc], writes=[Bgs2], sem_buf=Bgs2)
                    dma("sp", ya2[:], st_ya[gc], writes=[Bya2], sem_buf=Bya2)
                    dma("sp", ct2[:].rearrange("p a t -> p (a t)"), st_ct[gc], writes=[Bct2], sem_buf=Bct2)
                    dma("sp", sb2[:], st_sb[gc], writes=[Bsb2], sem_buf=Bsb2)
                    dma("sp", vc2[:], st_vec[gc], writes=[Bvc2], sem_buf=Bvc2)
                    dma("sp", xt2[:], xin[gc * 128:(gc + 1) * 128, :], writes=[Bxt2], sem_buf=Bxt2)
                    if not last:
                        op("act", I("copy", out=prevb[:], in_=Sb[:]), reads=[BSb], writes=[Bprevb])
                        for g in range(8):
                            bk = g // 2
                            op("pe", I("matmul", PB[bk][:, (g % 2) * 256:(g % 2 + 1) * 256], lhsT=ct2[:, g, :], rhs=prevb[:, g * 256:(g + 1) * 256], start=True, stop=True),
                               reads=[Bct2, Bprevb], writes=[BPB[bk]])
                        for bk in range(4):
                            op("dve", I("tensor_tensor", out=yw[:, bk * 512:(bk + 1) * 512].rearrange("p (h d) -> p h d", d=64),
                                                                       in0=PB[bk][:, :].rearrange("p (h d) -> p h d", d=64),
                                                                       in1=vc2[:, 8 * bk:8 * bk + 8].unsqueeze(2).to_broadcast([128, 8, 64]), op=ALU.mult),
                               reads=[BPB[bk], Bvc2], writes=[Byw])
                        op("dve", I("tensor_tensor", out=yp[:], in0=yp[:], in1=yw[:], op=ALU.add), reads=[Byp, Byw], writes=[Byp])
                    S3 = Sb[:].rearrange("p (h d) -> p h d", d=64)
                    op("dve", I("tensor_tensor", out=S3, in0=S3, in1=vc2[:, 32:64].unsqueeze(2).to_broadcast([128, 32, 64]), op=ALU.mult),
                       reads=[BSb, Bvc2, Bprevb], writes=[BSb])
                    op("dve", I("tensor_tensor", out=Sb[:], in0=Sb[:], in1=sb2[:], op=ALU.add), reads=[BSb, Bsb2], writes=[BSb])
                    op("dve", I("tensor_tensor", out=yp[:], in0=yp[:], in1=zs2[:], op=ALU.mult), reads=[Byp, Bzs2], writes=[Byp])
                    for g in range(8):
                        op("act", I("activation", out=junk2[:, g * 256:(g + 1) * 256], in_=yp[:, g * 256:(g + 1) * 256], func=AF.Square, accum_out=gst[:, g:g + 1]),
                           reads=[Byp], writes=[Bjunk2, Bgst])
                    rstd_of(gst[:, 0:8], gst[:, 8:16], Bgst, 256.0)
                    op("dve", I("tensor_tensor", out=yp[:].rearrange("p (g d) -> p g d", d=256), in0=yp[:].rearrange("p (g d) -> p g d", d=256),
                                                        in1=gst[:, 8:16].unsqueeze(2).to_broadcast([128, 8, 256]), op=ALU.mult), reads=[Byp, Bgst], writes=[Byp])
                    op("dve", I("tensor_tensor", out=yp[:], in0=yp[:], in1=nwbc[:], op=ALU.mult), reads=[Byp, Bnw], writes=[Byp])
                    op("dve", I("tensor_tensor", out=yp[:], in0=yp[:], in1=gs2[:], op=ALU.mult), reads=[Byp, Bgs2], writes=[Byp])
                    op("dve", I("tensor_tensor", out=mg[:], in0=yp[:], in1=ya2[:], op=ALU.add), reads=[Byp, Bya2], writes=[Bmg])
                    for hh in range(2):
                        for k8 in range(8):
                            op("pe", I("transpose", out=PT[hh][:, k8 * 128:(k8 + 1) * 128], in_=mg[:, (hh * 8 + k8) * 128:(hh * 8 + k8 + 1) * 128], identity=ident_b),
                               reads=[Bmg, Bcbf], writes=[BPT[hh]])
                        op("act", I("copy", out=mT[:, hh * 8:(hh + 1) * 8, :].rearrange("p a t -> p (a t)"), in_=PT[hh][:, :]), reads=[BPT[hh]], writes=[BmT])
                    for cb_ in range(4):
                        u, b = wnext2()
                        wv = wview(b, u)
                        grp("pe", [I("matmul", PB[cb_][:, :], lhsT=mT[:, kk, :], rhs=wv[:, kk, :], start=(kk == 0), stop=(kk == KC - 1)) for kk in range(KC)],
                            reads=[BmT, BWB[b]], writes=[BPB[cb_]])
                    for cb_ in range(4):
                        op("act", I("activation", out=junk2[:, cb_ * 512:(cb_ + 1) * 512], in_=PB[cb_][:, :], func=AF.Square, accum_out=gst[:, 16 + cb_:17 + cb_]),
                           reads=[BPB[cb_]], writes=[Bjunk2, Bgst])
                    op("dve", I("tensor_reduce", out=nst2[:, 0:1], in_=gst[:, 16:20], axis=AX.X, op=ALU.add), reads=[Bgst], writes=[Bnst2])
                    rstd_of(nst2[:, 0:1], nst2[:, 1:2], Bnst2, float(D))
                    for cb_ in range(4):
                        op("dve", I("scalar_tensor_tensor", out=x1[:, cb_ * 512:(cb_ + 1) * 512], in0=PB[cb_][:, :], scalar=nst2[:, 1:2],
                                                                            in1=GG[p][0][:, cb_ * 512:(cb_ + 1) * 512], op0=ALU.mult, op1=ALU.mult),
                           reads=[BPB[cb_], Bnst2, BGG[p][0]], writes=[Bx1])
                    op("dve", I("tensor_tensor", out=x1[:], in0=x1[:], in1=xt2[:], op=ALU.add), reads=[Bx1, Bxt2], writes=[Bx1])
                    op("act", I("activation", out=junk2[:], in_=x1[:], func=AF.Square, accum_out=nst2[:, 2:3]), reads=[Bx1], writes=[Bjunk2, Bnst2])
                    rstd_of(nst2[:, 2:3], nst2[:, 3:4], Bnst2, float(D))
                    op("act", I("activation", out=xn2[:], in_=x1[:], func=AF.Copy, scale=nst2[:, 3:4]), reads=[Bx1, Bnst2], writes=[Bxn2])
                    for hh in range(2):
                        for k8 in range(8):
                            kc = hh * 8 + k8
                            op("pe", I("transpose", out=PT[hh][:, k8 * 128:(k8 + 1) * 128], in_=xn2[:, kc * 128:(kc + 1) * 128], identity=ident_b),
                               reads=[Bxn2, Bcbf], writes=[BPT[hh]])
                        for k8 in range(8):
                            kc = hh * 8 + k8
                            op("dve", I("tensor_scalar", out=h2T[:, kc, :], in0=PT[hh][:, k8 * 128:(k8 + 1) * 128],
                                                                                    scalar1=fm[:, p, 3, kc:kc + 1], scalar2=fm[:, p, 2, kc:kc + 1], op0=ALU.mult, op1=ALU.add),
                               reads=[BPT[hh], Bfm], writes=[Bh2T])
                    for j in range(11):
                        for w_ in range(2):
                            u, b = wnext2()
                            wv = wview(b, u)
                            grp("pe", [I("matmul", PB[w_][:, :], lhsT=h2T[:, kk, :], rhs=wv[:, kk, :], start=(kk == 0), stop=(kk == KC - 1)) for kk in range(KC)],
                                reads=[Bh2T, BWB[b]], writes=[BPB[w_]])
                        op("act", I("activation", out=sg[:], in_=PB[0][:, :], func=AF.Silu), reads=[BPB[0]], writes=[Bsg])
                        op("dve", I("tensor_tensor", out=acttok[:], in0=sg[:], in1=PB[1][:, :], op=ALU.mult), reads=[Bsg, BPB[1]], writes=[Bacttok])
                        tb = j % 2
                        for q_ in range(4):
                            op("pe", I("transpose", out=PT[tb][:, q_ * 128:(q_ + 1) * 128], in_=acttok[:, q_ * 128:(q_ + 1) * 128], identity=ident_b),
                               reads=[Bacttok, Bcbf], writes=[BPT[tb]])
                        op("act", I("copy", out=actT[:, 4 * j:4 * j + 4, :].rearrange("p a t -> p (a t)"), in_=PT[tb][:, 0:512]), reads=[BPT[tb]], writes=[BactT])
                    for kg in range(4):
                        for cb_ in range(4):
                            u, b = wnext2()
                            wv = wview(b, u)
                            grp("pe", [I("matmul", PB[cb_][:, :], lhsT=actT[:, 11 * kg + kk, :], rhs=wv[:, kk, :],
                                                                                      start=(kg == 0 and kk == 0), stop=(kg == 3 and kk == 10)) for kk in range(11)],
                                reads=[BactT, BWB[b]], writes=[BPB[cb_]])
                    for cb_ in range(4):
                        op("act", I("activation", out=junk2[:, cb_ * 512:(cb_ + 1) * 512], in_=PB[cb_][:, :], func=AF.Square, accum_out=gst[:, 20 + cb_:21 + cb_]),
                           reads=[BPB[cb_]], writes=[Bjunk2, Bgst])
                    op("dve", I("tensor_reduce", out=nst2[:, 0:1], in_=gst[:, 20:24], axis=AX.X, op=ALU.add), reads=[Bgst], writes=[Bnst2])
                    rstd_of(nst2[:, 0:1], nst2[:, 1:2], Bnst2, float(D))
                    for cb_ in range(4):
                        op("dve", I("scalar_tensor_tensor", out=fo[:, cb_ * 512:(cb_ + 1) * 512], in0=PB[cb_][:, :], scalar=nst2[:, 1:2],
                                                                            in1=GG[p][1][:, cb_ * 512:(cb_ + 1) * 512], op0=ALU.mult, op1=ALU.mult),
                           reads=[BPB[cb_], Bnst2, BGG[p][1]], writes=[Bfo])
                    op("dve", I("tensor_tensor", out=fo[:], in0=fo[:], in1=x1[:], op=ALU.add), reads=[Bfo, Bx1], writes=[Bfo])
                    dma("sp", yout[gc * 128:(gc + 1) * 128, :], fo[:], reads=[Bfo], sem_buf=Bfo, final=True)
            fw._need("pool", fw.final_waits)
            fw._need("sp", fw.final_waits)
        fw.emit()
    return nc


def host_consts():
    t = np.arange(128)
    ident = np.eye(128, dtype=np.float32)
    U = (t[:, None] <= t[None, :]).astype(np.float32)
    Lw = (t[:, None] >= t[None, :]).astype(np.float32)
    TU = (t[:, None] > t[None, :]).astype(np.float32)
    TL = (t[:, None] < t[None, :]).astype(np.float32)
    ones = np.ones((128, 128), np.float32)
    return np.concatenate([ident, U, Lw, TU, TL, ones], axis=1)


def rope_table(pos0_list):
    half = 8
    inv = (np.float32(500000.0) ** (-np.arange(half, dtype=np.float32) * np.float32(2.0) / np.float32(16))).astype(np.float32)
    cols = []
    for p0 in pos0_list:
        pos = (p0 + np.arange(128)).astype(np.float32)
        ang = (pos[:, None] * inv[None, :]).astype(np.float32)
        cols.append(np.cos(ang).astype(np.float32))
        cols.append(np.sin(ang).astype(np.float32))
    return np.ascontiguousarray(np.concatenate(cols, axis=1))


def fmaj(v, kc):
    return np.ascontiguousarray(np.asarray(v, np.float32).reshape(kc, 128).T)


def bc128(v):
    return np.ascontiguousarray(np.broadcast_to(np.asarray(v, np.float32)[None, :], (128, np.asarray(v).shape[0])))


def make_in_map(xs, cs, P):
    sm = np.zeros((128, SM_W), np.float32)

    def put(name, arr):
        o, w = SM[name]
        sm[:, o:o + arr.shape[1]] = arr

    put("cfm", np.concatenate([fmaj(c, 16) for c in cs], axis=1))
    put("gpre1", fmaj(P["g_pre1"][0], 16))
    put("gpre2", fmaj(P["g_pre2"][0], 16))
    cw = np.asarray(P["conv_w"][0], np.float32)
    put("convw", np.ascontiguousarray(cw.reshape(5, 32, 128).transpose(2, 1, 0)).reshape(128, 160))
    put("convb", fmaj(P["conv_b"][0], 32))
    put("alog", bc128(np.concatenate([P["a_log_f"][0], P["a_log_b"][0]])))
    put("dtb", bc128(np.concatenate([P["dt_bias_f"][0], P["dt_bias_b"][0]])))
    put("dskip", bc128(P["d_skip"][0]))
    put("sinks", bc128(P["sinks"][0]))
    rowbc = np.concatenate([bc128(P["ssd_norm_w"][0]), bc128(P["g_post1"][0]), bc128(P["g_post2"][0]), bc128(P["b_ada"][0])], axis=1)
    pos0 = []
    for x in xs:
        pos0 += [128 * i for i in range(x.shape[0] // 128)]
    return {
        "xin": np.ascontiguousarray(np.concatenate(xs, axis=0), dtype=np.float32),
        "small": sm, "consts": host_consts(), "rope": rope_table(pos0), "rowbc": np.ascontiguousarray(rowbc),
        "w_ada": np.ascontiguousarray(P["w_ada"][0], dtype=np.float32), "w_in": np.ascontiguousarray(P["w_in"][0], dtype=np.float32),
        "w_out": np.ascontiguousarray(P["w_out"][0], dtype=np.float32), "w_gu": np.ascontiguousarray(P["w_gu"][0], dtype=np.float32),
        "w_down": np.ascontiguousarray(P["w_down"][0], dtype=np.float32),
    }


def kernel(x_prompt, x_sample, c_prompt, c_sample, **P):
    x_prompt = np.asarray(x_prompt, np.float32)
    x_sample = np.asarray(x_sample, np.float32)
    c_prompt = np.asarray(c_prompt, np.float32)
    c_sample = np.asarray(c_sample, np.float32)
    P = {k: np.asarray(v, np.float32) for k, v in P.items()}
    B, L, _ = x_prompt.shape
    Bs, Ls, _ = x_sample.shape
    ncores = 8
    zP, zS, zc = np.zeros((L, D), np.float32), np.zeros((Ls, D), np.float32), np.zeros((D,), np.float32)
    maps = []
    for c in range(ncores):
        xa, ca = (x_prompt[c], c_prompt[c]) if c < B else (zP, zc)
        xb, cb = (x_sample[c], c_sample[c]) if c < Bs else (zS, zc)
        maps.append(make_in_map([xa, xb], [ca, cb], P))
    nc = build([L // 128, Ls // 128])
    res = run_bass_kernel_spmd(nc, maps, core_ids=list(range(ncores)))
    y_p = np.stack([res.results[c]["yout"][:L] for c in range(B)], axis=0)
    y_s = np.stack([res.results[c]["yout"][L:L + Ls] for c in range(Bs)], axis=0)
    return (np.ascontiguousarray(y_p, dtype=np.float32), np.ascontiguousarray(y_s, dtype=np.float32))
```
